# Optimizing a Trainium2 kernel written in Bass

```python
import jax, jax.numpy as jnp
from jax import lax
import numpy as np

D_MODEL = 1024
BATCH = 8
SEQ = 2048
DEPTH = 2
DEC_BATCH = 128
DEC_SEQ = 8
PAST_LEN = 16384
PAGE_SIZE = 128

N_MIXERS = 2
N_CONV_LAYERS = (DEPTH + 1) // 2
N_POOL_LAYERS = DEPTH // 2
N_META = 16
CONV_W = 31
POOL_WINDOWS = (2, 4, 8, 16)
N_GROUPS = len(POOL_WINDOWS)
GROUP_W = D_MODEL // N_GROUPS
MAX_W = max(POOL_WINDOWS)
D_FF = 4 * D_MODEL
EPS = 1e-6

kernel_name = "hybrid_conformer_conv_pool_decoder_step"


def rmsnorm(x, g):
    xf = x.astype(jnp.float32)
    y = xf * lax.rsqrt(jnp.mean(xf * xf, axis=-1, keepdims=True) + EPS)
    return (y * g.astype(jnp.float32)).astype(x.dtype)


def layernorm(x, g, b):
    xf = x.astype(jnp.float32)
    mu = jnp.mean(xf, axis=-1, keepdims=True)
    var = jnp.mean(jnp.square(xf - mu), axis=-1, keepdims=True)
    y = (xf - mu) * lax.rsqrt(var + EPS)
    return (y * g.astype(jnp.float32) + b.astype(jnp.float32)).astype(x.dtype)


def conv_module(h, prev, w_pw1, b_pw1, w_dw, b_dw, ln_g, ln_b, w_pw2, b_pw2):
    u = jnp.einsum('btd,de->bte', h, w_pw1) + b_pw1
    a, gate = jnp.split(u, 2, axis=-1)
    v = a * jax.nn.sigmoid(gate)
    ext = jnp.concatenate([prev.astype(v.dtype), v], axis=1)
    c = lax.conv_general_dilated(
        ext, w_dw[:, None, :].astype(ext.dtype), window_strides=(1,), padding='VALID',
        dimension_numbers=('NWC', 'WIO', 'NWC'), feature_group_count=D_MODEL) + b_dw
    z = jax.nn.silu(layernorm(c, ln_g, ln_b))
    out = jnp.einsum('btd,de->bte', z, w_pw2) + b_pw2
    return out, ext[:, -(CONV_W - 1):]


def pool_mixer(h, prev, pos0, w_grp, b_grp, scale):
    B, T, _ = h.shape
    ext = jnp.concatenate([prev.astype(h.dtype), h], axis=1)
    cs = jnp.cumsum(ext.astype(jnp.float32), axis=1)
    cs = jnp.pad(cs, ((0, 0), (1, 0), (0, 0)))
    pos = pos0 + jnp.arange(T)
    pooled = []
    for g, w in enumerate(POOL_WINDOWS):
        sl = slice(g * GROUP_W, (g + 1) * GROUP_W)
        s = cs[:, MAX_W:MAX_W + T, sl] - cs[:, MAX_W - w:MAX_W - w + T, sl]
        cnt = jnp.minimum(w, pos + 1).astype(jnp.float32)
        pooled.append(s / cnt[None, :, None])
    p = jnp.concatenate(pooled, axis=-1).astype(h.dtype) - h
    pg = p.reshape(B, T, N_GROUPS, GROUP_W)
    out = jnp.einsum('btgc,gcd->btgd', pg, w_grp) + b_grp
    out = out.reshape(B, T, D_MODEL) * scale
    return out, ext[:, -(MAX_W - 1):]


def trunk(x, conv_prev, pool_prev, pos0, norm_mix_g, norm_mlp_g,
          conv_w_pw1, conv_b_pw1, conv_w_dw, conv_b_dw, conv_ln_g, conv_ln_b,
          conv_w_pw2, conv_b_pw2, pool_w, pool_b, pool_scale, mlp_w1, mlp_w2, final_g):
    new_conv, new_pool = [], []
    for i in range(DEPTH):
        h = rmsnorm(x, norm_mix_g[i])
        j = i // N_MIXERS
        if i % N_MIXERS == 0:
            out, st = conv_module(h, conv_prev[j], conv_w_pw1[j], conv_b_pw1[j], conv_w_dw[j],
                                  conv_b_dw[j], conv_ln_g[j], conv_ln_b[j], conv_w_pw2[j], conv_b_pw2[j])
            new_conv.append(st)
        else:
            out, st = pool_mixer(h, pool_prev[j], pos0, pool_w[j], pool_b[j], pool_scale[j])
            new_pool.append(st)
        x = x + out
        h = rmsnorm(x, norm_mlp_g[i])
        x = x + jnp.einsum('btf,fd->btd', jnp.square(jax.nn.relu(jnp.einsum('btd,df->btf', h, mlp_w1[i]))), mlp_w2[i])
    return rmsnorm(x, final_g), jnp.stack(new_conv), jnp.stack(new_pool)


def setup_inputs(seed: int = 0) -> dict:
    key = jax.random.key(seed)
    ks = jax.random.split(key, 24)
    n = lambda k, shape, s: jax.random.normal(k, shape, jnp.float32) * s
    D = D_MODEL
    return {
        "x_prompt": n(ks[0], (BATCH, SEQ, D), 1.0),
        "x_sample": n(ks[1], (DEC_BATCH, DEC_SEQ, D), 1.0),
        "state_conv": n(ks[2], (N_CONV_LAYERS, DEC_BATCH, CONV_W - 1, D), 0.5),
        "state_pool": n(ks[3], (N_POOL_LAYERS, DEC_BATCH, MAX_W - 1, D), 1.0),
        "meta_tokens": n(ks[4], (N_META, D), 1.0),
        "norm_mix_g": 1.0 + n(ks[5], (DEPTH, D), 0.05),
        "norm_mlp_g": 1.0 + n(ks[6], (DEPTH, D), 0.05),
        "conv_w_pw1": n(ks[7], (N_CONV_LAYERS, D, 2 * D), D ** -0.5),
        "conv_b_pw1": n(ks[8], (N_CONV_LAYERS, 2 * D), 0.02),
        "conv_w_dw": n(ks[9], (N_CONV_LAYERS, CONV_W, D), CONV_W ** -0.5),
        "conv_b_dw": n(ks[10], (N_CONV_LAYERS, D), 0.02),
        "conv_ln_g": 1.0 + n(ks[11], (N_CONV_LAYERS, D), 0.05),
        "conv_ln_b": n(ks[12], (N_CONV_LAYERS, D), 0.02),
        "conv_w_pw2": n(ks[13], (N_CONV_LAYERS, D, D), D ** -0.5),
        "conv_b_pw2": n(ks[14], (N_CONV_LAYERS, D), 0.02),
        "pool_w": n(ks[15], (N_POOL_LAYERS, N_GROUPS, GROUP_W, GROUP_W), GROUP_W ** -0.5),
        "pool_b": n(ks[16], (N_POOL_LAYERS, N_GROUPS, GROUP_W), 0.02),
        "pool_scale": 0.5 + n(ks[17], (N_POOL_LAYERS, D), 0.05),
        "mlp_w1": n(ks[18], (DEPTH, D, D_FF), D ** -0.5),
        "mlp_w2": n(ks[19], (DEPTH, D_FF, D), 0.5 * D_FF ** -0.5),
        "final_g": 1.0 + n(ks[20], (D,), 0.05),
    }


def reference(x_prompt, x_sample, state_conv, state_pool, meta_tokens, norm_mix_g, norm_mlp_g,
              conv_w_pw1, conv_b_pw1, conv_w_dw, conv_b_dw, conv_ln_g, conv_ln_b,
              conv_w_pw2, conv_b_pw2, pool_w, pool_b, pool_scale, mlp_w1, mlp_w2, final_g):
    weights = (norm_mix_g, norm_mlp_g, conv_w_pw1, conv_b_pw1, conv_w_dw, conv_b_dw,
               conv_ln_g, conv_ln_b, conv_w_pw2, conv_b_pw2, pool_w, pool_b, pool_scale,
               mlp_w1, mlp_w2, final_g)
    meta = jnp.broadcast_to(meta_tokens.astype(x_prompt.dtype)[None], (x_prompt.shape[0], N_META, D_MODEL))
    xp = jnp.concatenate([meta, x_prompt], axis=1)
    conv0 = jnp.zeros((N_CONV_LAYERS, x_prompt.shape[0], CONV_W - 1, D_MODEL), x_prompt.dtype)
    pool0 = jnp.zeros((N_POOL_LAYERS, x_prompt.shape[0], MAX_W - 1, D_MODEL), x_prompt.dtype)
    yp, new_conv_prompt, new_pool_prompt = trunk(xp, conv0, pool0, 0, *weights)
    y_prompt = yp[:, N_META:]
    y_sample, new_conv_sample, new_pool_sample = trunk(x_sample, state_conv, state_pool, PAST_LEN, *weights)
    return (y_prompt, y_sample, new_conv_prompt, new_conv_sample, new_pool_prompt, new_pool_sample)
```

```python
import numpy as np
from contextlib import ExitStack
import concourse.bass as bass
import concourse.mybir as mybir
from concourse.bass_utils import run_bass_kernel_spmd

F32 = mybir.dt.float32
BF16 = mybir.dt.bfloat16
AF = mybir.ActivationFunctionType
ALU = mybir.AluOpType

ENGS = ("pe", "act", "dve", "pool", "sp")
NCORES = 8
D = 1024
NT = 2192
NP = 2064
EPS = 1e-6
TILES = [(0, 512), (512, 512), (1024, 512), (1536, 512), (2048, 144)]
VW = 2702
VS0 = 2094
NSLOT = 3
DEBUG_DUMP = False
SELF_DIST = 2

V_GMIX0, V_GMIX1, V_GMLP0, V_GMLP1, V_GFIN, V_BA, V_BG, V_BDW, V_LNG, V_LNB, V_BPW2, V_PB, V_PS = range(13)
V_WDW = 13
NV = 13 + 31


class DSem:
    def __init__(self, handle, key):
        self.h = handle
        self.key = key
        self.count = 0


class Sched:
    def __init__(self, nc, stack):
        self.nc = nc
        self.stack = stack
        self.lists = {e: [] for e in ENGS}
        self.sem = {e: stack.enter_context(nc.semaphore("s_" + e)) for e in ENGS if e != "sp"}
        self.count = {e: 0 for e in ENGS}
        self.opidx = {e: 0 for e in ENGS}
        self.waited = {e: {} for e in ENGS}
        self.reg = {}
        self.semh = {e: self.sem[e] for e in self.sem}
        self.ndsem = 0
        self.out_events = []
        self.nwaits = 0
        self.dead = False

    def dsem(self, name=None):
        self.ndsem += 1
        key = "d%d" % self.ndsem
        h = self.stack.enter_context(self.nc.semaphore(name or key))
        self.semh[key] = h
        return DSem(h, key)

    @staticmethod
    def _norm(k):
        if isinstance(k, tuple) and len(k) == 3 and isinstance(k[1], int) and isinstance(k[2], int) \
                and isinstance(k[0], str):
            return k
        return (k, 0, 1)

    def _deps_and_register(self, reads, writes, ev):
        deps = []
        for is_w, lst in ((False, reads), (True, writes)):
            for k in lst:
                rg, lo, hi = self._norm(k)
                ents = self.reg.setdefault(rg, [])
                keep = []
                merged = False
                for e in ents:
                    elo, ehi, ew, eev = e
                    if ehi <= lo or elo >= hi:
                        keep.append(e)
                        continue
                    if eev is ev:
                        if is_w and lo <= elo and ehi <= hi:
                            continue
                        keep.append(e)
                        continue
                    if is_w or ew:
                        deps.append(eev)
                    if is_w and lo <= elo and ehi <= hi:
                        continue
                    if (not is_w) and (not ew) and elo == lo and ehi == hi and eev[1] == ev[1]:
                        e[3] = ev
                        merged = True
                    keep.append(e)
                if not merged:
                    keep.append([lo, hi, is_w, ev])
                self.reg[rg] = keep
        return deps

    def _emit_waits(self, eng, deps, is_dma):
        waits = {}
        for (deng, skey, val, didx, d_is_dma) in deps:
            if deng == eng and not d_is_dma and not is_dma:
                if eng == "pe":
                    continue
                if eng == "pool" and didx < self.opidx[eng] - SELF_DIST:
                    continue
            if self.waited[eng].get(skey, 0) >= val:
                continue
            if waits.get(skey, 0) < val:
                waits[skey] = val
        out = []
        for skey, val in waits.items():
            self.waited[eng][skey] = val
            out.append((self.semh[skey], val))
            self.nwaits += 1
        return out

    def op(self, eng, fn, reads=(), writes=()):
        if self.dead:
            return None
        self.count[eng] += 1
        ev = (eng, eng, self.count[eng], self.opidx[eng], False)
        deps = self._deps_and_register(reads, writes, ev)
        waits = self._emit_waits(eng, deps, False)
        self.opidx[eng] += 1
        self.lists[eng].append((waits, fn, [(self.sem[eng], 1)]))
        return ev

    def dma(self, eng, pairs, dsem, reads=(), writes=(), is_output=False, slow=False):
        if self.dead:
            return None
        dsem.count += 16 * len(pairs)
        ev = ("dma_" + eng, dsem.key, dsem.count, -1, True)
        deps = self._deps_and_register(reads, writes, ev)
        waits = self._emit_waits(eng, deps, True)
        self.opidx[eng] += 1

        def fn(e, pairs=pairs, h=dsem.h, slow=slow):
            for (o, i) in pairs:
                if slow:
                    e.dma_start(out=o, in_=i, allow_slow_non_contiguous=True).then_inc(h, 16)
                else:
                    e.dma_start(out=o, in_=i).then_inc(h, 16)
            return None

        self.lists[eng].append((waits, fn, None))
        if is_output:
            self.out_events.append(ev)
        return ev

    def finish(self):
        waits = {}
        for ev in self.out_events:
            if waits.get(ev[1], 0) < ev[2]:
                waits[ev[1]] = ev[2]
        for e in self.sem:
            if self.count[e] > 0:
                waits[e] = self.count[e]
        for rg, ents in self.reg.items():
            for e in ents:
                ev = e[3]
                if ev[4] and waits.get(ev[1], 0) < ev[2]:
                    waits[ev[1]] = ev[2]
        self.lists["sp"].append(([(self.semh[k], v) for k, v in waits.items()], None, None))

    def emit(self):
        nc = self.nc
        lists = self.lists

        def replay(name, eng):
            for (waits, fn, incs) in lists[name]:
                for (h, v) in waits:
                    eng.wait_ge(h, v)
                if fn is None:
                    continue
                ins = fn(eng)
                if incs:
                    for (h, a) in incs:
                        ins.then_inc(h, a)

        with nc.Block() as block:
            @block.tensor
            def _(e):
                replay("pe", e)

            @block.scalar
            def _(e):
                replay("act", e)

            @block.vector
            def _(e):
                replay("dve", e)

            @block.gpsimd
            def _(e):
                replay("pool", e)

            @block.sync
            def _(e):
                replay("sp", e)


class _Stop(Exception):
    pass


def build_program(stop=None):
    nc = bass.Bass("TRN2", target_bir_lowering=False)
    dt_in = lambda name, shape: nc.dram_tensor(name, shape, F32, kind="ExternalInput").ap()
    dt_out = lambda name, shape: nc.dram_tensor(name, shape, F32, kind="ExternalOutput").ap()
    xp = dt_in("xp", [2048, D])
    xs = dt_in("xs", [128, D])
    sconv = dt_in("sconv", [16, 30, D])
    spool = dt_in("spool", [16, 15, D])
    meta = dt_in("meta", [16, D])
    vecs = dt_in("vecs", [128, NV * 8])
    pbrow = dt_in("pbrow", [1, D])
    w_pw1 = dt_in("w_pw1", [D, 2 * D])
    w_pw2 = dt_in("w_pw2", [D, D])
    w_pool = dt_in("w_pool", [4, 256, 256])
    w_m1 = dt_in("w_m1", [2, D, 4 * D])
    w_m2 = dt_in("w_m2", [2, 4 * D, D])
    yp = dt_out("yp", [2048, D])
    ys = dt_out("ys", [128, D])
    ncp = dt_out("ncp", [30, D])
    ncs = dt_out("ncs", [16, 30, D])
    npp = dt_out("npp", [15, D])
    nps = dt_out("nps", [16, 15, D])
    if stop is not None:
        dbg_x = dt_out("dbg_x", [128, 8 * NT])
        dbg_h = nc.dram_tensor("dbg_h", [128, 8 * NT], BF16, kind="ExternalOutput").ap()
        dbg_s = nc.dram_tensor("dbg_s", [128, 8 * VW], BF16, kind="ExternalOutput").ap()

    with ExitStack() as st:
        sc = Sched(nc, st)
        SB = lambda name, shape, dt: st.enter_context(nc.sbuf_tensor(name, shape, dt))
        Xt = SB("X", [128, 8 * NT], F32)
        Hreg = SB("Hreg", [128, 8 * NT], BF16)
        Sreg = SB("Sreg", [128, 8 * VW], BF16)
        Wreg = SB("Wreg", [128, NSLOT * 4096], BF16)
        ZBt = SB("ZB", [128, 8 * 512], BF16)
        SQRt = SB("SQR", [128, 8 * 512], BF16)
        STt = SB("ST", [128, 4 * 512], F32)
        INt = SB("IN", [128, 2 * 1024], F32)
        PTt = SB("PT", [128, 2 * 512], F32)
        VEC = SB("VEC", [128, NV * 8], F32)
        G32 = SB("G32", [128, 5 * 8], F32)
        PBS = SB("PBS", [128, 8], F32)
        INVC = SB("INVC", [128, 4 * 16], F32)
        ident = SB("ident", [128, 128], F32)
        identb = SB("identb", [128, 128], BF16)
        ones = SB("ones", [128, 128], BF16)
        ps = st.enter_context(nc.psum_tensor("ps", [128, 8, 512], F32))

        X = Xt[:, :].rearrange("p (c n) -> p c n", c=8)
        H = Hreg[:, :].rearrange("p (c n) -> p c n", c=8)
        Cv = Hreg[:, 0:8192].bitcast(F32).rearrange("p (c n) -> p c n", c=8)
        DG = Hreg[:, 8192:8192 + 2 * 31 * 128].rearrange("p (b k n) -> p b k n", b=2, k=31)
        HF = Hreg[:, 0:2 * 8 * 527 * 2].bitcast(F32).rearrange("p (b c n) -> p b c n", b=2, c=8)
        YT = Hreg[:, 0:2 * 8 * 512 * 2].bitcast(F32).rearrange("p (b c n) -> p b c n", b=2, c=8)
        VE = Sreg[:, :].rearrange("p (c n) -> p c n", c=8)
        HID = Sreg[:, 0:8 * NT].rearrange("p (c n) -> p c n", c=8)
        OST = Sreg[:, 0:10 * 2048].bitcast(F32).rearrange("p (s n) -> p s n", s=10)
        WA = [Wreg[:, s * 4096:(s + 1) * 4096].rearrange("p (k m) -> p k m", k=8) for s in range(NSLOT)]
        WB = [Wreg[:, s * 4096:(s + 1) * 4096].rearrange("p (k m) -> p k m", k=4) for s in range(NSLOT)]
        WP = [Wreg[:, s * 4096:s * 4096 + 2048].rearrange("p (g k n) -> p g k n", g=4, k=2) for s in range(NSLOT)]
        ZB = ZBt[:, :].rearrange("p (c n) -> p c n", c=8)
        SQR = SQRt[:, :].rearrange("p (c n) -> p c n", c=8)
        ST = STt[:, :].rearrange("p (c n) -> p c n", c=4)
        VO = STt[:, 0:8 * 158].rearrange("p (c n) -> p c n", c=8)
        IN = INt[:, :].rearrange("p (s n) -> p s n", s=2)
        ZBF = ZBt[:, :].bitcast(F32).rearrange("p (s n) -> p s n", s=2)
        PT = PTt[:, :].rearrange("p (s n) -> p s n", s=2)
        vec = lambda v, c: VEC[:, v * 8 + c:v * 8 + c + 1]
        g32 = lambda i, c: G32[:, i * 8 + c:i * 8 + c + 1]

        rX = lambda c, a, n: ("X", (c * NT + a) * 4, (c * NT + a + n) * 4)
        rH = lambda c, a, n: ("H", (c * NT + a) * 2, (c * NT + a + n) * 2)
        rC = lambda c, a=0, n=512: ("H", (c * 512 + a) * 4, (c * 512 + a + n) * 4)
        rDG = lambda b, k0=0, k1=31: ("H", 16384 + (b * 31 + k0) * 256, 16384 + (b * 31 + k1) * 256)
        rHF = lambda b, c, a=0, n=527: ("H", ((b * 8 + c) * 527 + a) * 4, ((b * 8 + c) * 527 + a + n) * 4)
        rYT = lambda b, c, a=0, n=512: ("H", ((b * 8 + c) * 512 + a) * 4, ((b * 8 + c) * 512 + a + n) * 4)
        rVE = lambda c, a, n: ("S", (c * VW + a) * 2, (c * VW + a + n) * 2)
        rHID = lambda c, a, n: ("S", (c * NT + a) * 2, (c * NT + a + n) * 2)
        rOST = lambda s: ("S", s * 4096, (s + 1) * 4096)
        rW = lambda s: ("W", s * 8192, (s + 1) * 8192)
        rZ = lambda c, a=0, n=512: ("ZB", (c * 512 + a) * 2, (c * 512 + a + n) * 2)
        rSQ = lambda i: ("SQR", i * 1024, (i + 1) * 1024)
        rST = lambda i: ("ST", i * 2048, (i + 1) * 2048)
        rSTall = ("ST", 0, 8192)
        rIN = lambda s: ("IN", s * 4096, (s + 1) * 4096)
        rPT = lambda s: ("PT", s * 2048, (s + 1) * 2048)
        rPS = lambda b: ("PS", b, b + 1)

        def chk(stage):
            if stop == stage and not sc.dead:
                dd = sc.dsem("d_dbg")
                sc.dma("sp", [(dbg_x, Xt[:, :])], dd, reads=[("X", 0, 8 * NT * 4)], writes=["dbg_x"], is_output=True)
                sc.dma("sp", [(dbg_h, Hreg[:, :])], dd, reads=[("H", 0, 8 * NT * 2)], writes=["dbg_h"], is_output=True)
                sc.dma("sp", [(dbg_s, Sreg[:, :])], dd, reads=[("S", 0, 8 * VW * 2)], writes=["dbg_s"], is_output=True)
                sc.dead = True

        bank_ctr = [0]

        nb_mod = [8]

        def nb():
            b = bank_ctr[0] % nb_mod[0]
            bank_ctr[0] += 1
            return b

        sq_ctr = [0]

        def nsq():
            i = sq_ctr[0] % 8
            sq_ctr[0] += 1
            return i

        alt = [0]

        def alt_eng(a="act", b="dve"):
            alt[0] += 1
            return a if alt[0] % 2 else b

        def copy_op(eng, out, in_, reads, writes):
            if eng == "act":
                sc.op("act", lambda e: e.activation(out=out, in_=in_, func=AF.Copy), reads=reads, writes=writes)
            else:
                sc.op(eng, lambda e: e.tensor_copy(out=out, in_=in_), reads=reads, writes=writes)

        d_const = sc.dsem("d_const")
        d_in = [sc.dsem("d_in0"), sc.dsem("d_in1"), sc.dsem("d_in2"), sc.dsem("d_in3")]
        STG = [(IN[:, 0, :], ("IN", 0, 4096), d_in[0]), (IN[:, 1, :], ("IN", 4096, 8192), d_in[1]),
               (ZBF[:, 0, :], ("ZB", 0, 4096), d_in[2]), (ZBF[:, 1, :], ("ZB", 4096, 8192), d_in[3])]
        d_w = [sc.dsem("d_w%d" % s) for s in range(NSLOT)]
        d_ost = [sc.dsem("d_ost%d" % s) for s in range(10)]
        d_misc = sc.dsem("d_misc")
        d_pt = sc.dsem("d_pt")

        sc.op("pool", lambda e: e.memset(ident[:], 0.0), writes=["ident"])
        sc.op("pool", lambda e: e.affine_select(out=ident[:], in_=ident[:], compare_op=ALU.not_equal, fill=1.0,
                                                base=0, pattern=[[-1, 128]], channel_multiplier=1),
              reads=["ident"], writes=["ident"])
        sc.op("pool", lambda e: e.memset(ones[:], 1.0), writes=["ones"])
        sc.op("pool", lambda e: e.tensor_copy(out=identb[:], in_=ident[:]), reads=["ident"], writes=["identb"])

        for g, w in enumerate((2, 4, 8, 16)):
            sc.op("pool", lambda e, g=g, w=w: e.memset(INVC[:, g * 16:(g + 1) * 16], 1.0 / w),
                  writes=[("invc", g * 16, (g + 1) * 16)])
            for t in range(w - 1):
                sc.op("pool", lambda e, g=g, t=t: e.memset(INVC[:, g * 16 + t:g * 16 + t + 1], 1.0 / (t + 1)),
                      writes=[("invc", g * 16 + t, g * 16 + t + 1)])
        sc.dma("sp", [(VEC[:], vecs)], d_const, writes=["vec"])
        sc.op("pool", lambda e: e.memset(VE[:, :, 0:30], 0.0), writes=[rVE(c, 0, 30) for c in range(8)])
        for i, v in enumerate((V_GMIX0, V_GMLP0, V_GMIX1, V_GMLP1, V_GFIN)):
            sc.op("dve", lambda e, i=i, v=v: e.tensor_scalar(out=G32[:, i * 8:(i + 1) * 8], in0=VEC[:, v * 8:(v + 1) * 8],
                                                             scalar1=32.0, scalar2=None, op0=ALU.mult),
                  reads=["vec"], writes=[("g32", i, i + 1)])
        sc.op("dve", lambda e: e.tensor_tensor(out=PBS[:], in0=VEC[:, V_PB * 8:(V_PB + 1) * 8],
                                               in1=VEC[:, V_PS * 8:(V_PS + 1) * 8], op=ALU.mult),
              reads=["vec"], writes=["pbs"])

        chk('c0')
        wloads = []
        for hb in range(4):
            wloads.append(("A", [(lambda s, hb=hb: WA[s][:, :, 0:256],
                                  w_pw1[:, 256 * hb:256 * hb + 256].rearrange("(k p) m -> p k m", p=128)),
                                 (lambda s, hb=hb: WA[s][:, :, 256:512],
                                  w_pw1[:, D + 256 * hb:D + 256 * hb + 256].rearrange("(k p) m -> p k m", p=128))]))
        for hb in range(2):
            wloads.append(("A", [(lambda s: WA[s][:, :, :],
                                  w_pw2[:, 512 * hb:512 * hb + 512].rearrange("(k p) m -> p k m", p=128))]))

        def mlp_loads(layer):
            for fb in range(4):
                for hb in range(2):
                    c0 = fb * 1024 + hb * 512
                    wloads.append(("A", [(lambda s: WA[s][:, :, :],
                                          w_m1[layer, :, c0:c0 + 512].rearrange("(k p) m -> p k m", p=128))]))
                for hb in range(2):
                    r0 = fb * 1024 + hb * 512
                    wloads.append(("B", [(lambda s: WB[s][:, :, :],
                                          w_m2[layer, r0:r0 + 512, :].rearrange("(k p) m -> p k m", p=128))]))
        mlp_loads(0)
        wloads.append(("P", [(lambda s: WP[s][:, :, :, :], w_pool.rearrange("g (k p) n -> p g k n", p=128)),
                             (lambda s: Wreg[0:1, s * 4096 + 2048:s * 4096 + 3072], pbrow)]))
        mlp_loads(1)
        wnext = [0]

        def issue_wload():
            i = wnext[0]
            if i >= len(wloads):
                return
            wnext[0] += 1
            s = i % NSLOT
            pairs = [(vf(s), ap) for (vf, ap) in wloads[i][1]]
            sc.dma("pool", pairs, d_w[s], writes=[rW(s)])

        wuse = [0]

        def next_w():
            i = wuse[0]
            wuse[0] += 1
            assert sc.dead or i < wnext[0], "weight load not issued"
            return i % NSLOT

        for _ in range(NSLOT):
            if not _NOW:
                issue_wload()

        chk('cw')
        in_ctr = [0]

        def load_rows(src_ap, nrows, dst_fn, dst_ranges_fn):
            sap, srng, sd = STG[in_ctr[0] % 4]
            in_ctr[0] += 1
            sc.dma("sp", [(sap[0:nrows, :], src_ap)], sd, writes=[srng])
            for half in range(2):
                b = nb()

                def tr(e, half=half, b=b, sap=sap):
                    ins = None
                    for cc in range(4):
                        c = half * 4 + cc
                        ins = e.transpose(out=ps[:, b, cc * nrows:(cc + 1) * nrows],
                                          in_=sap[0:nrows, c * 128:(c + 1) * 128],
                                          identity=ident[0:nrows, 0:nrows])
                    return ins
                sc.op("pe", tr, reads=[srng, "ident"], writes=[rPS(b)])
                src = ps[:, b, 0:4 * nrows].rearrange("p (c n) -> p c n", c=4)
                copy_op(alt_eng(), dst_fn(half), src, [rPS(b)], dst_ranges_fn(half))

        def load_x_cols(col0, ncols, src_ap):
            load_rows(src_ap, ncols,
                      lambda half: X[:, half * 4:(half + 1) * 4, col0:col0 + ncols],
                      lambda half: [rX(half * 4 + cc, col0, ncols) for cc in range(4)])

        def rms_stats(c0, n, ri=0):
            b = nb()
            for c in range(8):
                i = nsq()
                sc.op("act", lambda e, c=c, i=i: e.activation(out=SQR[:, i, 0:n], in_=X[:, c, c0:c0 + n], func=AF.Square),
                      reads=[rX(c, c0, n)], writes=[rSQ(i)])
                sc.op("pe", lambda e, c=c, i=i, b=b: e.matmul(ps[:, b, 0:n], lhsT=ones[:], rhs=SQR[:, i, 0:n],
                                                             start=(c == 0), stop=(c == 7)),
                      reads=[rSQ(i), "ones"], writes=[rPS(b)])
            sc.op("act", lambda e, b=b: e.activation(out=ST[:, ri, 0:n], in_=ps[:, b, 0:n], func=AF.Ln,
                                                     scale=1.0, bias=D * EPS),
                  reads=[rPS(b)], writes=[rST(ri)])
            sc.op("act", lambda e: e.activation(out=ST[:, ri, 0:n], in_=ST[:, ri, 0:n], func=AF.Exp, scale=-0.5),
                  reads=[rST(ri)], writes=[rST(ri)])

        def rms_apply(gi, c, c0, n, out_ap, out_rng, eng="dve", ri=0):
            sc.op(eng, lambda e: e.scalar_tensor_tensor(out=out_ap, in0=X[:, c, c0:c0 + n], scalar=g32(gi, c),
                                                        in1=ST[:, ri, 0:n], op0=ALU.mult, op1=ALU.mult),
                  reads=[rX(c, c0, n), rST(ri), ("g32", gi, gi + 1)], writes=out_rng)

        def rmsnorm_to_H(gi, c0, n):
            rms_stats(c0, n)
            for c in range(8):
                rms_apply(gi, c, c0, n, H[:, c, c0:c0 + n], [rH(c, c0, n)])

        chk('c2')

        def load_tile(ti):
            c0, n = TILES[ti]
            if ti == 0:
                load_x_cols(0, 16, meta)
                load_x_cols(16, 112, xp[0:112, :])
                for j in range(3):
                    load_x_cols(128 + 128 * j, 128, xp[112 + 128 * j:240 + 128 * j, :])
            elif ti < 4:
                for j in range(4):
                    cc0 = c0 + 128 * j
                    load_x_cols(cc0, 128, xp[cc0 - 16:cc0 + 112, :])
            else:
                load_x_cols(2048, 16, xp[2032:2048, :])
                load_x_cols(2064, 128, xs)

        def pw1_step(hb, s, ti):
            c0, n = TILES[ti]
            for mi in range(2):
                m = hb * 2 + mi
                bg = nb()
                ba = nb()

                def mm_g(e, s=s, mi=mi, c0=c0, n=n, bg=bg):
                    ins = None
                    for k in range(8):
                        ins = e.matmul(ps[:, bg, 0:n], lhsT=WA[s][:, k, 256 + mi * 128:256 + (mi + 1) * 128],
                                       rhs=H[:, k, c0:c0 + n], start=(k == 0), stop=(k == 7))
                    return ins
                sc.op("pe", mm_g, reads=[rW(s)] + [rH(k, c0, n) for k in range(8)], writes=[rPS(bg)])

                def mm_a(e, s=s, mi=mi, c0=c0, n=n, ba=ba):
                    ins = None
                    for k in range(8):
                        ins = e.matmul(ps[:, ba, 0:n], lhsT=WA[s][:, k, mi * 128:(mi + 1) * 128],
                                       rhs=H[:, k, c0:c0 + n], start=(k == 0), stop=(k == 7))
                    return ins
                sc.op("pe", mm_a, reads=[rW(s)] + [rH(k, c0, n) for k in range(8)], writes=[rPS(ba)])
                sg = 2 + (m + ti) % 2
                sc.op("act", lambda e, bg=bg, n=n, m=m, sg=sg: e.activation(out=ST[:, sg, 0:n], in_=ps[:, bg, 0:n],
                                                                           func=AF.Sigmoid, bias=vec(V_BG, m), scale=1.0),
                      reads=[rPS(bg), "vec"], writes=[rST(sg)])
                if ti < 4:
                    sc.op("dve", lambda e, ba=ba, n=n, m=m, sg=sg, c0=c0: e.scalar_tensor_tensor(
                        out=VE[:, m, 30 + c0:30 + c0 + n], in0=ps[:, ba, 0:n], scalar=vec(V_BA, m), in1=ST[:, sg, 0:n],
                        op0=ALU.add, op1=ALU.mult), reads=[rPS(ba), rST(sg), "vec"], writes=[rVE(m, 30 + c0, n)])
                else:
                    sc.op("dve", lambda e, ba=ba, m=m, sg=sg: e.scalar_tensor_tensor(
                        out=VE[:, m, 30 + 2048:30 + 2064], in0=ps[:, ba, 0:16], scalar=vec(V_BA, m), in1=ST[:, sg, 0:16],
                        op0=ALU.add, op1=ALU.mult), reads=[rPS(ba), rST(sg), "vec"], writes=[rVE(m, 30 + 2048, 16)])
                    sc.op("dve", lambda e, ba=ba, m=m, sg=sg: e.scalar_tensor_tensor(
                        out=VE[:, m, VS0:VS0 + 608].rearrange("p (s j) -> p s j", j=38)[:, :, 30:38],
                        in0=ps[:, ba, 16:144].rearrange("p (s t) -> p s t", t=8), scalar=vec(V_BA, m),
                        in1=ST[:, sg, 16:144].rearrange("p (s t) -> p s t", t=8),
                        op0=ALU.add, op1=ALU.mult), reads=[rPS(ba), rST(sg), "vec"], writes=[rVE(m, VS0, 608)])

        def load_states():
            for q in range(4):
                def dstf(half, q=q):
                    return VE[:, half * 4:(half + 1) * 4, VS0 + q * 4 * 38:VS0 + (q + 1) * 4 * 38] \
                        .rearrange("p c (s j) -> p c s j", j=38)[:, :, :, 0:30]
                s_ = in_ctr[0] % 2
                in_ctr[0] += 1
                sc.dma("sp", [(IN[0:120, s_, :], sconv[q * 4:(q + 1) * 4].rearrange("s j d -> (s j) d"))], d_in[s_],
                       writes=[rIN(s_)])
                for half in range(2):
                    b = nb()

                    def tr(e, half=half, b=b, s_=s_):
                        ins = None
                        for cc in range(4):
                            c = half * 4 + cc
                            ins = e.transpose(out=ps[:, b, cc * 120:(cc + 1) * 120], in_=IN[0:120, s_, c * 128:(c + 1) * 128],
                                              identity=ident[0:120, 0:120])
                        return ins
                    sc.op("pe", tr, reads=[rIN(s_), "ident"], writes=[rPS(b)])
                    src = ps[:, b, 0:480].rearrange("p (c s j) -> p c s j", c=4, s=4)
                    copy_op(alt_eng(), dstf(half), src, [rPS(b)],
                            [rVE(half * 4 + cc, VS0 + q * 152, 152) for cc in range(4)])

            chk('c1')
            sc.dma("sp", [(ncs[:, 0:22, :], sconv[:, 8:30, :])], d_pt, writes=["ncs_old"], is_output=True)
            sc.dma("sp", [(nps[:, 0:7, :], spool[:, 8:15, :])], d_pt, writes=["nps_old"], is_output=True)


        dg_slot = {}
        DG2 = Wreg[:, 0:31 * 128].rearrange("p (k n) -> p k n", k=31)

        DG3 = INt[:, :].bitcast(BF16)[:, 0:31 * 128].rearrange("p (k n) -> p k n", k=31)
        DG4 = ZBt[:, 0:31 * 128].rearrange("p (k n) -> p k n", k=31)

        def DGv(db, k):
            if db < 2:
                return DG[:, db, k, :]
            return (DG2, DG3, DG4)[db - 2][:, k, :]

        def rDGx(db, k0=0, k1=31):
            if db < 2:
                return rDG(db, k0, k1)
            return (("W", "IN", "ZB")[db - 2], k0 * 256, k1 * 256)

        dg_ctr = [0]

        def dg_build(ti, m, db=None):
            if db is None:
                db = dg_ctr[0] % 3
                dg_ctr[0] += 1
            dg_slot[(ti, m)] = db
            for k in range(31):
                if k % 3 == 0:
                    sc.op("dve", lambda e, k=k, m=m, db=db: e.tensor_scalar(out=DGv(db, k), in0=identb[:],
                                                                          scalar1=vec(V_WDW + k, m), scalar2=None, op0=ALU.mult),
                          reads=["identb", "vec"], writes=[rDGx(db, k, k + 1)])
                elif k % 3 == 1:
                    sc.op("pool", lambda e, k=k, m=m, db=db: e.tensor_tensor(
                        out=DGv(db, k), in0=identb[:], in1=vec(V_WDW + k, m).to_broadcast([128, 128]), op=ALU.mult),
                        reads=["identb", "vec"], writes=[rDGx(db, k, k + 1)])
                else:
                    sc.op("act", lambda e, k=k, m=m, db=db: e.activation(out=DGv(db, k), in_=identb[:], func=AF.Identity,
                                                                         scale=vec(V_WDW + k, m)),
                          reads=["identb", "vec"], writes=[rDGx(db, k, k + 1)])

        pw1_slots = [next_w(), next_w(), next_w()]
        load_tile(0)
        for step in range(8):
            if step + 1 < 5:
                load_tile(step + 1)
            if step < 5:
                rmsnorm_to_H(0, *TILES[step])
            if step == 2:
                load_states()
            for hb in range(3):
                ti = step - 1 - hb
                if 0 <= ti < 5:
                    pw1_step(hb, pw1_slots[hb], ti)
                    if ti == 4:
                        issue_wload()
        chk('load')
        dg_build(0, 0, db=3)
        dg_build(0, 1, db=4)
        s3 = next_w()
        for ti in range(5):
            pw1_step(3, s3, ti)

        chk('pw1')
        for c in range(8):
            sc.op("dve", lambda e, c=c: e.tensor_copy(out=VO[:, c, 0:30], in_=VE[:, c, 30 + 2034:30 + 2064]),
                  reads=[rVE(c, 30 + 2034, 30)], writes=[rSTall])
            sc.op("dve", lambda e, c=c: e.tensor_copy(
                out=VO[:, c, 30:158].rearrange("p (s t) -> p s t", t=8),
                in_=VE[:, c, VS0:VS0 + 608].rearrange("p (s j) -> p s j", j=38)[:, :, 30:38]),
                reads=[rVE(c, VS0, 608)], writes=[rSTall])

        def out_rows(src_fn, nrows, stg_slot_ap, stg_rng, dma_pairs_fn, dsem, name, src_reads):
            for half in range(2):
                b = nb()

                def tr(e, half=half, b=b):
                    ins = None
                    for cc in range(4):
                        c = half * 4 + cc
                        ins = e.transpose(out=ps[0:nrows, b, cc * 128:(cc + 1) * 128], in_=src_fn(c), identity=ident[:])
                    return ins
                sc.op("pe", tr, reads=list(src_reads) + ["ident"], writes=[rPS(b)])
                copy_op(alt_eng(), stg_slot_ap[0:nrows, half * 512:(half + 1) * 512], ps[0:nrows, b, :], [rPS(b)], [stg_rng])
            sc.dma("sp", dma_pairs_fn(), dsem, reads=[stg_rng], writes=[name], is_output=True)

        def nc_out():
            out_rows(lambda c: VO[:, c, 0:30], 30, IN[:, 0, :], rIN(0), lambda: [(ncp[:, :], IN[0:30, 0, :])], d_in[0], "ncp",
                     [rSTall])
            out_rows(lambda c: VO[:, c, 30:158], 128, IN[:, 1, :], rIN(1),
                     lambda: [(ncs[s_, 22:30, :], IN[s_ * 8:(s_ + 1) * 8, 1, :]) for s_ in range(16)], d_in[1], "ncs_new",
                     [rSTall])

        chk('ncout')
        s_pw2 = [next_w(), next_w()]
        nb_mod[0] = 6
        bs1 = 6
        bs2 = 7
        pend = []
        STATS_LAG = 3

        C4 = INt[:, 0:8 * 144].rearrange("p (c n) -> p c n", c=8)
        Z4 = INt[:, 1152:1152 + 576].bitcast(BF16).rearrange("p (c n) -> p c n", c=8)
        SQ4 = PTt[:, :].bitcast(BF16).rearrange("p (c n) -> p c n", c=8)
        rC4 = lambda m: ("IN", m * 576, (m + 1) * 576)
        rZ4 = lambda m: ("IN", 4608 + m * 288, 4608 + (m + 1) * 288)
        rSQ4 = lambda i: ("PT", i * 512, (i + 1) * 512)
        sq4_ctr = [0]

        def nsq4():
            i = sq4_ctr[0] % 8
            sq4_ctr[0] += 1
            return i

        LANE = {
            0: dict(C=lambda m, n: Cv[:, m, 0:n], rC=lambda m, n: rC(m, 0, n), Z=lambda m, n: ZB[:, m, 0:n],
                    rZ=lambda m, n: rZ(m, 0, n), sq=lambda i, n: SQR[:, i, 0:n], rsq=rSQ, nsq=nsq, bs=(6, 7),
                    mean=lambda n: ST[:, 0, 0:n], rstd=lambda n: ST[:, 1, 0:n], msq=lambda n: ST[:, 2, 0:n],
                    rmean=rST(0), rrstd=rST(1), rmsq=rST(2)),
            1: dict(C=lambda m, n: C4[:, m, 0:n], rC=lambda m, n: rC4(m), Z=lambda m, n: Z4[:, m, 0:n],
                    rZ=lambda m, n: rZ4(m), sq=lambda i, n: SQ4[:, i, 0:n], rsq=rSQ4, nsq=nsq4, bs=(4, 5),
                    mean=lambda n: ST[:, 3, 0:n], rstd=lambda n: ST[:, 3, 160:160 + n], msq=lambda n: ST[:, 3, 320:320 + n],
                    rmean=("ST", 6144, 6144 + 640), rrstd=("ST", 6144 + 640, 6144 + 1280), rmsq=("ST", 6144 + 1280, 6144 + 1920)),
        }

        def conv_chunk(ti, m, lane=0, db=None):
            c0, n = TILES[ti]
            L = LANE[lane]
            if db is None:
                db = dg_slot[(ti, m)]
            b = nb()
            if ti < 4:
                def conv(e, m=m, db=db, b=b, c0=c0, n=n):
                    ins = None
                    for k in range(31):
                        ins = e.matmul(ps[:, b, 0:n], lhsT=DGv(db, k), rhs=VE[:, m, c0 + k:c0 + k + n],
                                       start=(k == 0), stop=(k == 30))
                    return ins
                sc.op("pe", conv, reads=[rDGx(db), rVE(m, c0, n + 30)], writes=[rPS(b)])
            else:
                def conv(e, m=m, db=db, b=b):
                    ins = None
                    for k in range(31):
                        ins = e.matmul(ps[:, b, 0:16], lhsT=DGv(db, k), rhs=VE[:, m, 2048 + k:2048 + k + 16],
                                       start=(k == 0), stop=(k == 30))
                    for k in range(31):
                        ins = e.matmul(ps[:, b, 16:144].rearrange("p (s t) -> p s t", t=8), lhsT=DGv(db, k),
                                       rhs=VE[:, m, VS0:VS0 + 608].rearrange("p (s j) -> p s j", j=38)[:, :, k:k + 8],
                                       start=(k == 0), stop=(k == 30))
                    return ins
                sc.op("pe", conv, reads=[rDGx(db), rVE(m, 2048, 46), rVE(m, VS0, 608)], writes=[rPS(b)])
            while len(pend) >= STATS_LAG * (2 if merged[0] else 1):
                pend.pop(0)()
            i1 = L["nsq"]()
            i2 = L["nsq"]()
            bs1_, bs2_ = L["bs"]
            sc.op("act", lambda e: e.activation(out=L["C"](m, n), in_=ps[:, b, 0:n], func=AF.Identity,
                                                bias=vec(V_BDW, m), scale=1.0),
                  reads=[rPS(b), "vec"], writes=[L["rC"](m, n)])
            sc.op("act", lambda e: e.activation(out=L["sq"](i1, n), in_=ps[:, b, 0:n],
                                                func=AF.Identity, bias=vec(V_BDW, m), scale=1.0),
                  reads=[rPS(b), "vec"], writes=[L["rsq"](i1)])
            sc.op("act", lambda e: e.activation(out=L["sq"](i2, n), in_=ps[:, b, 0:n],
                                                func=AF.Square, bias=vec(V_BDW, m), scale=1.0),
                  reads=[rPS(b), "vec"], writes=[L["rsq"](i2)])

            def stats_mm():
                sc.op("pe", lambda e: e.matmul(ps[:, bs1_, 0:n], lhsT=ones[:], rhs=L["sq"](i1, n),
                                               start=(m == 0), stop=(m == 7)),
                      reads=[L["rsq"](i1), "ones"], writes=[rPS(bs1_)])
                sc.op("pe", lambda e: e.matmul(ps[:, bs2_, 0:n], lhsT=ones[:], rhs=L["sq"](i2, n),
                                               start=(m == 0), stop=(m == 7)),
                      reads=[L["rsq"](i2), "ones"], writes=[rPS(bs2_)])
            pend.append(stats_mm)

        def ln_tile(ti, lane=0, parts=None, skip_stats=False):
            c0, n = TILES[ti]
            L = LANE[lane]
            bs1_, bs2_ = L["bs"]
            if not skip_stats:
                while pend:
                    pend.pop(0)()
                sc.op("dve", lambda e: e.tensor_scalar(out=L["mean"](n), in0=ps[:, bs1_, 0:n], scalar1=1.0 / D,
                                                       scalar2=None, op0=ALU.mult),
                      reads=[rPS(bs1_)], writes=[L["rmean"]])
                sc.op("dve", lambda e: e.tensor_tensor(out=L["msq"](n), in0=L["mean"](n), in1=L["mean"](n), op=ALU.mult),
                      reads=[L["rmean"]], writes=[L["rmsq"]])
                sc.op("dve", lambda e: e.scalar_tensor_tensor(out=L["rstd"](n), in0=ps[:, bs2_, 0:n], scalar=1.0 / D,
                                                              in1=L["msq"](n), op0=ALU.mult, op1=ALU.subtract),
                      reads=[rPS(bs2_), L["rmsq"]], writes=[L["rrstd"]])
                sc.op("act", lambda e: e.activation(out=L["rstd"](n), in_=L["rstd"](n), func=AF.Ln, scale=1.0, bias=EPS),
                      reads=[L["rrstd"]], writes=[L["rrstd"]])
                sc.op("act", lambda e: e.activation(out=L["rstd"](n), in_=L["rstd"](n), func=AF.Exp, scale=-0.5),
                      reads=[L["rrstd"]], writes=[L["rrstd"]])
            if parts is not None:
                for (a, nn) in parts:
                    for m in range(8):
                        sc.op("dve", lambda e, m=m, a=a, nn=nn: e.tensor_tensor(out=Cv[:, m, a:a + nn], in0=Cv[:, m, a:a + nn],
                                                                               in1=ST[:, 0, a:a + nn], op=ALU.subtract),
                              reads=[rC(m, a, nn), L["rmean"]], writes=[rC(m, a, nn)])
                        sc.op("dve", lambda e, m=m, a=a, nn=nn: e.tensor_tensor(out=Cv[:, m, a:a + nn], in0=Cv[:, m, a:a + nn],
                                                                               in1=ST[:, 1, a:a + nn], op=ALU.mult),
                              reads=[rC(m, a, nn), L["rrstd"]], writes=[rC(m, a, nn)])
                        sc.op("act", lambda e, m=m, a=a, nn=nn: e.activation(out=ZB[:, m, a:a + nn], in_=Cv[:, m, a:a + nn], func=AF.Silu,
                                                                            bias=vec(V_LNB, m), scale=vec(V_LNG, m)),
                              reads=[rC(m, a, nn), "vec"], writes=[rZ(m, a, nn)])
                return
            for m in range(8):
                sc.op("dve", lambda e, m=m: e.tensor_tensor(out=L["C"](m, n), in0=L["C"](m, n), in1=L["mean"](n),
                                                            op=ALU.subtract),
                      reads=[L["rC"](m, n), L["rmean"]], writes=[L["rC"](m, n)])
                sc.op("dve", lambda e, m=m: e.tensor_tensor(out=L["C"](m, n), in0=L["C"](m, n), in1=L["rstd"](n),
                                                            op=ALU.mult),
                      reads=[L["rC"](m, n), L["rrstd"]], writes=[L["rC"](m, n)])
                sc.op("act", lambda e, m=m: e.activation(out=L["Z"](m, n), in_=L["C"](m, n), func=AF.Silu,
                                                         bias=vec(V_LNB, m), scale=vec(V_LNG, m)),
                      reads=[L["rC"](m, n), "vec"], writes=[L["rZ"](m, n)])

        def pw2_tile(ti, lane=0, a=0, nn=None):
            c0, n = TILES[ti]
            L = LANE[lane]
            if nn is None:
                nn = n
            for m in range(8):
                s = s_pw2[m // 4]
                b = nb()
                if lane == 0:
                    zsrc = lambda k: ZB[:, k, a:a + nn]
                    zrng = lambda k: rZ(k, a, nn)
                else:
                    zsrc = lambda k: L["Z"](k, n)
                    zrng = lambda k: L["rZ"](k, n)

                def mm2(e, m=m, s=s, b=b, zsrc=zsrc):
                    ins = None
                    for k in range(8):
                        ins = e.matmul(ps[:, b, 0:nn], lhsT=WA[s][:, k, (m % 4) * 128:(m % 4 + 1) * 128], rhs=zsrc(k),
                                       start=(k == 0), stop=(k == 7))
                    return ins
                sc.op("pe", mm2, reads=[rW(s)] + [zrng(k) for k in range(8)], writes=[rPS(b)])
                sc.op("dve", lambda e, m=m, b=b: e.scalar_tensor_tensor(
                    out=X[:, m, c0 + a:c0 + a + nn], in0=ps[:, b, 0:nn], scalar=vec(V_BPW2, m), in1=X[:, m, c0 + a:c0 + a + nn],
                    op0=ALU.add, op1=ALU.add), reads=[rPS(b), rX(m, c0 + a, nn), "vec"], writes=[rX(m, c0 + a, nn)])

        def next_chunk(ti, m, la=3):
            m2 = m + la
            if m2 < 8:
                return (ti, m2)
            if ti + 1 < 4:
                return (ti + 1, m2 - 8)
            return None

        merged = [False]
        dg_build(0, 2)
        dg_build(0, 3)
        dg_build(0, 4)
        for ti in range(4):
            if ti == 3:
                merged[0] = True
                nb_mod[0] = 4
            for m in range(8):
                conv_chunk(ti, m)
                if ti == 3:
                    conv_chunk(4, m, lane=1, db=dg_slot[(3, m)])
                if ti == 0 and m == 1:
                    nc_out()
                nx = next_chunk(ti, m)
                if nx is not None and m < 7 and nx not in dg_slot:
                    dg_build(*nx)
                if m == 2 and ti > 0:
                    pw2_tile(ti - 1)
            nx = next_chunk(ti, 7)
            if nx is not None and nx not in dg_slot:
                dg_build(*nx)
            if ti < 3:
                ln_tile(ti)
        while pend:
            pend.pop(0)()
        ln_tile(3, parts=[(0, 256)])
        pw2_tile(3, a=0, nn=256)
        ln_tile(3, parts=[(256, 256)], skip_stats=True)
        pw2_tile(3, a=256, nn=256)
        ln_tile(4, lane=1)
        pw2_tile(4, lane=1)
        nb_mod[0] = 8
        issue_wload()
        issue_wload()
        issue_wload()

        chk('conv')
        def mlp(gi, after_tile=None, skip_rms=False, last_tiles=None):
            if not skip_rms:
                rmsnorm_to_H(gi, *TILES[0])
                rmsnorm_to_H(gi, *TILES[1])
            for fb in range(4):
                for hb in range(2):
                    s = next_w()
                    for j in range(4):
                        hc = hb * 4 + j
                        for ti in range(5):
                            c0, n = TILES[ti]
                            b = nb()

                            def mm1(e, s=s, j=j, c0=c0, n=n, b=b):
                                ins = None
                                for k in range(8):
                                    ins = e.matmul(ps[:, b, 0:n], lhsT=WA[s][:, k, j * 128:(j + 1) * 128],
                                                   rhs=H[:, k, c0:c0 + n], start=(k == 0), stop=(k == 7))
                                return ins
                            sc.op("pe", mm1, reads=[rW(s)] + [rH(k, c0, n) for k in range(8)], writes=[rPS(b)])
                            i = nsq()
                            sc.op("act", lambda e, b=b, n=n, i=i: e.activation(out=SQR[:, i, 0:n], in_=ps[:, b, 0:n], func=AF.Relu),
                                  reads=[rPS(b)], writes=[rSQ(i)])
                            sc.op(alt_eng("dve", "pool"), lambda e, hc=hc, c0=c0, n=n, i=i: e.tensor_tensor(
                                out=HID[:, hc, c0:c0 + n], in0=SQR[:, i, 0:n], in1=SQR[:, i, 0:n], op=ALU.mult),
                                reads=[rSQ(i)], writes=[rHID(hc, c0, n)])
                            if (not skip_rms) and fb == 0 and hb == 0 and j == 0 and ti + 2 < 5:
                                rmsnorm_to_H(gi, *TILES[ti + 2])
                    issue_wload()
                sA = next_w()
                sB = next_w()
                last = (fb == 3)
                tl = TILES
                if last and last_tiles is not None:
                    tl = last_tiles
                order = [(ti, m) for ti in range(len(tl)) for m in range(8)] if last else \
                    [(ti, m) for m in range(8) for ti in range(5)]
                for (ti, m) in order:
                    c0, n = tl[ti]
                    b = nb()

                    def mm2(e, sA=sA, sB=sB, m=m, c0=c0, n=n, b=b):
                        ins = None
                        for k in range(8):
                            sl = sA if k < 4 else sB
                            ins = e.matmul(ps[:, b, 0:n], lhsT=WB[sl][:, k % 4, m * 128:(m + 1) * 128],
                                           rhs=HID[:, k, c0:c0 + n], start=(k == 0), stop=(k == 7))
                        return ins
                    sc.op("pe", mm2, reads=[rW(sA), rW(sB)] + [rHID(k, c0, n) for k in range(8)], writes=[rPS(b)])
                    sc.op("dve", lambda e, m=m, b=b, c0=c0, n=n: e.tensor_tensor(out=X[:, m, c0:c0 + n], in0=ps[:, b, 0:n],
                                                                                 in1=X[:, m, c0:c0 + n], op=ALU.add),
                          reads=[rPS(b), rX(m, c0, n)], writes=[rX(m, c0, n)])
                    if last and after_tile is not None and m == 7:
                        after_tile(ti)
                issue_wload()
                issue_wload()

        mlp(1)
        chk('mlp0')

        s_pool = next_w()
        PBR = Wreg[0:1, s_pool * 4096 + 2048:s_pool * 4096 + 3072]
        onesrow = Wreg[0:1, s_pool * 4096 + 3072:s_pool * 4096 + 3584]
        sc.op("pool", lambda e: e.memset(onesrow, 1.0), reads=[rW(s_pool)], writes=[("W", s_pool * 8192 + 6144, s_pool * 8192 + 7168)])
        HB = Sreg[:, 0:2 * 8 * 527].rearrange("p (b c n) -> p b c n", b=2, c=8)
        AB = Sreg[:, 8432:8432 + 2 * 4 * 527].rearrange("p (b c n) -> p b c n", b=2, c=4)
        A1T = Sreg[:, 12648:12648 + 2 * 527].rearrange("p (c n) -> p c n", c=2)
        WPS = Sreg[:, 13704:13704 + 2048].rearrange("p (g k n) -> p g k n", g=4, k=2)
        WPN = Sreg[:, 15752:15752 + 2048].rearrange("p (g k n) -> p g k n", g=4, k=2)
        P15 = Sreg[:, 17800:17800 + 8 * 16].rearrange("p (c n) -> p c n", c=8)
        rHB = lambda b, c, a=0, n=527: ("S", ((b * 8 + c) * 527 + a) * 2, ((b * 8 + c) * 527 + a + n) * 2)
        rAB = lambda b, c: ("S", (8432 + (b * 4 + c) * 527) * 2, (8432 + (b * 4 + c + 1) * 527) * 2)
        rA1T = lambda c: ("S", (12648 + c * 527) * 2, (12648 + (c + 1) * 527) * 2)
        rWPS = ("S", 13704 * 2, (13704 + 2048) * 2)
        rWPN = ("S", 15752 * 2, (15752 + 2048) * 2)
        rP15 = ("S", 17800 * 2, (17800 + 128) * 2)
        for g, w in enumerate((2, 4, 8, 16)):
            sc.op("dve", lambda e, g=g, w=w: e.tensor_scalar(out=WPS[:, g, :, :], in0=WP[s_pool][:, g, :, :], scalar1=1.0 / w,
                                                             scalar2=None, op0=ALU.mult),
                  reads=[rW(s_pool)], writes=[rWPS])
        sc.op("dve", lambda e: e.tensor_scalar(out=WPN[:, :, :, :], in0=WP[s_pool][:, :, :, :], scalar1=-1.0, scalar2=None,
                                                op0=ALU.mult), reads=[rW(s_pool)], writes=[rWPN])
        RI = 3
        def stageA(ti):
            c0, n = TILES[ti]
            hb_ = ti % 2
            rms_stats(c0, n, ri=RI)
            if ti < 4:
                next_ = 15 + n
                if ti == 0:
                    sc.op("pool", lambda e, hb_=hb_: e.memset(HB[:, hb_, :, 0:15], 0.0), writes=[rHB(hb_, c, 0, 15) for c in range(8)])
                else:
                    sc.op("pool", lambda e, hb_=hb_: e.tensor_copy(out=HB[:, hb_, :, 0:15], in_=HB[:, 1 - hb_, :, 512:527]),
                          reads=[rHB(1 - hb_, c, 512, 15) for c in range(8)], writes=[rHB(hb_, c, 0, 15) for c in range(8)])
                for c in range(8):
                    rms_apply(2, c, c0, n, HB[:, hb_, c, 15:15 + n], [rHB(hb_, c, 15, n)], ri=RI)
            else:
                next_ = 31 + 368
                sc.op("pool", lambda e, hb_=hb_: e.tensor_copy(out=HB[:, hb_, :, 0:15], in_=HB[:, 1 - hb_, :, 512:527]),
                      reads=[rHB(1 - hb_, c, 512, 15) for c in range(8)], writes=[rHB(hb_, c, 0, 15) for c in range(8)])
                for q in range(2):
                    s_ = in_ctr[0] % 2
                    in_ctr[0] += 1
                    sc.dma("sp", [(IN[0:120, s_, :], spool[q * 8:(q + 1) * 8].rearrange("s j d -> (s j) d"))], d_in[s_],
                           writes=[rIN(s_)])
                    for half in range(2):
                        b = nb()

                        def tr(e, half=half, b=b, s_=s_):
                            ins = None
                            for cc in range(4):
                                c = half * 4 + cc
                                ins = e.transpose(out=ps[:, b, cc * 120:(cc + 1) * 120],
                                                  in_=IN[0:120, s_, c * 128:(c + 1) * 128], identity=ident[0:120, 0:120])
                            return ins
                        sc.op("pe", tr, reads=[rIN(s_), "ident"], writes=[rPS(b)])
                        src = ps[:, b, 0:480].rearrange("p (c s j) -> p c s j", c=4, s=8)
                        dst = HB[:, hb_, half * 4:(half + 1) * 4, 31 + q * 184:31 + (q + 1) * 184] \
                            .rearrange("p c (s j) -> p c s j", j=23)[:, :, :, 0:15]
                        copy_op(alt_eng(), dst, src, [rPS(b)], [rHB(hb_, half * 4 + cc, 31 + q * 184, 184) for cc in range(4)])
                for c in range(8):
                    rms_apply(2, c, 2048, 16, HB[:, hb_, c, 15:31], [rHB(hb_, c, 15, 16)], ri=RI)
                    sc.op("dve", lambda e, c=c, hb_=hb_: e.scalar_tensor_tensor(
                        out=HB[:, hb_, c, 31:399].rearrange("p (s j) -> p s j", j=23)[:, :, 15:23],
                        in0=X[:, c, 2064:2192].rearrange("p (s t) -> p s t", t=8), scalar=g32(2, c),
                        in1=ST[:, RI, 16:144].rearrange("p (s t) -> p s t", t=8), op0=ALU.mult, op1=ALU.mult),
                        reads=[rX(c, 2064, 128), rST(RI), ("g32", 2, 3)], writes=[rHB(hb_, c, 31, 368)])
                    sc.op("dve", lambda e, c=c: e.scalar_tensor_tensor(
                        out=VO[:, c, 15:30], in0=X[:, c, 2049:2064], scalar=g32(2, c), in1=ST[:, RI, 1:16],
                        op0=ALU.mult, op1=ALU.mult), reads=[rX(c, 2049, 15), rST(RI), ("g32", 2, 3)], writes=[("ST", 0, 6144)])
                    sc.op("dve", lambda e, c=c: e.scalar_tensor_tensor(
                        out=VO[:, c, 30:158], in0=X[:, c, 2064:2192], scalar=g32(2, c), in1=ST[:, RI, 16:144],
                        op0=ALU.mult, op1=ALU.mult), reads=[rX(c, 2064, 128), rST(RI), ("g32", 2, 3)], writes=[("ST", 0, 6144)])
            for c in (4, 5):
                sc.op("pool" if c % 2 else "dve", lambda e, c=c, hb_=hb_, next_=next_: e.tensor_tensor(
                    out=AB[:, hb_, c - 4, 1:next_], in0=HB[:, hb_, c, 1:next_], in1=HB[:, hb_, c, 0:next_ - 1], op=ALU.add),
                    reads=[rHB(hb_, c, 0, next_)], writes=[rAB(hb_, c - 4)])
            for c in (6, 7):
                eng = "pool" if c % 2 else "dve"
                sc.op(eng, lambda e, c=c, hb_=hb_, next_=next_: e.tensor_tensor(
                    out=A1T[:, c - 6, 1:next_], in0=HB[:, hb_, c, 1:next_], in1=HB[:, hb_, c, 0:next_ - 1], op=ALU.add),
                    reads=[rHB(hb_, c, 0, next_)], writes=[rA1T(c - 6)])
                sc.op(eng, lambda e, c=c, hb_=hb_, next_=next_: e.tensor_tensor(
                    out=AB[:, hb_, c - 4, 3:next_], in0=A1T[:, c - 6, 3:next_], in1=A1T[:, c - 6, 1:next_ - 2], op=ALU.add),
                    reads=[rA1T(c - 6)], writes=[rAB(hb_, c - 4)])
            if ti == 0:
                PA = PT[:, 0, 0:240].rearrange("p (c n) -> p c n", c=8)
                PB = PT[:, 1, 0:240].rearrange("p (c n) -> p c n", c=8)
                IV8 = PT[:, 1, 256:384].rearrange("p (c n) -> p c n", c=8)
                rPA = ("PT", 0, 960)
                rPB = ("PT", 2048, 2048 + 960)
                rIV8 = ("PT", 2048 + 1024, 2048 + 1536)
                for c in range(8):
                    sc.op("pool", lambda e, c=c: e.tensor_copy(out=IV8[:, c, :], in_=INVC[:, (c // 2) * 16:(c // 2 + 1) * 16]),
                          reads=[("invc", 0, 64)], writes=[rIV8])
                hb_all = [rHB(hb_, c, 0, 30) for c in range(8)]
                sc.op("dve", lambda e, hb_=hb_: e.tensor_tensor(out=PA[:, :, 1:30], in0=HB[:, hb_, :, 1:30],
                                                               in1=HB[:, hb_, :, 0:29], op=ALU.add),
                      reads=hb_all, writes=[rPA])
                sc.op("dve", lambda e: e.tensor_tensor(out=PB[:, 2:8, 3:30], in0=PA[:, 2:8, 3:30], in1=PA[:, 2:8, 1:28], op=ALU.add),
                      reads=[rPA], writes=[rPB])
                sc.op("dve", lambda e: e.tensor_tensor(out=PA[:, 4:8, 7:30], in0=PB[:, 4:8, 7:30], in1=PB[:, 4:8, 3:26], op=ALU.add),
                      reads=[rPB], writes=[rPA])
                sc.op("dve", lambda e: e.tensor_tensor(out=PB[:, 6:8, 15:30], in0=PA[:, 6:8, 15:30], in1=PA[:, 6:8, 7:22], op=ALU.add),
                      reads=[rPA], writes=[rPB])
                for (buf, rbuf, cA) in ((PA, rPA, 0), (PB, rPB, 2), (PA, rPA, 4), (PB, rPB, 6)):
                    sc.op("dve", lambda e, buf=buf, cA=cA: e.tensor_tensor(out=buf[:, cA:cA + 2, 15:30], in0=buf[:, cA:cA + 2, 15:30],
                                                                           in1=IV8[:, cA:cA + 2, 0:15], op=ALU.mult),
                          reads=[rbuf, rIV8], writes=[rbuf])
                    sc.op("dve", lambda e, buf=buf, cA=cA, hb_=hb_: e.tensor_tensor(out=P15[:, cA:cA + 2, 0:15], in0=buf[:, cA:cA + 2, 15:30],
                                                                                    in1=HB[:, hb_, cA:cA + 2, 15:30], op=ALU.subtract),
                          reads=[rbuf, rHB(hb_, cA, 15, 15), rHB(hb_, cA + 1, 15, 15)], writes=[rP15])

        def stageB(ti):
            c0, n = TILES[ti]
            hb_ = ti % 2
            for m in range(8):
                g = m // 2
                mi = m % 2
                b = nb()
                stride = (1, 1, 2, 4)[g]

                def mmp(e, g=g, mi=mi, b=b, n=n, ti=ti, hb_=hb_, stride=stride, m=m):
                    ins = None
                    if ti < 4:
                        segs = [("flat", 15 if ti == 0 else 0, n, 15 + (15 if ti == 0 else 0))]
                    else:
                        segs = [("flat", 0, 16, 15), ("samp", 16, 128, 0)]
                    for (kind, pc0, cnt_, e0) in segs:
                        pc1 = n if kind == "flat" and ti < 4 else pc0 + cnt_
                        terms = []
                        for ki in range(2):
                            cs = 2 * g + ki
                            for j in range(4 if g > 0 else 2):
                                sh = j * stride
                                srcbuf = (HB[:, hb_, cs, :] if g < 2 else AB[:, hb_, cs - 4, :])
                                terms.append((WPS[:, g, ki, mi * 128:(mi + 1) * 128], srcbuf, sh))
                            terms.append((WPN[:, g, ki, mi * 128:(mi + 1) * 128], HB[:, hb_, cs, :], 0))
                        for idx, (lw, srcbuf, sh) in enumerate(terms):
                            if kind == "flat":
                                ncols = pc1 - pc0
                                rhs = srcbuf[:, e0 - sh:e0 - sh + ncols]
                                outp = ps[:, b, pc0:pc1]
                            else:
                                rhs = srcbuf[:, 31:399].rearrange("p (s j) -> p s j", j=23)[:, :, 15 - sh:23 - sh]
                                outp = ps[:, b, 16:144].rearrange("p (s t) -> p s t", t=8)
                            ins = e.matmul(outp, lhsT=lw, rhs=rhs, start=(idx == 0), stop=False)
                        if kind == "flat":
                            ins = e.matmul(ps[:, b, pc0:pc1], lhsT=PBR[0:1, m * 128:(m + 1) * 128], rhs=onesrow[0:1, 0:pc1 - pc0],
                                           start=False, stop=True)
                        else:
                            ins = e.matmul(ps[:, b, 16:144], lhsT=PBR[0:1, m * 128:(m + 1) * 128], rhs=onesrow[0:1, 0:128],
                                           start=False, stop=True)
                    if ti == 0:
                        for ki in range(2):
                            ins = e.matmul(ps[:, b, 0:15], lhsT=WP[s_pool][:, g, ki, mi * 128:(mi + 1) * 128],
                                           rhs=P15[:, 2 * g + ki, 0:15], start=(ki == 0), stop=False)
                        ins = e.matmul(ps[:, b, 0:15], lhsT=PBR[0:1, m * 128:(m + 1) * 128], rhs=onesrow[0:1, 0:15],
                                       start=False, stop=True)
                    return ins
                rd = [rW(s_pool), rWPS, rWPN, rP15] + [rHB(hb_, 2 * g + ki, 0, 527) for ki in range(2)]
                if g >= 2:
                    rd += [rAB(hb_, 2 * g + ki - 4) for ki in range(2)]
                sc.op("pe", mmp, reads=rd, writes=[rPS(b)])
                sc.op("dve", lambda e, m=m, b=b, n=n, c0=c0: e.scalar_tensor_tensor(
                    out=X[:, m, c0:c0 + n], in0=ps[:, b, 0:n], scalar=vec(V_PS, m), in1=X[:, m, c0:c0 + n],
                    op0=ALU.mult, op1=ALU.add), reads=[rPS(b), rX(m, c0, n), "vec"], writes=[rX(m, c0, n)])

        def np_out():
            out_rows(lambda c: VO[:, c, 15:30], 15, IN[:, 0, :], rIN(0), lambda: [(npp[:, :], IN[0:15, 0, :])],
                     d_in[0], "npp", [("ST", 0, 6144)])
            out_rows(lambda c: VO[:, c, 30:158], 128, IN[:, 1, :], rIN(1),
                     lambda: [(nps[s_, 7:15, :], IN[s_ * 8:(s_ + 1) * 8, 1, :]) for s_ in range(16)], d_in[1], "nps_new",
                     [("ST", 0, 6144)])

        stageA(0)
        for ti in range(5):
            if ti + 1 < 5:
                stageA(ti + 1)
            stageB(ti)
            if ti == 3:
                np_out()
            if ti >= 1:
                rmsnorm_to_H(3, *TILES[ti - 1])
        rmsnorm_to_H(3, *TILES[4])
        issue_wload()

        ost_ctr = [0]

        FTL = [(0, 512), (512, 512), (1024, 512), (1536, 256), (1792, 256), (2048, 144)]
        NF = len(FTL)

        def final_S(fi):
            c0, n = FTL[fi]
            yb = fi % 2
            rms_stats(c0, n, ri=1)
            for c in range(8):
                rms_apply(4, c, c0, n, YT[:, yb, c, 0:n], [rYT(yb, c, 0, n)], ri=1)

        def final_Tr(fi):
            c0, n = FTL[fi]
            yb = fi % 2
            blocks = []
            if fi < NF - 1:
                for blk in range(n // 128):
                    cc = c0 + blk * 128
                    if cc == 0:
                        blocks.append((0, 128, 16, yp[0:112, :]))
                    else:
                        blocks.append((blk * 128, 128, 0, yp[cc - 16:cc + 112, :]))
            else:
                blocks.append((0, 16, 0, yp[2032:2048, :]))
                blocks.append((16, 128, 0, ys[:, :]))
            for (o, nc_, skip, dst) in blocks:
                sap, srng, sd = STG[ost_ctr[0] % 4]
                ost_ctr[0] += 1
                out_rows(lambda c, yb=yb, o=o, nc_=nc_: YT[:, yb, c, o:o + nc_], nc_, sap, srng,
                         lambda dst=dst, sap=sap, skip=skip, nc_=nc_: [(dst, sap[skip:nc_, :])], sd, ("y", fi, o),
                         [rYT(yb, c, 0, 512) for c in range(8)])

        def after_last(fi):
            if fi >= 2:
                final_Tr(fi - 2)
            if fi >= 1:
                final_S(fi - 1)
            if fi == NF - 1:
                final_Tr(NF - 2)
                final_S(NF - 1)
                final_Tr(NF - 1)

        chk('pool')
        mlp(3, skip_rms=True, after_tile=after_last, last_tiles=FTL)
        chk('mlp1')

        sc.finish()
        sc.emit()
    return nc


_NC_CACHE = {}
_STOP = None
_NOW = False


def _to_pc(v):
    return np.ascontiguousarray(np.asarray(v, np.float32).reshape(8, 128).T)


def kernel(x_prompt, x_sample, state_conv, state_pool, meta_tokens, norm_mix_g, norm_mlp_g,
           conv_w_pw1, conv_b_pw1, conv_w_dw, conv_b_dw, conv_ln_g, conv_ln_b,
           conv_w_pw2, conv_b_pw2, pool_w, pool_b, pool_scale, mlp_w1, mlp_w2, final_g):
    f = lambda a: np.ascontiguousarray(np.asarray(a, dtype=np.float32))
    x_prompt, x_sample, state_conv, state_pool = f(x_prompt), f(x_sample), f(state_conv), f(state_pool)
    vec_list = [norm_mix_g[0], norm_mix_g[1], norm_mlp_g[0], norm_mlp_g[1], final_g,
                np.asarray(conv_b_pw1)[0, :D], np.asarray(conv_b_pw1)[0, D:], conv_b_dw[0], conv_ln_g[0], conv_ln_b[0],
                conv_b_pw2[0], np.asarray(pool_b)[0].reshape(-1), pool_scale[0]]
    vec_list += [np.asarray(conv_w_dw)[0, k] for k in range(31)]
    vecs = np.ascontiguousarray(np.concatenate([_to_pc(v) for v in vec_list], axis=1))
    shared = {
        "meta": f(meta_tokens), "vecs": vecs, "pbrow": f(np.asarray(pool_b)[0].reshape(1, D)), "w_pw1": f(np.asarray(conv_w_pw1)[0]), "w_pw2": f(np.asarray(conv_w_pw2)[0]),
        "w_pool": f(np.asarray(pool_w)[0]), "w_m1": f(mlp_w1), "w_m2": f(mlp_w2),
    }
    in_maps = []
    for c in range(NCORES):
        m = dict(shared)
        m["xp"] = x_prompt[c]
        m["xs"] = x_sample[16 * c:16 * (c + 1)].reshape(128, D)
        m["sconv"] = state_conv[0, 16 * c:16 * (c + 1)]
        m["spool"] = state_pool[0, 16 * c:16 * (c + 1)]
        in_maps.append(m)
    if "nc" not in _NC_CACHE:
        _NC_CACHE["nc"] = build_program(_STOP)
    nc = _NC_CACHE["nc"]
    res = run_bass_kernel_spmd(nc, in_maps, core_ids=list(range(NCORES)))
    R = res.results
    if _STOP is not None:
        _NC_CACHE["dbg"] = R
    y_prompt = np.stack([R[c]["yp"] for c in range(NCORES)], axis=0)
    y_sample = np.concatenate([R[c]["ys"].reshape(16, 8, D) for c in range(NCORES)], axis=0)
    ncp = np.stack([R[c]["ncp"] for c in range(NCORES)], axis=0)[None]
    ncs = np.concatenate([R[c]["ncs"] for c in range(NCORES)], axis=0)[None]
    npp = np.stack([R[c]["npp"] for c in range(NCORES)], axis=0)[None]
    nps = np.concatenate([R[c]["nps"] for c in range(NCORES)], axis=0)[None]
    return (y_prompt.astype(np.float32), y_sample.astype(np.float32), ncp.astype(np.float32),
            ncs.astype(np.float32), npp.astype(np.float32), nps.astype(np.float32))
```

```python
import numpy as np
from contextlib import ExitStack
import concourse.bass as bass
import concourse.mybir as mybir
from concourse.bass_utils import run_bass_kernel_spmd

F32 = mybir.dt.float32
BF16 = mybir.dt.bfloat16
AF = mybir.ActivationFunctionType
ALU = mybir.AluOpType

ENGS = ("pe", "act", "dve", "pool", "sp")
NCORES = 8
D = 1024
NT = 2192
NP = 2064
EPS = 1e-6
TILES = [(0, 512), (512, 512), (1024, 512), (1536, 512), (2048, 144)]
VW = 2702
VS0 = 2094
NSLOT = 3
DEBUG_DUMP = False
SELF_DIST = 2

V_GMIX0, V_GMIX1, V_GMLP0, V_GMLP1, V_GFIN, V_BA, V_BG, V_BDW, V_LNG, V_LNB, V_BPW2, V_PB, V_PS = range(13)
V_WDW = 13
NV = 13 + 31


class DSem:
    def __init__(self, handle, key):
        self.h = handle
        self.key = key
        self.count = 0


class Sched:
    def __init__(self, nc, stack):
        self.nc = nc
        self.stack = stack
        self.lists = {e: [] for e in ENGS}
        self.sem = {e: stack.enter_context(nc.semaphore("s_" + e)) for e in ENGS if e != "sp"}
        self.count = {e: 0 for e in ENGS}
        self.opidx = {e: 0 for e in ENGS}
        self.waited = {e: {} for e in ENGS}
        self.reg = {}
        self.semh = {e: self.sem[e] for e in self.sem}
        self.ndsem = 0
        self.out_events = []
        self.nwaits = 0
        self.dead = False

    def dsem(self, name=None):
        self.ndsem += 1
        key = "d%d" % self.ndsem
        h = self.stack.enter_context(self.nc.semaphore(name or key))
        self.semh[key] = h
        return DSem(h, key)

    @staticmethod
    def _norm(k):
        if isinstance(k, tuple) and len(k) == 3 and isinstance(k[1], int) and isinstance(k[2], int) \
                and isinstance(k[0], str):
            return k
        return (k, 0, 1)

    def _deps_and_register(self, reads, writes, ev):
        deps = []
        for is_w, lst in ((False, reads), (True, writes)):
            for k in lst:
                rg, lo, hi = self._norm(k)
                ents = self.reg.setdefault(rg, [])
                keep = []
                merged = False
                for e in ents:
                    elo, ehi, ew, eev = e
                    if ehi <= lo or elo >= hi:
                        keep.append(e)
                        continue
                    if eev is ev:
                        if is_w and lo <= elo and ehi <= hi:
                            continue
                        keep.append(e)
                        continue
                    if is_w or ew:
                        deps.append(eev)
                    if is_w and lo <= elo and ehi <= hi:
                        continue
                    if (not is_w) and (not ew) and elo == lo and ehi == hi and eev[1] == ev[1]:
                        e[3] = ev
                        merged = True
                    keep.append(e)
                if not merged:
                    keep.append([lo, hi, is_w, ev])
                self.reg[rg] = keep
        return deps

    def _emit_waits(self, eng, deps, is_dma):
        waits = {}
        for (deng, skey, val, didx, d_is_dma) in deps:
            if deng == eng and not d_is_dma and not is_dma:
                if eng == "pe":
                    continue
                if eng == "pool" and didx < self.opidx[eng] - SELF_DIST:
                    continue
            if self.waited[eng].get(skey, 0) >= val:
                continue
            if waits.get(skey, 0) < val:
                waits[skey] = val
        out = []
        for skey, val in waits.items():
            self.waited[eng][skey] = val
            out.append((self.semh[skey], val))
            self.nwaits += 1
        return out

    def op(self, eng, fn, reads=(), writes=()):
        if self.dead:
            return None
        self.count[eng] += 1
        ev = (eng, eng, self.count[eng], self.opidx[eng], False)
        deps = self._deps_and_register(reads, writes, ev)
        waits = self._emit_waits(eng, deps, False)
        self.opidx[eng] += 1
        self.lists[eng].append((waits, fn, [(self.sem[eng], 1)]))
        return ev

    def dma(self, eng, pairs, dsem, reads=(), writes=(), is_output=False, slow=False):
        if self.dead:
            return None
        dsem.count += 16 * len(pairs)
        ev = ("dma_" + eng, dsem.key, dsem.count, -1, True)
        deps = self._deps_and_register(reads, writes, ev)
        waits = self._emit_waits(eng, deps, True)
        self.opidx[eng] += 1

        def fn(e, pairs=pairs, h=dsem.h, slow=slow):
            for (o, i) in pairs:
                if slow:
                    e.dma_start(out=o, in_=i, allow_slow_non_contiguous=True).then_inc(h, 16)
                else:
                    e.dma_start(out=o, in_=i).then_inc(h, 16)
            return None

        self.lists[eng].append((waits, fn, None))
        if is_output:
            self.out_events.append(ev)
        return ev

    def finish(self):
        waits = {}
        for ev in self.out_events:
            if waits.get(ev[1], 0) < ev[2]:
                waits[ev[1]] = ev[2]
        for e in self.sem:
            if self.count[e] > 0:
                waits[e] = self.count[e]
        for rg, ents in self.reg.items():
            for e in ents:
                ev = e[3]
                if ev[4] and waits.get(ev[1], 0) < ev[2]:
                    waits[ev[1]] = ev[2]
        self.lists["sp"].append(([(self.semh[k], v) for k, v in waits.items()], None, None))

    def emit(self):
        nc = self.nc
        lists = self.lists

        def replay(name, eng):
            for (waits, fn, incs) in lists[name]:
                for (h, v) in waits:
                    eng.wait_ge(h, v)
                if fn is None:
                    continue
                ins = fn(eng)
                if incs:
                    for (h, a) in incs:
                        ins.then_inc(h, a)

        with nc.Block() as block:
            @block.tensor
            def _(e):
                replay("pe", e)

            @block.scalar
            def _(e):
                replay("act", e)

            @block.vector
            def _(e):
                replay("dve", e)

            @block.gpsimd
            def _(e):
                replay("pool", e)

            @block.sync
            def _(e):
                replay("sp", e)


class _Stop(Exception):
    pass


def build_program(stop=None):
    nc = bass.Bass("TRN2", target_bir_lowering=False)
    dt_in = lambda name, shape: nc.dram_tensor(name, shape, F32, kind="ExternalInput").ap()
    dt_out = lambda name, shape: nc.dram_tensor(name, shape, F32, kind="ExternalOutput").ap()
    xp = dt_in("xp", [2048, D])
    xs = dt_in("xs", [128, D])
    sconv = dt_in("sconv", [16, 30, D])
    spool = dt_in("spool", [16, 15, D])
    meta = dt_in("meta", [16, D])
    vecs = dt_in("vecs", [128, NV * 8])
    pbrow = dt_in("pbrow", [1, D])
    w_pw1 = dt_in("w_pw1", [D, 2 * D])
    w_pw2 = dt_in("w_pw2", [D, D])
    w_pool = dt_in("w_pool", [4, 256, 256])
    w_m1 = dt_in("w_m1", [2, D, 4 * D])
    w_m2 = dt_in("w_m2", [2, 4 * D, D])
    yp = dt_out("yp", [2048, D])
    ys = dt_out("ys", [128, D])
    ncp = dt_out("ncp", [30, D])
    ncs = dt_out("ncs", [16, 30, D])
    npp = dt_out("npp", [15, D])
    nps = dt_out("nps", [16, 15, D])
    if stop is not None:
        dbg_x = dt_out("dbg_x", [128, 8 * NT])
        dbg_h = nc.dram_tensor("dbg_h", [128, 8 * NT], BF16, kind="ExternalOutput").ap()
        dbg_s = nc.dram_tensor("dbg_s", [128, 8 * VW], BF16, kind="ExternalOutput").ap()

    with ExitStack() as st:
        sc = Sched(nc, st)
        SB = lambda name, shape, dt: st.enter_context(nc.sbuf_tensor(name, shape, dt))
        Xt = SB("X", [128, 8 * NT], F32)
        Hreg = SB("Hreg", [128, 8 * NT], BF16)
        Sreg = SB("Sreg", [128, 8 * VW], BF16)
        Wreg = SB("Wreg", [128, NSLOT * 4096], BF16)
        ZBt = SB("ZB", [128, 8 * 512], BF16)
        SQRt = SB("SQR", [128, 8 * 512], BF16)
        STt = SB("ST", [128, 4 * 512], F32)
        INt = SB("IN", [128, 2 * 1024], F32)
        PTt = SB("PT", [128, 2 * 512], F32)
        VEC = SB("VEC", [128, NV * 8], F32)
        G32 = SB("G32", [128, 5 * 8], F32)
        PBS = SB("PBS", [128, 8], F32)
        INVC = SB("INVC", [128, 4 * 16], F32)
        ident = SB("ident", [128, 128], F32)
        identb = SB("identb", [128, 128], BF16)
        ones = SB("ones", [128, 128], BF16)
        ps = st.enter_context(nc.psum_tensor("ps", [128, 8, 512], F32))

        X = Xt[:, :].rearrange("p (c n) -> p c n", c=8)
        H = Hreg[:, :].rearrange("p (c n) -> p c n", c=8)
        Cv = Hreg[:, 0:8192].bitcast(F32).rearrange("p (c n) -> p c n", c=8)
        DG = Hreg[:, 8192:8192 + 2 * 31 * 128].rearrange("p (b k n) -> p b k n", b=2, k=31)
        HF = Hreg[:, 0:2 * 8 * 527 * 2].bitcast(F32).rearrange("p (b c n) -> p b c n", b=2, c=8)
        YT = Hreg[:, 0:2 * 8 * 512 * 2].bitcast(F32).rearrange("p (b c n) -> p b c n", b=2, c=8)
        VE = Sreg[:, :].rearrange("p (c n) -> p c n", c=8)
        HID = Sreg[:, 0:8 * NT].rearrange("p (c n) -> p c n", c=8)
        OST = Sreg[:, 0:10 * 2048].bitcast(F32).rearrange("p (s n) -> p s n", s=10)
        WA = [Wreg[:, s * 4096:(s + 1) * 4096].rearrange("p (k m) -> p k m", k=8) for s in range(NSLOT)]
        WB = [Wreg[:, s * 4096:(s + 1) * 4096].rearrange("p (k m) -> p k m", k=4) for s in range(NSLOT)]
        WP = [Wreg[:, s * 4096:s * 4096 + 2048].rearrange("p (g k n) -> p g k n", g=4, k=2) for s in range(NSLOT)]
        ZB = ZBt[:, :].rearrange("p (c n) -> p c n", c=8)
        SQR = SQRt[:, :].rearrange("p (c n) -> p c n", c=8)
        ST = STt[:, :].rearrange("p (c n) -> p c n", c=4)
        VO = STt[:, 0:8 * 158].rearrange("p (c n) -> p c n", c=8)
        IN = INt[:, :].rearrange("p (s n) -> p s n", s=2)
        ZBF = ZBt[:, :].bitcast(F32).rearrange("p (s n) -> p s n", s=2)
        PT = PTt[:, :].rearrange("p (s n) -> p s n", s=2)
        vec = lambda v, c: VEC[:, v * 8 + c:v * 8 + c + 1]
        g32 = lambda i, c: G32[:, i * 8 + c:i * 8 + c + 1]

        rX = lambda c, a, n: ("X", (c * NT + a) * 4, (c * NT + a + n) * 4)
        rH = lambda c, a, n: ("H", (c * NT + a) * 2, (c * NT + a + n) * 2)
        rC = lambda c, a=0, n=512: ("H", (c * 512 + a) * 4, (c * 512 + a + n) * 4)
        rDG = lambda b, k0=0, k1=31: ("H", 16384 + (b * 31 + k0) * 256, 16384 + (b * 31 + k1) * 256)
        rHF = lambda b, c, a=0, n=527: ("H", ((b * 8 + c) * 527 + a) * 4, ((b * 8 + c) * 527 + a + n) * 4)
        rYT = lambda b, c, a=0, n=512: ("H", ((b * 8 + c) * 512 + a) * 4, ((b * 8 + c) * 512 + a + n) * 4)
        rVE = lambda c, a, n: ("S", (c * VW + a) * 2, (c * VW + a + n) * 2)
        rHID = lambda c, a, n: ("S", (c * NT + a) * 2, (c * NT + a + n) * 2)
        rOST = lambda s: ("S", s * 4096, (s + 1) * 4096)
        rW = lambda s: ("W", s * 8192, (s + 1) * 8192)
        rZ = lambda c, a=0, n=512: ("ZB", (c * 512 + a) * 2, (c * 512 + a + n) * 2)
        rSQ = lambda i: ("SQR", i * 1024, (i + 1) * 1024)
        rST = lambda i: ("ST", i * 2048, (i + 1) * 2048)
        rSTall = ("ST", 0, 8192)
        rIN = lambda s: ("IN", s * 4096, (s + 1) * 4096)
        rPT = lambda s: ("PT", s * 2048, (s + 1) * 2048)
        rPS = lambda b: ("PS", b, b + 1)

        def chk(stage):
            if stop == stage and not sc.dead:
                dd = sc.dsem("d_dbg")
                sc.dma("sp", [(dbg_x, Xt[:, :])], dd, reads=[("X", 0, 8 * NT * 4)], writes=["dbg_x"], is_output=True)
                sc.dma("sp", [(dbg_h, Hreg[:, :])], dd, reads=[("H", 0, 8 * NT * 2)], writes=["dbg_h"], is_output=True)
                sc.dma("sp", [(dbg_s, Sreg[:, :])], dd, reads=[("S", 0, 8 * VW * 2)], writes=["dbg_s"], is_output=True)
                sc.dead = True

        bank_ctr = [0]

        nb_mod = [8]

        def nb():
            b = bank_ctr[0] % nb_mod[0]
            bank_ctr[0] += 1
            return b

        sq_ctr = [0]

        def nsq():
            i = sq_ctr[0] % 8
            sq_ctr[0] += 1
            return i

        alt = [0]

        def alt_eng(a="act", b="dve"):
            alt[0] += 1
            return a if alt[0] % 2 else b

        def copy_op(eng, out, in_, reads, writes):
            if eng == "act":
                sc.op("act", lambda e: e.activation(out=out, in_=in_, func=AF.Copy), reads=reads, writes=writes)
            else:
                sc.op(eng, lambda e: e.tensor_copy(out=out, in_=in_), reads=reads, writes=writes)

        d_const = sc.dsem("d_const")
        d_in = [sc.dsem("d_in0"), sc.dsem("d_in1"), sc.dsem("d_in2"), sc.dsem("d_in3")]
        STG = [(IN[:, 0, :], ("IN", 0, 4096), d_in[0]), (IN[:, 1, :], ("IN", 4096, 8192), d_in[1]),
               (ZBF[:, 0, :], ("ZB", 0, 4096), d_in[2]), (ZBF[:, 1, :], ("ZB", 4096, 8192), d_in[3])]
        d_w = [sc.dsem("d_w%d" % s) for s in range(NSLOT)]
        d_ost = [sc.dsem("d_ost%d" % s) for s in range(10)]
        d_misc = sc.dsem("d_misc")
        d_pt = sc.dsem("d_pt")

        sc.op("pool", lambda e: e.memset(ident[:], 0.0), writes=["ident"])
        sc.op("pool", lambda e: e.affine_select(out=ident[:], in_=ident[:], compare_op=ALU.not_equal, fill=1.0,
                                                base=0, pattern=[[-1, 128]], channel_multiplier=1),
              reads=["ident"], writes=["ident"])
        sc.op("pool", lambda e: e.memset(ones[:], 1.0), writes=["ones"])
        sc.op("pool", lambda e: e.tensor_copy(out=identb[:], in_=ident[:]), reads=["ident"], writes=["identb"])

        for g, w in enumerate((2, 4, 8, 16)):
            sc.op("pool", lambda e, g=g, w=w: e.memset(INVC[:, g * 16:(g + 1) * 16], 1.0 / w),
                  writes=[("invc", g * 16, (g + 1) * 16)])
            for t in range(w - 1):
                sc.op("pool", lambda e, g=g, t=t: e.memset(INVC[:, g * 16 + t:g * 16 + t + 1], 1.0 / (t + 1)),
                      writes=[("invc", g * 16 + t, g * 16 + t + 1)])
        sc.dma("sp", [(VEC[:], vecs)], d_const, writes=["vec"])
        sc.op("pool", lambda e: e.memset(VE[:, :, 0:30], 0.0), writes=[rVE(c, 0, 30) for c in range(8)])
        for i, v in enumerate((V_GMIX0, V_GMLP0, V_GMIX1, V_GMLP1, V_GFIN)):
            sc.op("dve", lambda e, i=i, v=v: e.tensor_scalar(out=G32[:, i * 8:(i + 1) * 8], in0=VEC[:, v * 8:(v + 1) * 8],
                                                             scalar1=32.0, scalar2=None, op0=ALU.mult),
                  reads=["vec"], writes=[("g32", i, i + 1)])
        sc.op("dve", lambda e: e.tensor_tensor(out=PBS[:], in0=VEC[:, V_PB * 8:(V_PB + 1) * 8],
                                               in1=VEC[:, V_PS * 8:(V_PS + 1) * 8], op=ALU.mult),
              reads=["vec"], writes=["pbs"])

        chk('c0')
        wloads = []
        for hb in range(4):
            wloads.append(("A", [(lambda s, hb=hb: WA[s][:, :, 0:256],
                                  w_pw1[:, 256 * hb:256 * hb + 256].rearrange("(k p) m -> p k m", p=128)),
                                 (lambda s, hb=hb: WA[s][:, :, 256:512],
                                  w_pw1[:, D + 256 * hb:D + 256 * hb + 256].rearrange("(k p) m -> p k m", p=128))]))
        for hb in range(2):
            wloads.append(("A", [(lambda s: WA[s][:, :, :],
                                  w_pw2[:, 512 * hb:512 * hb + 512].rearrange("(k p) m -> p k m", p=128))]))

        def mlp_loads(layer):
            for fb in range(4):
                for hb in range(2):
                    c0 = fb * 1024 + hb * 512
                    wloads.append(("A", [(lambda s: WA[s][:, :, :],
                                          w_m1[layer, :, c0:c0 + 512].rearrange("(k p) m -> p k m", p=128))]))
                for hb in range(2):
                    r0 = fb * 1024 + hb * 512
                    wloads.append(("B", [(lambda s: WB[s][:, :, :],
                                          w_m2[layer, r0:r0 + 512, :].rearrange("(k p) m -> p k m", p=128))]))
        mlp_loads(0)
        wloads.append(("P", [(lambda s: WP[s][:, :, :, :], w_pool.rearrange("g (k p) n -> p g k n", p=128)),
                             (lambda s: Wreg[0:1, s * 4096 + 2048:s * 4096 + 3072], pbrow)]))
        mlp_loads(1)
        wnext = [0]

        def issue_wload():
            i = wnext[0]
            if i >= len(wloads):
                return
            wnext[0] += 1
            s = i % NSLOT
            pairs = [(vf(s), ap) for (vf, ap) in wloads[i][1]]
            sc.dma("pool", pairs, d_w[s], writes=[rW(s)])

        wuse = [0]

        def next_w():
            i = wuse[0]
            wuse[0] += 1
            assert sc.dead or i < wnext[0], "weight load not issued"
            return i % NSLOT

        for _ in range(NSLOT):
            if not _NOW:
                issue_wload()

        chk('cw')
        in_ctr = [0]

        def load_rows(src_ap, nrows, dst_fn, dst_ranges_fn):
            sap, srng, sd = STG[in_ctr[0] % 4]
            in_ctr[0] += 1
            sc.dma("sp", [(sap[0:nrows, :], src_ap)], sd, writes=[srng])
            for half in range(2):
                b = nb()

                def tr(e, half=half, b=b, sap=sap):
                    ins = None
                    for cc in range(4):
                        c = half * 4 + cc
                        ins = e.transpose(out=ps[:, b, cc * nrows:(cc + 1) * nrows],
                                          in_=sap[0:nrows, c * 128:(c + 1) * 128],
                                          identity=ident[0:nrows, 0:nrows])
                    return ins
                sc.op("pe", tr, reads=[srng, "ident"], writes=[rPS(b)])
                src = ps[:, b, 0:4 * nrows].rearrange("p (c n) -> p c n", c=4)
                copy_op(alt_eng(), dst_fn(half), src, [rPS(b)], dst_ranges_fn(half))

        def load_x_cols(col0, ncols, src_ap):
            load_rows(src_ap, ncols,
                      lambda half: X[:, half * 4:(half + 1) * 4, col0:col0 + ncols],
                      lambda half: [rX(half * 4 + cc, col0, ncols) for cc in range(4)])

        def rms_stats(c0, n, ri=0):
            b = nb()
            for c in range(8):
                i = nsq()
                sc.op("act", lambda e, c=c, i=i: e.activation(out=SQR[:, i, 0:n], in_=X[:, c, c0:c0 + n], func=AF.Square),
                      reads=[rX(c, c0, n)], writes=[rSQ(i)])
                sc.op("pe", lambda e, c=c, i=i, b=b: e.matmul(ps[:, b, 0:n], lhsT=ones[:], rhs=SQR[:, i, 0:n],
                                                             start=(c == 0), stop=(c == 7)),
                      reads=[rSQ(i), "ones"], writes=[rPS(b)])
            sc.op("act", lambda e, b=b: e.activation(out=ST[:, ri, 0:n], in_=ps[:, b, 0:n], func=AF.Ln,
                                                     scale=1.0, bias=D * EPS),
                  reads=[rPS(b)], writes=[rST(ri)])
            sc.op("act", lambda e: e.activation(out=ST[:, ri, 0:n], in_=ST[:, ri, 0:n], func=AF.Exp, scale=-0.5),
                  reads=[rST(ri)], writes=[rST(ri)])

        def rms_apply(gi, c, c0, n, out_ap, out_rng, eng="dve", ri=0):
            sc.op(eng, lambda e: e.scalar_tensor_tensor(out=out_ap, in0=X[:, c, c0:c0 + n], scalar=g32(gi, c),
                                                        in1=ST[:, ri, 0:n], op0=ALU.mult, op1=ALU.mult),
                  reads=[rX(c, c0, n), rST(ri), ("g32", gi, gi + 1)], writes=out_rng)

        def rmsnorm_to_H(gi, c0, n):
            rms_stats(c0, n)
            for c in range(8):
                rms_apply(gi, c, c0, n, H[:, c, c0:c0 + n], [rH(c, c0, n)])

        chk('c2')

        def load_tile(ti):
            c0, n = TILES[ti]
            if ti == 0:
                load_x_cols(0, 16, meta)
                load_x_cols(16, 112, xp[0:112, :])
                for j in range(3):
                    load_x_cols(128 + 128 * j, 128, xp[112 + 128 * j:240 + 128 * j, :])
            elif ti < 4:
                for j in range(4):
                    cc0 = c0 + 128 * j
                    load_x_cols(cc0, 128, xp[cc0 - 16:cc0 + 112, :])
            else:
                load_x_cols(2048, 16, xp[2032:2048, :])
                load_x_cols(2064, 128, xs)

        def pw1_step(hb, s, ti):
            c0, n = TILES[ti]
            for mi in range(2):
                m = hb * 2 + mi
                bg = nb()
                ba = nb()

                def mm_g(e, s=s, mi=mi, c0=c0, n=n, bg=bg):
                    ins = None
                    for k in range(8):
                        ins = e.matmul(ps[:, bg, 0:n], lhsT=WA[s][:, k, 256 + mi * 128:256 + (mi + 1) * 128],
                                       rhs=H[:, k, c0:c0 + n], start=(k == 0), stop=(k == 7))
                    return ins
                sc.op("pe", mm_g, reads=[rW(s)] + [rH(k, c0, n) for k in range(8)], writes=[rPS(bg)])

                def mm_a(e, s=s, mi=mi, c0=c0, n=n, ba=ba):
                    ins = None
                    for k in range(8):
                        ins = e.matmul(ps[:, ba, 0:n], lhsT=WA[s][:, k, mi * 128:(mi + 1) * 128],
                                       rhs=H[:, k, c0:c0 + n], start=(k == 0), stop=(k == 7))
                    return ins
                sc.op("pe", mm_a, reads=[rW(s)] + [rH(k, c0, n) for k in range(8)], writes=[rPS(ba)])
                sg = 2 + (m + ti) % 2
                sc.op("act", lambda e, bg=bg, n=n, m=m, sg=sg: e.activation(out=ST[:, sg, 0:n], in_=ps[:, bg, 0:n],
                                                                           func=AF.Sigmoid, bias=vec(V_BG, m), scale=1.0),
                      reads=[rPS(bg), "vec"], writes=[rST(sg)])
                if ti < 4:
                    sc.op("dve", lambda e, ba=ba, n=n, m=m, sg=sg, c0=c0: e.scalar_tensor_tensor(
                        out=VE[:, m, 30 + c0:30 + c0 + n], in0=ps[:, ba, 0:n], scalar=vec(V_BA, m), in1=ST[:, sg, 0:n],
                        op0=ALU.add, op1=ALU.mult), reads=[rPS(ba), rST(sg), "vec"], writes=[rVE(m, 30 + c0, n)])
                else:
                    sc.op("dve", lambda e, ba=ba, m=m, sg=sg: e.scalar_tensor_tensor(
                        out=VE[:, m, 30 + 2048:30 + 2064], in0=ps[:, ba, 0:16], scalar=vec(V_BA, m), in1=ST[:, sg, 0:16],
                        op0=ALU.add, op1=ALU.mult), reads=[rPS(ba), rST(sg), "vec"], writes=[rVE(m, 30 + 2048, 16)])
                    sc.op("dve", lambda e, ba=ba, m=m, sg=sg: e.scalar_tensor_tensor(
                        out=VE[:, m, VS0:VS0 + 608].rearrange("p (s j) -> p s j", j=38)[:, :, 30:38],
                        in0=ps[:, ba, 16:144].rearrange("p (s t) -> p s t", t=8), scalar=vec(V_BA, m),
                        in1=ST[:, sg, 16:144].rearrange("p (s t) -> p s t", t=8),
                        op0=ALU.add, op1=ALU.mult), reads=[rPS(ba), rST(sg), "vec"], writes=[rVE(m, VS0, 608)])

        def load_states():
            for q in range(4):
                def dstf(half, q=q):
                    return VE[:, half * 4:(half + 1) * 4, VS0 + q * 4 * 38:VS0 + (q + 1) * 4 * 38] \
                        .rearrange("p c (s j) -> p c s j", j=38)[:, :, :, 0:30]
                s_ = in_ctr[0] % 2
                in_ctr[0] += 1
                sc.dma("sp", [(IN[0:120, s_, :], sconv[q * 4:(q + 1) * 4].rearrange("s j d -> (s j) d"))], d_in[s_],
                       writes=[rIN(s_)])
                for half in range(2):
                    b = nb()

                    def tr(e, half=half, b=b, s_=s_):
                        ins = None
                        for cc in range(4):
                            c = half * 4 + cc
                            ins = e.transpose(out=ps[:, b, cc * 120:(cc + 1) * 120], in_=IN[0:120, s_, c * 128:(c + 1) * 128],
                                              identity=ident[0:120, 0:120])
                        return ins
                    sc.op("pe", tr, reads=[rIN(s_), "ident"], writes=[rPS(b)])
                    src = ps[:, b, 0:480].rearrange("p (c s j) -> p c s j", c=4, s=4)
                    copy_op(alt_eng(), dstf(half), src, [rPS(b)],
                            [rVE(half * 4 + cc, VS0 + q * 152, 152) for cc in range(4)])

            chk('c1')
            sc.dma("sp", [(ncs[:, 0:22, :], sconv[:, 8:30, :])], d_pt, writes=["ncs_old"], is_output=True)
            sc.dma("sp", [(nps[:, 0:7, :], spool[:, 8:15, :])], d_pt, writes=["nps_old"], is_output=True)


        dg_slot = {}
        DG2 = Wreg[:, 0:31 * 128].rearrange("p (k n) -> p k n", k=31)

        DG3 = INt[:, :].bitcast(BF16)[:, 0:31 * 128].rearrange("p (k n) -> p k n", k=31)
        DG4 = ZBt[:, 0:31 * 128].rearrange("p (k n) -> p k n", k=31)

        def DGv(db, k):
            if db < 2:
                return DG[:, db, k, :]
            return (DG2, DG3, DG4)[db - 2][:, k, :]

        def rDGx(db, k0=0, k1=31):
            if db < 2:
                return rDG(db, k0, k1)
            return (("W", "IN", "ZB")[db - 2], k0 * 256, k1 * 256)

        dg_ctr = [0]

        def dg_build(ti, m, db=None):
            if db is None:
                db = dg_ctr[0] % 3
                dg_ctr[0] += 1
            dg_slot[(ti, m)] = db
            for k in range(31):
                if k % 3 == 0:
                    sc.op("dve", lambda e, k=k, m=m, db=db: e.tensor_scalar(out=DGv(db, k), in0=identb[:],
                                                                          scalar1=vec(V_WDW + k, m), scalar2=None, op0=ALU.mult),
                          reads=["identb", "vec"], writes=[rDGx(db, k, k + 1)])
                elif k % 3 == 1:
                    sc.op("pool", lambda e, k=k, m=m, db=db: e.tensor_tensor(
                        out=DGv(db, k), in0=identb[:], in1=vec(V_WDW + k, m).to_broadcast([128, 128]), op=ALU.mult),
                        reads=["identb", "vec"], writes=[rDGx(db, k, k + 1)])
                else:
                    sc.op("act", lambda e, k=k, m=m, db=db: e.activation(out=DGv(db, k), in_=identb[:], func=AF.Identity,
                                                                         scale=vec(V_WDW + k, m)),
                          reads=["identb", "vec"], writes=[rDGx(db, k, k + 1)])

        pw1_slots = [next_w(), next_w(), next_w()]
        load_tile(0)
        for step in range(8):
            if step + 1 < 5:
                load_tile(step + 1)
            if step < 5:
                rmsnorm_to_H(0, *TILES[step])
            if step == 2:
                load_states()
            for hb in range(3):
                ti = step - 1 - hb
                if 0 <= ti < 5:
                    pw1_step(hb, pw1_slots[hb], ti)
                    if ti == 4:
                        issue_wload()
        chk('load')
        dg_build(0, 0, db=3)
        dg_build(0, 1, db=4)
        s3 = next_w()
        for ti in range(5):
            pw1_step(3, s3, ti)

        chk('pw1')
        for c in range(8):
            sc.op("dve", lambda e, c=c: e.tensor_copy(out=VO[:, c, 0:30], in_=VE[:, c, 30 + 2034:30 + 2064]),
                  reads=[rVE(c, 30 + 2034, 30)], writes=[rSTall])
            sc.op("dve", lambda e, c=c: e.tensor_copy(
                out=VO[:, c, 30:158].rearrange("p (s t) -> p s t", t=8),
                in_=VE[:, c, VS0:VS0 + 608].rearrange("p (s j) -> p s j", j=38)[:, :, 30:38]),
                reads=[rVE(c, VS0, 608)], writes=[rSTall])

        def out_rows(src_fn, nrows, stg_slot_ap, stg_rng, dma_pairs_fn, dsem, name, src_reads):
            for half in range(2):
                b = nb()

                def tr(e, half=half, b=b):
                    ins = None
                    for cc in range(4):
                        c = half * 4 + cc
                        ins = e.transpose(out=ps[0:nrows, b, cc * 128:(cc + 1) * 128], in_=src_fn(c), identity=ident[:])
                    return ins
                sc.op("pe", tr, reads=list(src_reads) + ["ident"], writes=[rPS(b)])
                copy_op(alt_eng(), stg_slot_ap[0:nrows, half * 512:(half + 1) * 512], ps[0:nrows, b, :], [rPS(b)], [stg_rng])
            sc.dma("sp", dma_pairs_fn(), dsem, reads=[stg_rng], writes=[name], is_output=True)

        def nc_out():
            out_rows(lambda c: VO[:, c, 0:30], 30, IN[:, 0, :], rIN(0), lambda: [(ncp[:, :], IN[0:30, 0, :])], d_in[0], "ncp",
                     [rSTall])
            out_rows(lambda c: VO[:, c, 30:158], 128, IN[:, 1, :], rIN(1),
                     lambda: [(ncs[s_, 22:30, :], IN[s_ * 8:(s_ + 1) * 8, 1, :]) for s_ in range(16)], d_in[1], "ncs_new",
                     [rSTall])

        chk('ncout')
        s_pw2 = [next_w(), next_w()]
        nb_mod[0] = 6
        bs1 = 6
        bs2 = 7
        pend = []
        STATS_LAG = 3

        C4 = INt[:, 0:8 * 144].rearrange("p (c n) -> p c n", c=8)
        Z4 = INt[:, 1152:1152 + 576].bitcast(BF16).rearrange("p (c n) -> p c n", c=8)
        SQ4 = PTt[:, :].bitcast(BF16).rearrange("p (c n) -> p c n", c=8)
        rC4 = lambda m: ("IN", m * 576, (m + 1) * 576)
        rZ4 = lambda m: ("IN", 4608 + m * 288, 4608 + (m + 1) * 288)
        rSQ4 = lambda i: ("PT", i * 512, (i + 1) * 512)
        sq4_ctr = [0]

        def nsq4():
            i = sq4_ctr[0] % 8
            sq4_ctr[0] += 1
            return i

        LANE = {
            0: dict(C=lambda m, n: Cv[:, m, 0:n], rC=lambda m, n: rC(m, 0, n), Z=lambda m, n: ZB[:, m, 0:n],
                    rZ=lambda m, n: rZ(m, 0, n), sq=lambda i, n: SQR[:, i, 0:n], rsq=rSQ, nsq=nsq, bs=(6, 7),
                    mean=lambda n: ST[:, 0, 0:n], rstd=lambda n: ST[:, 1, 0:n], msq=lambda n: ST[:, 2, 0:n],
                    rmean=rST(0), rrstd=rST(1), rmsq=rST(2)),
            1: dict(C=lambda m, n: C4[:, m, 0:n], rC=lambda m, n: rC4(m), Z=lambda m, n: Z4[:, m, 0:n],
                    rZ=lambda m, n: rZ4(m), sq=lambda i, n: SQ4[:, i, 0:n], rsq=rSQ4, nsq=nsq4, bs=(4, 5),
                    mean=lambda n: ST[:, 3, 0:n], rstd=lambda n: ST[:, 3, 160:160 + n], msq=lambda n: ST[:, 3, 320:320 + n],
                    rmean=("ST", 6144, 6144 + 640), rrstd=("ST", 6144 + 640, 6144 + 1280), rmsq=("ST", 6144 + 1280, 6144 + 1920)),
        }

        def conv_chunk(ti, m, lane=0, db=None):
            c0, n = TILES[ti]
            L = LANE[lane]
            if db is None:
                db = dg_slot[(ti, m)]
            b = nb()
            if ti < 4:
                def conv(e, m=m, db=db, b=b, c0=c0, n=n):
                    ins = None
                    for k in range(31):
                        ins = e.matmul(ps[:, b, 0:n], lhsT=DGv(db, k), rhs=VE[:, m, c0 + k:c0 + k + n],
                                       start=(k == 0), stop=(k == 30))
                    return ins
                sc.op("pe", conv, reads=[rDGx(db), rVE(m, c0, n + 30)], writes=[rPS(b)])
            else:
                def conv(e, m=m, db=db, b=b):
                    ins = None
                    for k in range(31):
                        ins = e.matmul(ps[:, b, 0:16], lhsT=DGv(db, k), rhs=VE[:, m, 2048 + k:2048 + k + 16],
                                       start=(k == 0), stop=(k == 30))
                    for k in range(31):
                        ins = e.matmul(ps[:, b, 16:144].rearrange("p (s t) -> p s t", t=8), lhsT=DGv(db, k),
                                       rhs=VE[:, m, VS0:VS0 + 608].rearrange("p (s j) -> p s j", j=38)[:, :, k:k + 8],
                                       start=(k == 0), stop=(k == 30))
                    return ins
                sc.op("pe", conv, reads=[rDGx(db), rVE(m, 2048, 46), rVE(m, VS0, 608)], writes=[rPS(b)])
            while len(pend) >= STATS_LAG * (2 if merged[0] else 1):
                pend.pop(0)()
            i1 = L["nsq"]()
            i2 = L["nsq"]()
            bs1_, bs2_ = L["bs"]
            sc.op("act", lambda e: e.activation(out=L["C"](m, n), in_=ps[:, b, 0:n], func=AF.Identity,
                                                bias=vec(V_BDW, m), scale=1.0),
                  reads=[rPS(b), "vec"], writes=[L["rC"](m, n)])
            sc.op("act", lambda e: e.activation(out=L["sq"](i1, n), in_=ps[:, b, 0:n],
                                                func=AF.Identity, bias=vec(V_BDW, m), scale=1.0),
                  reads=[rPS(b), "vec"], writes=[L["rsq"](i1)])
            sc.op("act", lambda e: e.activation(out=L["sq"](i2, n), in_=ps[:, b, 0:n],
                                                func=AF.Square, bias=vec(V_BDW, m), scale=1.0),
                  reads=[rPS(b), "vec"], writes=[L["rsq"](i2)])

            def stats_mm():
                sc.op("pe", lambda e: e.matmul(ps[:, bs1_, 0:n], lhsT=ones[:], rhs=L["sq"](i1, n),
                                               start=(m == 0), stop=(m == 7)),
                      reads=[L["rsq"](i1), "ones"], writes=[rPS(bs1_)])
                sc.op("pe", lambda e: e.matmul(ps[:, bs2_, 0:n], lhsT=ones[:], rhs=L["sq"](i2, n),
                                               start=(m == 0), stop=(m == 7)),
                      reads=[L["rsq"](i2), "ones"], writes=[rPS(bs2_)])
            pend.append(stats_mm)

        def ln_tile(ti, lane=0):
            c0, n = TILES[ti]
            L = LANE[lane]
            bs1_, bs2_ = L["bs"]
            while pend:
                pend.pop(0)()
            sc.op("dve", lambda e: e.tensor_scalar(out=L["mean"](n), in0=ps[:, bs1_, 0:n], scalar1=1.0 / D,
                                                   scalar2=None, op0=ALU.mult),
                  reads=[rPS(bs1_)], writes=[L["rmean"]])
            sc.op("dve", lambda e: e.tensor_tensor(out=L["msq"](n), in0=L["mean"](n), in1=L["mean"](n), op=ALU.mult),
                  reads=[L["rmean"]], writes=[L["rmsq"]])
            sc.op("dve", lambda e: e.scalar_tensor_tensor(out=L["rstd"](n), in0=ps[:, bs2_, 0:n], scalar=1.0 / D,
                                                          in1=L["msq"](n), op0=ALU.mult, op1=ALU.subtract),
                  reads=[rPS(bs2_), L["rmsq"]], writes=[L["rrstd"]])
            sc.op("act", lambda e: e.activation(out=L["rstd"](n), in_=L["rstd"](n), func=AF.Ln, scale=1.0, bias=EPS),
                  reads=[L["rrstd"]], writes=[L["rrstd"]])
            sc.op("act", lambda e: e.activation(out=L["rstd"](n), in_=L["rstd"](n), func=AF.Exp, scale=-0.5),
                  reads=[L["rrstd"]], writes=[L["rrstd"]])
            for m in range(8):
                sc.op("dve", lambda e, m=m: e.tensor_tensor(out=L["C"](m, n), in0=L["C"](m, n), in1=L["mean"](n),
                                                            op=ALU.subtract),
                      reads=[L["rC"](m, n), L["rmean"]], writes=[L["rC"](m, n)])
                sc.op("dve", lambda e, m=m: e.tensor_tensor(out=L["C"](m, n), in0=L["C"](m, n), in1=L["rstd"](n),
                                                            op=ALU.mult),
                      reads=[L["rC"](m, n), L["rrstd"]], writes=[L["rC"](m, n)])
                sc.op("act", lambda e, m=m: e.activation(out=L["Z"](m, n), in_=L["C"](m, n), func=AF.Silu,
                                                         bias=vec(V_LNB, m), scale=vec(V_LNG, m)),
                      reads=[L["rC"](m, n), "vec"], writes=[L["rZ"](m, n)])

        def pw2_tile(ti, lane=0):
            c0, n = TILES[ti]
            L = LANE[lane]
            for m in range(8):
                s = s_pw2[m // 4]
                b = nb()

                def mm2(e, m=m, s=s, b=b, n=n):
                    ins = None
                    for k in range(8):
                        ins = e.matmul(ps[:, b, 0:n], lhsT=WA[s][:, k, (m % 4) * 128:(m % 4 + 1) * 128], rhs=L["Z"](k, n),
                                       start=(k == 0), stop=(k == 7))
                    return ins
                sc.op("pe", mm2, reads=[rW(s)] + [L["rZ"](k, n) for k in range(8)], writes=[rPS(b)])
                sc.op("dve", lambda e, m=m, b=b, n=n, c0=c0: e.scalar_tensor_tensor(
                    out=X[:, m, c0:c0 + n], in0=ps[:, b, 0:n], scalar=vec(V_BPW2, m), in1=X[:, m, c0:c0 + n],
                    op0=ALU.add, op1=ALU.add), reads=[rPS(b), rX(m, c0, n), "vec"], writes=[rX(m, c0, n)])

        def next_chunk(ti, m, la=3):
            m2 = m + la
            if m2 < 8:
                return (ti, m2)
            if ti + 1 < 4:
                return (ti + 1, m2 - 8)
            return None

        merged = [False]
        dg_build(0, 2)
        dg_build(0, 3)
        dg_build(0, 4)
        for ti in range(4):
            if ti == 3:
                merged[0] = True
                nb_mod[0] = 4
            for m in range(8):
                conv_chunk(ti, m)
                if ti == 3:
                    conv_chunk(4, m, lane=1, db=dg_slot[(3, m)])
                if ti == 0 and m == 1:
                    nc_out()
                nx = next_chunk(ti, m)
                if nx is not None and m < 7 and nx not in dg_slot:
                    dg_build(*nx)
                if m == 2 and ti > 0:
                    pw2_tile(ti - 1)
            nx = next_chunk(ti, 7)
            if nx is not None and nx not in dg_slot:
                dg_build(*nx)
            if ti < 3:
                ln_tile(ti)
        while pend:
            pend.pop(0)()
        ln_tile(3)
        ln_tile(4, lane=1)
        pw2_tile(3)
        rmsnorm_to_H(1, *TILES[0])
        pw2_tile(4, lane=1)
        nb_mod[0] = 8
        issue_wload()
        issue_wload()
        issue_wload()

        chk('conv')
        def mlp(gi, after_tile=None, skip_rms=False, last_tiles=None, t0_done=False):
            if not skip_rms:
                if not t0_done:
                    rmsnorm_to_H(gi, *TILES[0])
                rmsnorm_to_H(gi, *TILES[1])
            for fb in range(4):
                for hb in range(2):
                    s = next_w()
                    for j in range(4):
                        hc = hb * 4 + j
                        for ti in range(5):
                            c0, n = TILES[ti]
                            b = nb()

                            def mm1(e, s=s, j=j, c0=c0, n=n, b=b):
                                ins = None
                                for k in range(8):
                                    ins = e.matmul(ps[:, b, 0:n], lhsT=WA[s][:, k, j * 128:(j + 1) * 128],
                                                   rhs=H[:, k, c0:c0 + n], start=(k == 0), stop=(k == 7))
                                return ins
                            sc.op("pe", mm1, reads=[rW(s)] + [rH(k, c0, n) for k in range(8)], writes=[rPS(b)])
                            i = nsq()
                            sc.op("act", lambda e, b=b, n=n, i=i: e.activation(out=SQR[:, i, 0:n], in_=ps[:, b, 0:n], func=AF.Relu),
                                  reads=[rPS(b)], writes=[rSQ(i)])
                            sc.op(alt_eng("dve", "pool"), lambda e, hc=hc, c0=c0, n=n, i=i: e.tensor_tensor(
                                out=HID[:, hc, c0:c0 + n], in0=SQR[:, i, 0:n], in1=SQR[:, i, 0:n], op=ALU.mult),
                                reads=[rSQ(i)], writes=[rHID(hc, c0, n)])
                            if (not skip_rms) and fb == 0 and hb == 0 and j == 0 and ti + 2 < 5:
                                rmsnorm_to_H(gi, *TILES[ti + 2])
                    issue_wload()
                sA = next_w()
                sB = next_w()
                last = (fb == 3)
                tl = TILES
                if last and last_tiles is not None:
                    tl = last_tiles
                order = [(ti, m) for ti in range(len(tl)) for m in range(8)] if last else \
                    [(ti, m) for m in range(8) for ti in range(5)]
                for (ti, m) in order:
                    c0, n = tl[ti]
                    b = nb()

                    def mm2(e, sA=sA, sB=sB, m=m, c0=c0, n=n, b=b):
                        ins = None
                        for k in range(8):
                            sl = sA if k < 4 else sB
                            ins = e.matmul(ps[:, b, 0:n], lhsT=WB[sl][:, k % 4, m * 128:(m + 1) * 128],
                                           rhs=HID[:, k, c0:c0 + n], start=(k == 0), stop=(k == 7))
                        return ins
                    sc.op("pe", mm2, reads=[rW(sA), rW(sB)] + [rHID(k, c0, n) for k in range(8)], writes=[rPS(b)])
                    sc.op("dve", lambda e, m=m, b=b, c0=c0, n=n: e.tensor_tensor(out=X[:, m, c0:c0 + n], in0=ps[:, b, 0:n],
                                                                                 in1=X[:, m, c0:c0 + n], op=ALU.add),
                          reads=[rPS(b), rX(m, c0, n)], writes=[rX(m, c0, n)])
                    if last and after_tile is not None and m == 7:
                        after_tile(ti)
                issue_wload()
                issue_wload()

        mlp(1, t0_done=True)
        chk('mlp0')

        s_pool = next_w()
        PBR = Wreg[0:1, s_pool * 4096 + 2048:s_pool * 4096 + 3072]
        onesrow = Wreg[0:1, s_pool * 4096 + 3072:s_pool * 4096 + 3584]
        sc.op("pool", lambda e: e.memset(onesrow, 1.0), reads=[rW(s_pool)], writes=[("W", s_pool * 8192 + 6144, s_pool * 8192 + 7168)])
        HB = Sreg[:, 0:2 * 8 * 527].rearrange("p (b c n) -> p b c n", b=2, c=8)
        AB = Sreg[:, 8432:8432 + 2 * 4 * 527].rearrange("p (b c n) -> p b c n", b=2, c=4)
        A1T = Sreg[:, 12648:12648 + 2 * 527].rearrange("p (c n) -> p c n", c=2)
        WPS = Sreg[:, 13704:13704 + 2048].rearrange("p (g k n) -> p g k n", g=4, k=2)
        WPN = Sreg[:, 15752:15752 + 2048].rearrange("p (g k n) -> p g k n", g=4, k=2)
        P15 = Sreg[:, 17800:17800 + 8 * 16].rearrange("p (c n) -> p c n", c=8)
        rHB = lambda b, c, a=0, n=527: ("S", ((b * 8 + c) * 527 + a) * 2, ((b * 8 + c) * 527 + a + n) * 2)
        rAB = lambda b, c: ("S", (8432 + (b * 4 + c) * 527) * 2, (8432 + (b * 4 + c + 1) * 527) * 2)
        rA1T = lambda c: ("S", (12648 + c * 527) * 2, (12648 + (c + 1) * 527) * 2)
        rWPS = ("S", 13704 * 2, (13704 + 2048) * 2)
        rWPN = ("S", 15752 * 2, (15752 + 2048) * 2)
        rP15 = ("S", 17800 * 2, (17800 + 128) * 2)
        for g, w in enumerate((2, 4, 8, 16)):
            sc.op("dve", lambda e, g=g, w=w: e.tensor_scalar(out=WPS[:, g, :, :], in0=WP[s_pool][:, g, :, :], scalar1=1.0 / w,
                                                             scalar2=None, op0=ALU.mult),
                  reads=[rW(s_pool)], writes=[rWPS])
        sc.op("dve", lambda e: e.tensor_scalar(out=WPN[:, :, :, :], in0=WP[s_pool][:, :, :, :], scalar1=-1.0, scalar2=None,
                                                op0=ALU.mult), reads=[rW(s_pool)], writes=[rWPN])
        RI = 3
        def stageA(ti):
            c0, n = TILES[ti]
            hb_ = ti % 2
            rms_stats(c0, n, ri=RI)
            if ti < 4:
                next_ = 15 + n
                if ti == 0:
                    sc.op("pool", lambda e, hb_=hb_: e.memset(HB[:, hb_, :, 0:15], 0.0), writes=[rHB(hb_, c, 0, 15) for c in range(8)])
                else:
                    sc.op("pool", lambda e, hb_=hb_: e.tensor_copy(out=HB[:, hb_, :, 0:15], in_=HB[:, 1 - hb_, :, 512:527]),
                          reads=[rHB(1 - hb_, c, 512, 15) for c in range(8)], writes=[rHB(hb_, c, 0, 15) for c in range(8)])
                for c in range(8):
                    rms_apply(2, c, c0, n, HB[:, hb_, c, 15:15 + n], [rHB(hb_, c, 15, n)], ri=RI)
            else:
                next_ = 31 + 368
                sc.op("pool", lambda e, hb_=hb_: e.tensor_copy(out=HB[:, hb_, :, 0:15], in_=HB[:, 1 - hb_, :, 512:527]),
                      reads=[rHB(1 - hb_, c, 512, 15) for c in range(8)], writes=[rHB(hb_, c, 0, 15) for c in range(8)])
                for q in range(2):
                    s_ = in_ctr[0] % 2
                    in_ctr[0] += 1
                    sc.dma("sp", [(IN[0:120, s_, :], spool[q * 8:(q + 1) * 8].rearrange("s j d -> (s j) d"))], d_in[s_],
                           writes=[rIN(s_)])
                    for half in range(2):
                        b = nb()

                        def tr(e, half=half, b=b, s_=s_):
                            ins = None
                            for cc in range(4):
                                c = half * 4 + cc
                                ins = e.transpose(out=ps[:, b, cc * 120:(cc + 1) * 120],
                                                  in_=IN[0:120, s_, c * 128:(c + 1) * 128], identity=ident[0:120, 0:120])
                            return ins
                        sc.op("pe", tr, reads=[rIN(s_), "ident"], writes=[rPS(b)])
                        src = ps[:, b, 0:480].rearrange("p (c s j) -> p c s j", c=4, s=8)
                        dst = HB[:, hb_, half * 4:(half + 1) * 4, 31 + q * 184:31 + (q + 1) * 184] \
                            .rearrange("p c (s j) -> p c s j", j=23)[:, :, :, 0:15]
                        copy_op(alt_eng(), dst, src, [rPS(b)], [rHB(hb_, half * 4 + cc, 31 + q * 184, 184) for cc in range(4)])
                for c in range(8):
                    rms_apply(2, c, 2048, 16, HB[:, hb_, c, 15:31], [rHB(hb_, c, 15, 16)], ri=RI)
                    sc.op("dve", lambda e, c=c, hb_=hb_: e.scalar_tensor_tensor(
                        out=HB[:, hb_, c, 31:399].rearrange("p (s j) -> p s j", j=23)[:, :, 15:23],
                        in0=X[:, c, 2064:2192].rearrange("p (s t) -> p s t", t=8), scalar=g32(2, c),
                        in1=ST[:, RI, 16:144].rearrange("p (s t) -> p s t", t=8), op0=ALU.mult, op1=ALU.mult),
                        reads=[rX(c, 2064, 128), rST(RI), ("g32", 2, 3)], writes=[rHB(hb_, c, 31, 368)])
                    sc.op("dve", lambda e, c=c: e.scalar_tensor_tensor(
                        out=VO[:, c, 15:30], in0=X[:, c, 2049:2064], scalar=g32(2, c), in1=ST[:, RI, 1:16],
                        op0=ALU.mult, op1=ALU.mult), reads=[rX(c, 2049, 15), rST(RI), ("g32", 2, 3)], writes=[("ST", 0, 6144)])
                    sc.op("dve", lambda e, c=c: e.scalar_tensor_tensor(
                        out=VO[:, c, 30:158], in0=X[:, c, 2064:2192], scalar=g32(2, c), in1=ST[:, RI, 16:144],
                        op0=ALU.mult, op1=ALU.mult), reads=[rX(c, 2064, 128), rST(RI), ("g32", 2, 3)], writes=[("ST", 0, 6144)])
            for c in (4, 5):
                sc.op("pool" if c % 2 else "dve", lambda e, c=c, hb_=hb_, next_=next_: e.tensor_tensor(
                    out=AB[:, hb_, c - 4, 1:next_], in0=HB[:, hb_, c, 1:next_], in1=HB[:, hb_, c, 0:next_ - 1], op=ALU.add),
                    reads=[rHB(hb_, c, 0, next_)], writes=[rAB(hb_, c - 4)])
            for c in (6, 7):
                eng = "pool" if c % 2 else "dve"
                sc.op(eng, lambda e, c=c, hb_=hb_, next_=next_: e.tensor_tensor(
                    out=A1T[:, c - 6, 1:next_], in0=HB[:, hb_, c, 1:next_], in1=HB[:, hb_, c, 0:next_ - 1], op=ALU.add),
                    reads=[rHB(hb_, c, 0, next_)], writes=[rA1T(c - 6)])
                sc.op(eng, lambda e, c=c, hb_=hb_, next_=next_: e.tensor_tensor(
                    out=AB[:, hb_, c - 4, 3:next_], in0=A1T[:, c - 6, 3:next_], in1=A1T[:, c - 6, 1:next_ - 2], op=ALU.add),
                    reads=[rA1T(c - 6)], writes=[rAB(hb_, c - 4)])
            if ti == 0:
                PA = PT[:, 0, 0:240].rearrange("p (c n) -> p c n", c=8)
                PB = PT[:, 1, 0:240].rearrange("p (c n) -> p c n", c=8)
                IV8 = PT[:, 1, 256:384].rearrange("p (c n) -> p c n", c=8)
                rPA = ("PT", 0, 960)
                rPB = ("PT", 2048, 2048 + 960)
                rIV8 = ("PT", 2048 + 1024, 2048 + 1536)
                for c in range(8):
                    sc.op("pool", lambda e, c=c: e.tensor_copy(out=IV8[:, c, :], in_=INVC[:, (c // 2) * 16:(c // 2 + 1) * 16]),
                          reads=[("invc", 0, 64)], writes=[rIV8])
                hb_all = [rHB(hb_, c, 0, 30) for c in range(8)]
                sc.op("dve", lambda e, hb_=hb_: e.tensor_tensor(out=PA[:, :, 1:30], in0=HB[:, hb_, :, 1:30],
                                                               in1=HB[:, hb_, :, 0:29], op=ALU.add),
                      reads=hb_all, writes=[rPA])
                sc.op("dve", lambda e: e.tensor_tensor(out=PB[:, 2:8, 3:30], in0=PA[:, 2:8, 3:30], in1=PA[:, 2:8, 1:28], op=ALU.add),
                      reads=[rPA], writes=[rPB])
                sc.op("dve", lambda e: e.tensor_tensor(out=PA[:, 4:8, 7:30], in0=PB[:, 4:8, 7:30], in1=PB[:, 4:8, 3:26], op=ALU.add),
                      reads=[rPB], writes=[rPA])
                sc.op("dve", lambda e: e.tensor_tensor(out=PB[:, 6:8, 15:30], in0=PA[:, 6:8, 15:30], in1=PA[:, 6:8, 7:22], op=ALU.add),
                      reads=[rPA], writes=[rPB])
                for (buf, rbuf, cA) in ((PA, rPA, 0), (PB, rPB, 2), (PA, rPA, 4), (PB, rPB, 6)):
                    sc.op("dve", lambda e, buf=buf, cA=cA: e.tensor_tensor(out=buf[:, cA:cA + 2, 15:30], in0=buf[:, cA:cA + 2, 15:30],
                                                                           in1=IV8[:, cA:cA + 2, 0:15], op=ALU.mult),
                          reads=[rbuf, rIV8], writes=[rbuf])
                    sc.op("dve", lambda e, buf=buf, cA=cA, hb_=hb_: e.tensor_tensor(out=P15[:, cA:cA + 2, 0:15], in0=buf[:, cA:cA + 2, 15:30],
                                                                                    in1=HB[:, hb_, cA:cA + 2, 15:30], op=ALU.subtract),
                          reads=[rbuf, rHB(hb_, cA, 15, 15), rHB(hb_, cA + 1, 15, 15)], writes=[rP15])

        def stageB(ti):
            c0, n = TILES[ti]
            hb_ = ti % 2
            for m in range(8):
                g = m // 2
                mi = m % 2
                b = nb()
                stride = (1, 1, 2, 4)[g]

                def mmp(e, g=g, mi=mi, b=b, n=n, ti=ti, hb_=hb_, stride=stride, m=m):
                    ins = None
                    if ti < 4:
                        segs = [("flat", 15 if ti == 0 else 0, n, 15 + (15 if ti == 0 else 0))]
                    else:
                        segs = [("flat", 0, 16, 15), ("samp", 16, 128, 0)]
                    for (kind, pc0, cnt_, e0) in segs:
                        pc1 = n if kind == "flat" and ti < 4 else pc0 + cnt_
                        terms = []
                        for ki in range(2):
                            cs = 2 * g + ki
                            for j in range(4 if g > 0 else 2):
                                sh = j * stride
                                srcbuf = (HB[:, hb_, cs, :] if g < 2 else AB[:, hb_, cs - 4, :])
                                terms.append((WPS[:, g, ki, mi * 128:(mi + 1) * 128], srcbuf, sh))
                            terms.append((WPN[:, g, ki, mi * 128:(mi + 1) * 128], HB[:, hb_, cs, :], 0))
                        for idx, (lw, srcbuf, sh) in enumerate(terms):
                            if kind == "flat":
                                ncols = pc1 - pc0
                                rhs = srcbuf[:, e0 - sh:e0 - sh + ncols]
                                outp = ps[:, b, pc0:pc1]
                            else:
                                rhs = srcbuf[:, 31:399].rearrange("p (s j) -> p s j", j=23)[:, :, 15 - sh:23 - sh]
                                outp = ps[:, b, 16:144].rearrange("p (s t) -> p s t", t=8)
                            ins = e.matmul(outp, lhsT=lw, rhs=rhs, start=(idx == 0), stop=False)
                        if kind == "flat":
                            ins = e.matmul(ps[:, b, pc0:pc1], lhsT=PBR[0:1, m * 128:(m + 1) * 128], rhs=onesrow[0:1, 0:pc1 - pc0],
                                           start=False, stop=True)
                        else:
                            ins = e.matmul(ps[:, b, 16:144], lhsT=PBR[0:1, m * 128:(m + 1) * 128], rhs=onesrow[0:1, 0:128],
                                           start=False, stop=True)
                    if ti == 0:
                        for ki in range(2):
                            ins = e.matmul(ps[:, b, 0:15], lhsT=WP[s_pool][:, g, ki, mi * 128:(mi + 1) * 128],
                                           rhs=P15[:, 2 * g + ki, 0:15], start=(ki == 0), stop=False)
                        ins = e.matmul(ps[:, b, 0:15], lhsT=PBR[0:1, m * 128:(m + 1) * 128], rhs=onesrow[0:1, 0:15],
                                       start=False, stop=True)
                    return ins
                rd = [rW(s_pool), rWPS, rWPN, rP15] + [rHB(hb_, 2 * g + ki, 0, 527) for ki in range(2)]
                if g >= 2:
                    rd += [rAB(hb_, 2 * g + ki - 4) for ki in range(2)]
                sc.op("pe", mmp, reads=rd, writes=[rPS(b)])
                sc.op("dve", lambda e, m=m, b=b, n=n, c0=c0: e.scalar_tensor_tensor(
                    out=X[:, m, c0:c0 + n], in0=ps[:, b, 0:n], scalar=vec(V_PS, m), in1=X[:, m, c0:c0 + n],
                    op0=ALU.mult, op1=ALU.add), reads=[rPS(b), rX(m, c0, n), "vec"], writes=[rX(m, c0, n)])

        def np_out():
            out_rows(lambda c: VO[:, c, 15:30], 15, IN[:, 0, :], rIN(0), lambda: [(npp[:, :], IN[0:15, 0, :])],
                     d_in[0], "npp", [("ST", 0, 6144)])
            out_rows(lambda c: VO[:, c, 30:158], 128, IN[:, 1, :], rIN(1),
                     lambda: [(nps[s_, 7:15, :], IN[s_ * 8:(s_ + 1) * 8, 1, :]) for s_ in range(16)], d_in[1], "nps_new",
                     [("ST", 0, 6144)])

        stageA(0)
        for ti in range(5):
            if ti + 1 < 5:
                stageA(ti + 1)
            stageB(ti)
            if ti == 3:
                np_out()
            if ti >= 1:
                rmsnorm_to_H(3, *TILES[ti - 1])
        rmsnorm_to_H(3, *TILES[4])
        issue_wload()

        ost_ctr = [0]

        FTL = [(0, 512), (512, 512), (1024, 512), (1536, 256), (1792, 256), (2048, 144)]
        NF = len(FTL)

        def final_S(fi):
            c0, n = FTL[fi]
            yb = fi % 2
            rms_stats(c0, n, ri=1)
            for c in range(8):
                rms_apply(4, c, c0, n, YT[:, yb, c, 0:n], [rYT(yb, c, 0, n)], ri=1)

        def final_Tr(fi):
            c0, n = FTL[fi]
            yb = fi % 2
            blocks = []
            if fi < NF - 1:
                for blk in range(n // 128):
                    cc = c0 + blk * 128
                    if cc == 0:
                        blocks.append((0, 128, 16, yp[0:112, :]))
                    else:
                        blocks.append((blk * 128, 128, 0, yp[cc - 16:cc + 112, :]))
            else:
                blocks.append((0, 16, 0, yp[2032:2048, :]))
                blocks.append((16, 128, 0, ys[:, :]))
            for (o, nc_, skip, dst) in blocks:
                sap, srng, sd = STG[ost_ctr[0] % 4]
                ost_ctr[0] += 1
                out_rows(lambda c, yb=yb, o=o, nc_=nc_: YT[:, yb, c, o:o + nc_], nc_, sap, srng,
                         lambda dst=dst, sap=sap, skip=skip, nc_=nc_: [(dst, sap[skip:nc_, :])], sd, ("y", fi, o),
                         [rYT(yb, c, 0, 512) for c in range(8)])

        def after_last(fi):
            if fi >= 2:
                final_Tr(fi - 2)
            if fi >= 1:
                final_S(fi - 1)
            if fi == NF - 1:
                final_Tr(NF - 2)
                final_S(NF - 1)
                final_Tr(NF - 1)

        chk('pool')
        mlp(3, skip_rms=True, after_tile=after_last, last_tiles=FTL)
        chk('mlp1')

        sc.finish()
        sc.emit()
    return nc


_NC_CACHE = {}
_STOP = None
_NOW = False


def _to_pc(v):
    return np.ascontiguousarray(np.asarray(v, np.float32).reshape(8, 128).T)


def kernel(x_prompt, x_sample, state_conv, state_pool, meta_tokens, norm_mix_g, norm_mlp_g,
           conv_w_pw1, conv_b_pw1, conv_w_dw, conv_b_dw, conv_ln_g, conv_ln_b,
           conv_w_pw2, conv_b_pw2, pool_w, pool_b, pool_scale, mlp_w1, mlp_w2, final_g):
    f = lambda a: np.ascontiguousarray(np.asarray(a, dtype=np.float32))
    x_prompt, x_sample, state_conv, state_pool = f(x_prompt), f(x_sample), f(state_conv), f(state_pool)
    vec_list = [norm_mix_g[0], norm_mix_g[1], norm_mlp_g[0], norm_mlp_g[1], final_g,
                np.asarray(conv_b_pw1)[0, :D], np.asarray(conv_b_pw1)[0, D:], conv_b_dw[0], conv_ln_g[0], conv_ln_b[0],
                conv_b_pw2[0], np.asarray(pool_b)[0].reshape(-1), pool_scale[0]]
    vec_list += [np.asarray(conv_w_dw)[0, k] for k in range(31)]
    vecs = np.ascontiguousarray(np.concatenate([_to_pc(v) for v in vec_list], axis=1))
    shared = {
        "meta": f(meta_tokens), "vecs": vecs, "pbrow": f(np.asarray(pool_b)[0].reshape(1, D)), "w_pw1": f(np.asarray(conv_w_pw1)[0]), "w_pw2": f(np.asarray(conv_w_pw2)[0]),
        "w_pool": f(np.asarray(pool_w)[0]), "w_m1": f(mlp_w1), "w_m2": f(mlp_w2),
    }
    in_maps = []
    for c in range(NCORES):
        m = dict(shared)
        m["xp"] = x_prompt[c]
        m["xs"] = x_sample[16 * c:16 * (c + 1)].reshape(128, D)
        m["sconv"] = state_conv[0, 16 * c:16 * (c + 1)]
        m["spool"] = state_pool[0, 16 * c:16 * (c + 1)]
        in_maps.append(m)
    if "nc" not in _NC_CACHE:
        _NC_CACHE["nc"] = build_program(_STOP)
    nc = _NC_CACHE["nc"]
    res = run_bass_kernel_spmd(nc, in_maps, core_ids=list(range(NCORES)))
    R = res.results
    if _STOP is not None:
        _NC_CACHE["dbg"] = R
    y_prompt = np.stack([R[c]["yp"] for c in range(NCORES)], axis=0)
    y_sample = np.concatenate([R[c]["ys"].reshape(16, 8, D) for c in range(NCORES)], axis=0)
    ncp = np.stack([R[c]["ncp"] for c in range(NCORES)], axis=0)[None]
    ncs = np.concatenate([R[c]["ncs"] for c in range(NCORES)], axis=0)[None]
    npp = np.stack([R[c]["npp"] for c in range(NCORES)], axis=0)[None]
    nps = np.concatenate([R[c]["nps"] for c in range(NCORES)], axis=0)[None]
    return (y_prompt.astype(np.float32), y_sample.astype(np.float32), ncp.astype(np.float32),
            ncs.astype(np.float32), npp.astype(np.float32), nps.astype(np.float32))
```

```python
import numpy as np
from contextlib import ExitStack
import concourse.bass as bass
import concourse.mybir as mybir
from concourse.bass_utils import run_bass_kernel_spmd

F32 = mybir.dt.float32
BF16 = mybir.dt.bfloat16
AF = mybir.ActivationFunctionType
ALU = mybir.AluOpType

ENGS = ("pe", "act", "dve", "pool", "sp")
NCORES = 8
D = 1024
NT = 2192
NP = 2064
EPS = 1e-6
TILES = [(0, 512), (512, 512), (1024, 512), (1536, 512), (2048, 144)]
VW = 2702
VS0 = 2094
NSLOT = 3
DEBUG_DUMP = False
SELF_DIST = 2

V_GMIX0, V_GMIX1, V_GMLP0, V_GMLP1, V_GFIN, V_BA, V_BG, V_BDW, V_LNG, V_LNB, V_BPW2, V_PB, V_PS = range(13)
V_WDW = 13
NV = 13 + 31


class DSem:
    def __init__(self, handle, key):
        self.h = handle
        self.key = key
        self.count = 0


class Sched:
    def __init__(self, nc, stack):
        self.nc = nc
        self.stack = stack
        self.lists = {e: [] for e in ENGS}
        self.sem = {e: stack.enter_context(nc.semaphore("s_" + e)) for e in ENGS if e != "sp"}
        self.count = {e: 0 for e in ENGS}
        self.opidx = {e: 0 for e in ENGS}
        self.waited = {e: {} for e in ENGS}
        self.reg = {}
        self.semh = {e: self.sem[e] for e in self.sem}
        self.ndsem = 0
        self.out_events = []
        self.nwaits = 0
        self.dead = False

    def dsem(self, name=None):
        self.ndsem += 1
        key = "d%d" % self.ndsem
        h = self.stack.enter_context(self.nc.semaphore(name or key))
        self.semh[key] = h
        return DSem(h, key)

    @staticmethod
    def _norm(k):
        if isinstance(k, tuple) and len(k) == 3 and isinstance(k[1], int) and isinstance(k[2], int) \
                and isinstance(k[0], str):
            return k
        return (k, 0, 1)

    def _deps_and_register(self, reads, writes, ev):
        deps = []
        for is_w, lst in ((False, reads), (True, writes)):
            for k in lst:
                rg, lo, hi = self._norm(k)
                ents = self.reg.setdefault(rg, [])
                keep = []
                merged = False
                for e in ents:
                    elo, ehi, ew, eev = e
                    if ehi <= lo or elo >= hi:
                        keep.append(e)
                        continue
                    if eev is ev:
                        if is_w and lo <= elo and ehi <= hi:
                            continue
                        keep.append(e)
                        continue
                    if is_w or ew:
                        deps.append(eev)
                    if is_w and lo <= elo and ehi <= hi:
                        continue
                    if (not is_w) and (not ew) and elo == lo and ehi == hi and eev[1] == ev[1]:
                        e[3] = ev
                        merged = True
                    keep.append(e)
                if not merged:
                    keep.append([lo, hi, is_w, ev])
                self.reg[rg] = keep
        return deps

    def _emit_waits(self, eng, deps, is_dma):
        waits = {}
        for (deng, skey, val, didx, d_is_dma) in deps:
            if deng == eng and not d_is_dma and not is_dma:
                if eng == "pe":
                    continue
                if eng == "pool" and didx < self.opidx[eng] - SELF_DIST:
                    continue
            if self.waited[eng].get(skey, 0) >= val:
                continue
            if waits.get(skey, 0) < val:
                waits[skey] = val
        out = []
        for skey, val in waits.items():
            self.waited[eng][skey] = val
            out.append((self.semh[skey], val))
            self.nwaits += 1
        return out

    def op(self, eng, fn, reads=(), writes=()):
        if self.dead:
            return None
        self.count[eng] += 1
        ev = (eng, eng, self.count[eng], self.opidx[eng], False)
        deps = self._deps_and_register(reads, writes, ev)
        waits = self._emit_waits(eng, deps, False)
        self.opidx[eng] += 1
        self.lists[eng].append((waits, fn, [(self.sem[eng], 1)]))
        return ev

    def dma(self, eng, pairs, dsem, reads=(), writes=(), is_output=False, slow=False):
        if self.dead:
            return None
        dsem.count += 16 * len(pairs)
        ev = ("dma_" + eng, dsem.key, dsem.count, -1, True)
        deps = self._deps_and_register(reads, writes, ev)
        waits = self._emit_waits(eng, deps, True)
        self.opidx[eng] += 1

        def fn(e, pairs=pairs, h=dsem.h, slow=slow):
            for (o, i) in pairs:
                if slow:
                    e.dma_start(out=o, in_=i, allow_slow_non_contiguous=True).then_inc(h, 16)
                else:
                    e.dma_start(out=o, in_=i).then_inc(h, 16)
            return None

        self.lists[eng].append((waits, fn, None))
        if is_output:
            self.out_events.append(ev)
        return ev

    def finish(self):
        waits = {}
        for ev in self.out_events:
            if waits.get(ev[1], 0) < ev[2]:
                waits[ev[1]] = ev[2]
        for e in self.sem:
            if self.count[e] > 0:
                waits[e] = self.count[e]
        for rg, ents in self.reg.items():
            for e in ents:
                ev = e[3]
                if ev[4] and waits.get(ev[1], 0) < ev[2]:
                    waits[ev[1]] = ev[2]
        self.lists["sp"].append(([(self.semh[k], v) for k, v in waits.items()], None, None))

    def emit(self):
        nc = self.nc
        lists = self.lists

        def replay(name, eng):
            for (waits, fn, incs) in lists[name]:
                for (h, v) in waits:
                    eng.wait_ge(h, v)
                if fn is None:
                    continue
                ins = fn(eng)
                if incs:
                    for (h, a) in incs:
                        ins.then_inc(h, a)

        with nc.Block() as block:
            @block.tensor
            def _(e):
                replay("pe", e)

            @block.scalar
            def _(e):
                replay("act", e)

            @block.vector
            def _(e):
                replay("dve", e)

            @block.gpsimd
            def _(e):
                replay("pool", e)

            @block.sync
            def _(e):
                replay("sp", e)


class _Stop(Exception):
    pass


def build_program(stop=None):
    nc = bass.Bass("TRN2", target_bir_lowering=False)
    dt_in = lambda name, shape: nc.dram_tensor(name, shape, F32, kind="ExternalInput").ap()
    dt_out = lambda name, shape: nc.dram_tensor(name, shape, F32, kind="ExternalOutput").ap()
    xp = dt_in("xp", [2048, D])
    xs = dt_in("xs", [128, D])
    sconv = dt_in("sconv", [16, 30, D])
    spool = dt_in("spool", [16, 15, D])
    meta = dt_in("meta", [16, D])
    vecs = dt_in("vecs", [128, NV * 8])
    pbrow = dt_in("pbrow", [1, D])
    w_pw1 = dt_in("w_pw1", [D, 2 * D])
    w_pw2 = dt_in("w_pw2", [D, D])
    w_pool = dt_in("w_pool", [4, 256, 256])
    w_m1 = dt_in("w_m1", [2, D, 4 * D])
    w_m2 = dt_in("w_m2", [2, 4 * D, D])
    yp = dt_out("yp", [2048, D])
    ys = dt_out("ys", [128, D])
    ncp = dt_out("ncp", [30, D])
    ncs = dt_out("ncs", [16, 30, D])
    npp = dt_out("npp", [15, D])
    nps = dt_out("nps", [16, 15, D])
    if stop is not None:
        dbg_x = dt_out("dbg_x", [128, 8 * NT])
        dbg_h = nc.dram_tensor("dbg_h", [128, 8 * NT], BF16, kind="ExternalOutput").ap()
        dbg_s = nc.dram_tensor("dbg_s", [128, 8 * VW], BF16, kind="ExternalOutput").ap()

    with ExitStack() as st:
        sc = Sched(nc, st)
        SB = lambda name, shape, dt: st.enter_context(nc.sbuf_tensor(name, shape, dt))
        Xt = SB("X", [128, 8 * NT], F32)
        Hreg = SB("Hreg", [128, 8 * NT], BF16)
        Sreg = SB("Sreg", [128, 8 * VW], BF16)
        Wreg = SB("Wreg", [128, NSLOT * 4096], BF16)
        ZBt = SB("ZB", [128, 8 * 512], BF16)
        SQRt = SB("SQR", [128, 8 * 512], BF16)
        STt = SB("ST", [128, 4 * 512], F32)
        INt = SB("IN", [128, 2 * 1024], F32)
        PTt = SB("PT", [128, 2 * 512], F32)
        VEC = SB("VEC", [128, NV * 8], F32)
        G32 = SB("G32", [128, 5 * 8], F32)
        PBS = SB("PBS", [128, 8], F32)
        INVC = SB("INVC", [128, 4 * 16], F32)
        ident = SB("ident", [128, 128], F32)
        identb = SB("identb", [128, 128], BF16)
        ones = SB("ones", [128, 128], BF16)
        ps = st.enter_context(nc.psum_tensor("ps", [128, 8, 512], F32))

        X = Xt[:, :].rearrange("p (c n) -> p c n", c=8)
        H = Hreg[:, :].rearrange("p (c n) -> p c n", c=8)
        Cv = Hreg[:, 0:8192].bitcast(F32).rearrange("p (c n) -> p c n", c=8)
        DG = Hreg[:, 8192:8192 + 2 * 31 * 128].rearrange("p (b k n) -> p b k n", b=2, k=31)
        HF = Hreg[:, 0:2 * 8 * 527 * 2].bitcast(F32).rearrange("p (b c n) -> p b c n", b=2, c=8)
        YT = Hreg[:, 0:2 * 8 * 512 * 2].bitcast(F32).rearrange("p (b c n) -> p b c n", b=2, c=8)
        VE = Sreg[:, :].rearrange("p (c n) -> p c n", c=8)
        HID = Sreg[:, 0:8 * NT].rearrange("p (c n) -> p c n", c=8)
        OST = Sreg[:, 0:10 * 2048].bitcast(F32).rearrange("p (s n) -> p s n", s=10)
        WA = [Wreg[:, s * 4096:(s + 1) * 4096].rearrange("p (k m) -> p k m", k=8) for s in range(NSLOT)]
        WB = [Wreg[:, s * 4096:(s + 1) * 4096].rearrange("p (k m) -> p k m", k=4) for s in range(NSLOT)]
        WP = [Wreg[:, s * 4096:s * 4096 + 2048].rearrange("p (g k n) -> p g k n", g=4, k=2) for s in range(NSLOT)]
        ZB = ZBt[:, :].rearrange("p (c n) -> p c n", c=8)
        SQR = SQRt[:, :].rearrange("p (c n) -> p c n", c=8)
        ST = STt[:, :].rearrange("p (c n) -> p c n", c=4)
        VO = STt[:, 0:8 * 158].rearrange("p (c n) -> p c n", c=8)
        IN = INt[:, :].rearrange("p (s n) -> p s n", s=2)
        ZBF = ZBt[:, :].bitcast(F32).rearrange("p (s n) -> p s n", s=2)
        PT = PTt[:, :].rearrange("p (s n) -> p s n", s=2)
        vec = lambda v, c: VEC[:, v * 8 + c:v * 8 + c + 1]
        g32 = lambda i, c: G32[:, i * 8 + c:i * 8 + c + 1]

        rX = lambda c, a, n: ("X", (c * NT + a) * 4, (c * NT + a + n) * 4)
        rH = lambda c, a, n: ("H", (c * NT + a) * 2, (c * NT + a + n) * 2)
        rC = lambda c, a=0, n=512: ("H", (c * 512 + a) * 4, (c * 512 + a + n) * 4)
        rDG = lambda b, k0=0, k1=31: ("H", 16384 + (b * 31 + k0) * 256, 16384 + (b * 31 + k1) * 256)
        rHF = lambda b, c, a=0, n=527: ("H", ((b * 8 + c) * 527 + a) * 4, ((b * 8 + c) * 527 + a + n) * 4)
        rYT = lambda b, c, a=0, n=512: ("H", ((b * 8 + c) * 512 + a) * 4, ((b * 8 + c) * 512 + a + n) * 4)
        rVE = lambda c, a, n: ("S", (c * VW + a) * 2, (c * VW + a + n) * 2)
        rHID = lambda c, a, n: ("S", (c * NT + a) * 2, (c * NT + a + n) * 2)
        rOST = lambda s: ("S", s * 4096, (s + 1) * 4096)
        rW = lambda s: ("W", s * 8192, (s + 1) * 8192)
        rZ = lambda c, a=0, n=512: ("ZB", (c * 512 + a) * 2, (c * 512 + a + n) * 2)
        rSQ = lambda i: ("SQR", i * 1024, (i + 1) * 1024)
        rST = lambda i: ("ST", i * 2048, (i + 1) * 2048)
        rSTall = ("ST", 0, 8192)
        rIN = lambda s: ("IN", s * 4096, (s + 1) * 4096)
        rPT = lambda s: ("PT", s * 2048, (s + 1) * 2048)
        rPS = lambda b: ("PS", b, b + 1)

        def chk(stage):
            if stop == stage and not sc.dead:
                dd = sc.dsem("d_dbg")
                sc.dma("sp", [(dbg_x, Xt[:, :])], dd, reads=[("X", 0, 8 * NT * 4)], writes=["dbg_x"], is_output=True)
                sc.dma("sp", [(dbg_h, Hreg[:, :])], dd, reads=[("H", 0, 8 * NT * 2)], writes=["dbg_h"], is_output=True)
                sc.dma("sp", [(dbg_s, Sreg[:, :])], dd, reads=[("S", 0, 8 * VW * 2)], writes=["dbg_s"], is_output=True)
                sc.dead = True

        bank_ctr = [0]

        nb_mod = [8]

        def nb():
            b = bank_ctr[0] % nb_mod[0]
            bank_ctr[0] += 1
            return b

        sq_ctr = [0]

        def nsq():
            i = sq_ctr[0] % 8
            sq_ctr[0] += 1
            return i

        alt = [0]

        def alt_eng(a="act", b="dve"):
            alt[0] += 1
            return a if alt[0] % 2 else b

        def copy_op(eng, out, in_, reads, writes):
            if eng == "act":
                sc.op("act", lambda e: e.activation(out=out, in_=in_, func=AF.Copy), reads=reads, writes=writes)
            else:
                sc.op(eng, lambda e: e.tensor_copy(out=out, in_=in_), reads=reads, writes=writes)

        d_const = sc.dsem("d_const")
        d_in = [sc.dsem("d_in0"), sc.dsem("d_in1"), sc.dsem("d_in2"), sc.dsem("d_in3")]
        STG = [(IN[:, 0, :], ("IN", 0, 4096), d_in[0]), (IN[:, 1, :], ("IN", 4096, 8192), d_in[1]),
               (ZBF[:, 0, :], ("ZB", 0, 4096), d_in[2]), (ZBF[:, 1, :], ("ZB", 4096, 8192), d_in[3])]
        d_w = [sc.dsem("d_w%d" % s) for s in range(NSLOT)]
        d_ost = [sc.dsem("d_ost%d" % s) for s in range(10)]
        d_misc = sc.dsem("d_misc")
        d_pt = sc.dsem("d_pt")

        sc.op("pool", lambda e: e.memset(ident[:], 0.0), writes=["ident"])
        sc.op("pool", lambda e: e.affine_select(out=ident[:], in_=ident[:], compare_op=ALU.not_equal, fill=1.0,
                                                base=0, pattern=[[-1, 128]], channel_multiplier=1),
              reads=["ident"], writes=["ident"])
        sc.op("pool", lambda e: e.memset(ones[:], 1.0), writes=["ones"])
        sc.op("pool", lambda e: e.tensor_copy(out=identb[:], in_=ident[:]), reads=["ident"], writes=["identb"])

        for g, w in enumerate((2, 4, 8, 16)):
            sc.op("pool", lambda e, g=g, w=w: e.memset(INVC[:, g * 16:(g + 1) * 16], 1.0 / w),
                  writes=[("invc", g * 16, (g + 1) * 16)])
            for t in range(w - 1):
                sc.op("pool", lambda e, g=g, t=t: e.memset(INVC[:, g * 16 + t:g * 16 + t + 1], 1.0 / (t + 1)),
                      writes=[("invc", g * 16 + t, g * 16 + t + 1)])
        sc.dma("sp", [(VEC[:], vecs)], d_const, writes=["vec"])
        sc.op("pool", lambda e: e.memset(VE[:, :, 0:30], 0.0), writes=[rVE(c, 0, 30) for c in range(8)])
        for i, v in enumerate((V_GMIX0, V_GMLP0, V_GMIX1, V_GMLP1, V_GFIN)):
            sc.op("dve", lambda e, i=i, v=v: e.tensor_scalar(out=G32[:, i * 8:(i + 1) * 8], in0=VEC[:, v * 8:(v + 1) * 8],
                                                             scalar1=32.0, scalar2=None, op0=ALU.mult),
                  reads=["vec"], writes=[("g32", i, i + 1)])
        sc.op("dve", lambda e: e.tensor_tensor(out=PBS[:], in0=VEC[:, V_PB * 8:(V_PB + 1) * 8],
                                               in1=VEC[:, V_PS * 8:(V_PS + 1) * 8], op=ALU.mult),
              reads=["vec"], writes=["pbs"])

        chk('c0')
        wloads = []
        for hb in range(4):
            wloads.append(("A", [(lambda s, hb=hb: WA[s][:, :, 0:256],
                                  w_pw1[:, 256 * hb:256 * hb + 256].rearrange("(k p) m -> p k m", p=128)),
                                 (lambda s, hb=hb: WA[s][:, :, 256:512],
                                  w_pw1[:, D + 256 * hb:D + 256 * hb + 256].rearrange("(k p) m -> p k m", p=128))]))
        for hb in range(2):
            wloads.append(("A", [(lambda s: WA[s][:, :, :],
                                  w_pw2[:, 512 * hb:512 * hb + 512].rearrange("(k p) m -> p k m", p=128))]))

        def mlp_loads(layer):
            for fb in range(4):
                for hb in range(2):
                    c0 = fb * 1024 + hb * 512
                    wloads.append(("A", [(lambda s: WA[s][:, :, :],
                                          w_m1[layer, :, c0:c0 + 512].rearrange("(k p) m -> p k m", p=128))]))
                for hb in range(2):
                    r0 = fb * 1024 + hb * 512
                    wloads.append(("B", [(lambda s: WB[s][:, :, :],
                                          w_m2[layer, r0:r0 + 512, :].rearrange("(k p) m -> p k m", p=128))]))
        mlp_loads(0)
        wloads.append(("P", [(lambda s: WP[s][:, :, :, :], w_pool.rearrange("g (k p) n -> p g k n", p=128)),
                             (lambda s: Wreg[0:1, s * 4096 + 2048:s * 4096 + 3072], pbrow)]))
        mlp_loads(1)
        wnext = [0]

        def issue_wload():
            i = wnext[0]
            if i >= len(wloads):
                return
            wnext[0] += 1
            s = i % NSLOT
            pairs = [(vf(s), ap) for (vf, ap) in wloads[i][1]]
            sc.dma("pool", pairs, d_w[s], writes=[rW(s)])

        wuse = [0]

        def next_w():
            i = wuse[0]
            wuse[0] += 1
            assert sc.dead or i < wnext[0], "weight load not issued"
            return i % NSLOT

        for _ in range(NSLOT):
            if not _NOW:
                issue_wload()

        chk('cw')
        in_ctr = [0]

        def load_rows(src_ap, nrows, dst_fn, dst_ranges_fn):
            sap, srng, sd = STG[in_ctr[0] % 4]
            in_ctr[0] += 1
            sc.dma("sp", [(sap[0:nrows, :], src_ap)], sd, writes=[srng])
            for half in range(2):
                b = nb()

                def tr(e, half=half, b=b, sap=sap):
                    ins = None
                    for cc in range(4):
                        c = half * 4 + cc
                        ins = e.transpose(out=ps[:, b, cc * nrows:(cc + 1) * nrows],
                                          in_=sap[0:nrows, c * 128:(c + 1) * 128],
                                          identity=ident[0:nrows, 0:nrows])
                    return ins
                sc.op("pe", tr, reads=[srng, "ident"], writes=[rPS(b)])
                src = ps[:, b, 0:4 * nrows].rearrange("p (c n) -> p c n", c=4)
                copy_op(alt_eng(), dst_fn(half), src, [rPS(b)], dst_ranges_fn(half))

        def load_x_cols(col0, ncols, src_ap):
            load_rows(src_ap, ncols,
                      lambda half: X[:, half * 4:(half + 1) * 4, col0:col0 + ncols],
                      lambda half: [rX(half * 4 + cc, col0, ncols) for cc in range(4)])

        def rms_stats(c0, n, ri=0):
            b = nb()
            for c in range(8):
                i = nsq()
                sc.op("act", lambda e, c=c, i=i: e.activation(out=SQR[:, i, 0:n], in_=X[:, c, c0:c0 + n], func=AF.Square),
                      reads=[rX(c, c0, n)], writes=[rSQ(i)])
                sc.op("pe", lambda e, c=c, i=i, b=b: e.matmul(ps[:, b, 0:n], lhsT=ones[:], rhs=SQR[:, i, 0:n],
                                                             start=(c == 0), stop=(c == 7)),
                      reads=[rSQ(i), "ones"], writes=[rPS(b)])
            sc.op("act", lambda e, b=b: e.activation(out=ST[:, ri, 0:n], in_=ps[:, b, 0:n], func=AF.Ln,
                                                     scale=1.0, bias=D * EPS),
                  reads=[rPS(b)], writes=[rST(ri)])
            sc.op("act", lambda e: e.activation(out=ST[:, ri, 0:n], in_=ST[:, ri, 0:n], func=AF.Exp, scale=-0.5),
                  reads=[rST(ri)], writes=[rST(ri)])

        def rms_apply(gi, c, c0, n, out_ap, out_rng, eng="dve", ri=0):
            sc.op(eng, lambda e: e.scalar_tensor_tensor(out=out_ap, in0=X[:, c, c0:c0 + n], scalar=g32(gi, c),
                                                        in1=ST[:, ri, 0:n], op0=ALU.mult, op1=ALU.mult),
                  reads=[rX(c, c0, n), rST(ri), ("g32", gi, gi + 1)], writes=out_rng)

        def rmsnorm_to_H(gi, c0, n):
            rms_stats(c0, n)
            for c in range(8):
                rms_apply(gi, c, c0, n, H[:, c, c0:c0 + n], [rH(c, c0, n)])

        chk('c2')

        def load_tile(ti):
            c0, n = TILES[ti]
            if ti == 0:
                load_x_cols(0, 16, meta)
                load_x_cols(16, 112, xp[0:112, :])
                for j in range(3):
                    load_x_cols(128 + 128 * j, 128, xp[112 + 128 * j:240 + 128 * j, :])
            elif ti < 4:
                for j in range(4):
                    cc0 = c0 + 128 * j
                    load_x_cols(cc0, 128, xp[cc0 - 16:cc0 + 112, :])
            else:
                load_x_cols(2048, 16, xp[2032:2048, :])
                load_x_cols(2064, 128, xs)

        def pw1_step(hb, s, ti):
            c0, n = TILES[ti]
            for mi in range(2):
                m = hb * 2 + mi
                bg = nb()
                ba = nb()

                def mm_g(e, s=s, mi=mi, c0=c0, n=n, bg=bg):
                    ins = None
                    for k in range(8):
                        ins = e.matmul(ps[:, bg, 0:n], lhsT=WA[s][:, k, 256 + mi * 128:256 + (mi + 1) * 128],
                                       rhs=H[:, k, c0:c0 + n], start=(k == 0), stop=(k == 7))
                    return ins
                sc.op("pe", mm_g, reads=[rW(s)] + [rH(k, c0, n) for k in range(8)], writes=[rPS(bg)])

                def mm_a(e, s=s, mi=mi, c0=c0, n=n, ba=ba):
                    ins = None
                    for k in range(8):
                        ins = e.matmul(ps[:, ba, 0:n], lhsT=WA[s][:, k, mi * 128:(mi + 1) * 128],
                                       rhs=H[:, k, c0:c0 + n], start=(k == 0), stop=(k == 7))
                    return ins
                sc.op("pe", mm_a, reads=[rW(s)] + [rH(k, c0, n) for k in range(8)], writes=[rPS(ba)])
                sg = 2 + (m + ti) % 2
                sc.op("act", lambda e, bg=bg, n=n, m=m, sg=sg: e.activation(out=ST[:, sg, 0:n], in_=ps[:, bg, 0:n],
                                                                           func=AF.Sigmoid, bias=vec(V_BG, m), scale=1.0),
                      reads=[rPS(bg), "vec"], writes=[rST(sg)])
                if ti < 4:
                    sc.op("dve", lambda e, ba=ba, n=n, m=m, sg=sg, c0=c0: e.scalar_tensor_tensor(
                        out=VE[:, m, 30 + c0:30 + c0 + n], in0=ps[:, ba, 0:n], scalar=vec(V_BA, m), in1=ST[:, sg, 0:n],
                        op0=ALU.add, op1=ALU.mult), reads=[rPS(ba), rST(sg), "vec"], writes=[rVE(m, 30 + c0, n)])
                else:
                    sc.op("dve", lambda e, ba=ba, m=m, sg=sg: e.scalar_tensor_tensor(
                        out=VE[:, m, 30 + 2048:30 + 2064], in0=ps[:, ba, 0:16], scalar=vec(V_BA, m), in1=ST[:, sg, 0:16],
                        op0=ALU.add, op1=ALU.mult), reads=[rPS(ba), rST(sg), "vec"], writes=[rVE(m, 30 + 2048, 16)])
                    sc.op("dve", lambda e, ba=ba, m=m, sg=sg: e.scalar_tensor_tensor(
                        out=VE[:, m, VS0:VS0 + 608].rearrange("p (s j) -> p s j", j=38)[:, :, 30:38],
                        in0=ps[:, ba, 16:144].rearrange("p (s t) -> p s t", t=8), scalar=vec(V_BA, m),
                        in1=ST[:, sg, 16:144].rearrange("p (s t) -> p s t", t=8),
                        op0=ALU.add, op1=ALU.mult), reads=[rPS(ba), rST(sg), "vec"], writes=[rVE(m, VS0, 608)])

        def load_states():
            for q in range(4):
                def dstf(half, q=q):
                    return VE[:, half * 4:(half + 1) * 4, VS0 + q * 4 * 38:VS0 + (q + 1) * 4 * 38] \
                        .rearrange("p c (s j) -> p c s j", j=38)[:, :, :, 0:30]
                s_ = in_ctr[0] % 2
                in_ctr[0] += 1
                sc.dma("sp", [(IN[0:120, s_, :], sconv[q * 4:(q + 1) * 4].rearrange("s j d -> (s j) d"))], d_in[s_],
                       writes=[rIN(s_)])
                for half in range(2):
                    b = nb()

                    def tr(e, half=half, b=b, s_=s_):
                        ins = None
                        for cc in range(4):
                            c = half * 4 + cc
                            ins = e.transpose(out=ps[:, b, cc * 120:(cc + 1) * 120], in_=IN[0:120, s_, c * 128:(c + 1) * 128],
                                              identity=ident[0:120, 0:120])
                        return ins
                    sc.op("pe", tr, reads=[rIN(s_), "ident"], writes=[rPS(b)])
                    src = ps[:, b, 0:480].rearrange("p (c s j) -> p c s j", c=4, s=4)
                    copy_op(alt_eng(), dstf(half), src, [rPS(b)],
                            [rVE(half * 4 + cc, VS0 + q * 152, 152) for cc in range(4)])

            chk('c1')
            sc.dma("sp", [(ncs[:, 0:22, :], sconv[:, 8:30, :])], d_pt, writes=["ncs_old"], is_output=True)
            sc.dma("sp", [(nps[:, 0:7, :], spool[:, 8:15, :])], d_pt, writes=["nps_old"], is_output=True)


        dg_slot = {}
        DG2 = Wreg[:, 0:31 * 128].rearrange("p (k n) -> p k n", k=31)

        DG3 = INt[:, :].bitcast(BF16)[:, 0:31 * 128].rearrange("p (k n) -> p k n", k=31)
        DG4 = ZBt[:, 0:31 * 128].rearrange("p (k n) -> p k n", k=31)

        def DGv(db, k):
            if db < 2:
                return DG[:, db, k, :]
            return (DG2, DG3, DG4)[db - 2][:, k, :]

        def rDGx(db, k0=0, k1=31):
            if db < 2:
                return rDG(db, k0, k1)
            return (("W", "IN", "ZB")[db - 2], k0 * 256, k1 * 256)

        dg_ctr = [0]

        def dg_build(ti, m, db=None):
            if db is None:
                db = dg_ctr[0] % 3
                dg_ctr[0] += 1
            dg_slot[(ti, m)] = db
            for k in range(31):
                if k % 3 == 0:
                    sc.op("dve", lambda e, k=k, m=m, db=db: e.tensor_scalar(out=DGv(db, k), in0=identb[:],
                                                                          scalar1=vec(V_WDW + k, m), scalar2=None, op0=ALU.mult),
                          reads=["identb", "vec"], writes=[rDGx(db, k, k + 1)])
                elif k % 3 == 1:
                    sc.op("pool", lambda e, k=k, m=m, db=db: e.tensor_tensor(
                        out=DGv(db, k), in0=identb[:], in1=vec(V_WDW + k, m).to_broadcast([128, 128]), op=ALU.mult),
                        reads=["identb", "vec"], writes=[rDGx(db, k, k + 1)])
                else:
                    sc.op("act", lambda e, k=k, m=m, db=db: e.activation(out=DGv(db, k), in_=identb[:], func=AF.Identity,
                                                                         scale=vec(V_WDW + k, m)),
                          reads=["identb", "vec"], writes=[rDGx(db, k, k + 1)])

        pw1_slots = [next_w(), next_w(), next_w()]
        load_tile(0)
        for step in range(8):
            if step + 1 < 5:
                load_tile(step + 1)
            if step < 5:
                rmsnorm_to_H(0, *TILES[step])
            if step == 2:
                load_states()
            for hb in range(3):
                ti = step - 1 - hb
                if 0 <= ti < 5:
                    pw1_step(hb, pw1_slots[hb], ti)
                    if ti == 4:
                        issue_wload()
        chk('load')
        dg_build(0, 0, db=3)
        dg_build(0, 1, db=4)
        s3 = next_w()
        for ti in range(5):
            pw1_step(3, s3, ti)

        chk('pw1')
        for c in range(8):
            sc.op("dve", lambda e, c=c: e.tensor_copy(out=VO[:, c, 0:30], in_=VE[:, c, 30 + 2034:30 + 2064]),
                  reads=[rVE(c, 30 + 2034, 30)], writes=[rSTall])
            sc.op("dve", lambda e, c=c: e.tensor_copy(
                out=VO[:, c, 30:158].rearrange("p (s t) -> p s t", t=8),
                in_=VE[:, c, VS0:VS0 + 608].rearrange("p (s j) -> p s j", j=38)[:, :, 30:38]),
                reads=[rVE(c, VS0, 608)], writes=[rSTall])

        def out_rows(src_fn, nrows, stg_slot_ap, stg_rng, dma_pairs_fn, dsem, name, src_reads):
            for half in range(2):
                b = nb()

                def tr(e, half=half, b=b):
                    ins = None
                    for cc in range(4):
                        c = half * 4 + cc
                        ins = e.transpose(out=ps[0:nrows, b, cc * 128:(cc + 1) * 128], in_=src_fn(c), identity=ident[:])
                    return ins
                sc.op("pe", tr, reads=list(src_reads) + ["ident"], writes=[rPS(b)])
                copy_op(alt_eng(), stg_slot_ap[0:nrows, half * 512:(half + 1) * 512], ps[0:nrows, b, :], [rPS(b)], [stg_rng])
            sc.dma("sp", dma_pairs_fn(), dsem, reads=[stg_rng], writes=[name], is_output=True)

        def nc_out():
            out_rows(lambda c: VO[:, c, 0:30], 30, IN[:, 0, :], rIN(0), lambda: [(ncp[:, :], IN[0:30, 0, :])], d_in[0], "ncp",
                     [rSTall])
            out_rows(lambda c: VO[:, c, 30:158], 128, IN[:, 1, :], rIN(1),
                     lambda: [(ncs[s_, 22:30, :], IN[s_ * 8:(s_ + 1) * 8, 1, :]) for s_ in range(16)], d_in[1], "ncs_new",
                     [rSTall])

        chk('ncout')
        s_pw2 = [next_w(), next_w()]
        nb_mod[0] = 6
        bs1 = 6
        bs2 = 7
        pend = []
        STATS_LAG = 3

        C4 = INt[:, 0:8 * 144].rearrange("p (c n) -> p c n", c=8)
        Z4 = INt[:, 1152:1152 + 576].bitcast(BF16).rearrange("p (c n) -> p c n", c=8)
        SQ4 = PTt[:, :].bitcast(BF16).rearrange("p (c n) -> p c n", c=8)
        rC4 = lambda m: ("IN", m * 576, (m + 1) * 576)
        rZ4 = lambda m: ("IN", 4608 + m * 288, 4608 + (m + 1) * 288)
        rSQ4 = lambda i: ("PT", i * 512, (i + 1) * 512)
        sq4_ctr = [0]

        def nsq4():
            i = sq4_ctr[0] % 8
            sq4_ctr[0] += 1
            return i

        LANE = {
            0: dict(C=lambda m, n: Cv[:, m, 0:n], rC=lambda m, n: rC(m, 0, n), Z=lambda m, n: ZB[:, m, 0:n],
                    rZ=lambda m, n: rZ(m, 0, n), sq=lambda i, n: SQR[:, i, 0:n], rsq=rSQ, nsq=nsq, bs=(6, 7),
                    mean=lambda n: ST[:, 0, 0:n], rstd=lambda n: ST[:, 1, 0:n], msq=lambda n: ST[:, 2, 0:n],
                    rmean=rST(0), rrstd=rST(1), rmsq=rST(2)),
            1: dict(C=lambda m, n: C4[:, m, 0:n], rC=lambda m, n: rC4(m), Z=lambda m, n: Z4[:, m, 0:n],
                    rZ=lambda m, n: rZ4(m), sq=lambda i, n: SQ4[:, i, 0:n], rsq=rSQ4, nsq=nsq4, bs=(4, 5),
                    mean=lambda n: ST[:, 3, 0:n], rstd=lambda n: ST[:, 3, 160:160 + n], msq=lambda n: ST[:, 3, 320:320 + n],
                    rmean=("ST", 6144, 6144 + 640), rrstd=("ST", 6144 + 640, 6144 + 1280), rmsq=("ST", 6144 + 1280, 6144 + 1920)),
        }

        def conv_chunk(ti, m, lane=0, db=None):
            c0, n = TILES[ti]
            L = LANE[lane]
            if db is None:
                db = dg_slot[(ti, m)]
            b = nb()
            if ti < 4:
                def conv(e, m=m, db=db, b=b, c0=c0, n=n):
                    ins = None
                    for k in range(31):
                        ins = e.matmul(ps[:, b, 0:n], lhsT=DGv(db, k), rhs=VE[:, m, c0 + k:c0 + k + n],
                                       start=(k == 0), stop=(k == 30))
                    return ins
                sc.op("pe", conv, reads=[rDGx(db), rVE(m, c0, n + 30)], writes=[rPS(b)])
            else:
                def conv(e, m=m, db=db, b=b):
                    ins = None
                    for k in range(31):
                        ins = e.matmul(ps[:, b, 0:16], lhsT=DGv(db, k), rhs=VE[:, m, 2048 + k:2048 + k + 16],
                                       start=(k == 0), stop=(k == 30))
                    for k in range(31):
                        ins = e.matmul(ps[:, b, 16:144].rearrange("p (s t) -> p s t", t=8), lhsT=DGv(db, k),
                                       rhs=VE[:, m, VS0:VS0 + 608].rearrange("p (s j) -> p s j", j=38)[:, :, k:k + 8],
                                       start=(k == 0), stop=(k == 30))
                    return ins
                sc.op("pe", conv, reads=[rDGx(db), rVE(m, 2048, 46), rVE(m, VS0, 608)], writes=[rPS(b)])
            while len(pend) >= STATS_LAG * (2 if merged[0] else 1):
                pend.pop(0)()
            i1 = L["nsq"]()
            i2 = L["nsq"]()
            bs1_, bs2_ = L["bs"]
            sc.op("act", lambda e: e.activation(out=L["C"](m, n), in_=ps[:, b, 0:n], func=AF.Identity,
                                                bias=vec(V_BDW, m), scale=1.0),
                  reads=[rPS(b), "vec"], writes=[L["rC"](m, n)])
            sc.op("act", lambda e: e.activation(out=L["sq"](i1, n), in_=ps[:, b, 0:n],
                                                func=AF.Identity, bias=vec(V_BDW, m), scale=1.0),
                  reads=[rPS(b), "vec"], writes=[L["rsq"](i1)])
            sc.op("act", lambda e: e.activation(out=L["sq"](i2, n), in_=ps[:, b, 0:n],
                                                func=AF.Square, bias=vec(V_BDW, m), scale=1.0),
                  reads=[rPS(b), "vec"], writes=[L["rsq"](i2)])

            def stats_mm():
                sc.op("pe", lambda e: e.matmul(ps[:, bs1_, 0:n], lhsT=ones[:], rhs=L["sq"](i1, n),
                                               start=(m == 0), stop=(m == 7)),
                      reads=[L["rsq"](i1), "ones"], writes=[rPS(bs1_)])
                sc.op("pe", lambda e: e.matmul(ps[:, bs2_, 0:n], lhsT=ones[:], rhs=L["sq"](i2, n),
                                               start=(m == 0), stop=(m == 7)),
                      reads=[L["rsq"](i2), "ones"], writes=[rPS(bs2_)])
            pend.append(stats_mm)

        def ln_tile(ti, lane=0):
            c0, n = TILES[ti]
            L = LANE[lane]
            bs1_, bs2_ = L["bs"]
            while pend:
                pend.pop(0)()
            sc.op("dve", lambda e: e.tensor_scalar(out=L["mean"](n), in0=ps[:, bs1_, 0:n], scalar1=1.0 / D,
                                                   scalar2=None, op0=ALU.mult),
                  reads=[rPS(bs1_)], writes=[L["rmean"]])
            sc.op("dve", lambda e: e.tensor_tensor(out=L["msq"](n), in0=L["mean"](n), in1=L["mean"](n), op=ALU.mult),
                  reads=[L["rmean"]], writes=[L["rmsq"]])
            sc.op("dve", lambda e: e.scalar_tensor_tensor(out=L["rstd"](n), in0=ps[:, bs2_, 0:n], scalar=1.0 / D,
                                                          in1=L["msq"](n), op0=ALU.mult, op1=ALU.subtract),
                  reads=[rPS(bs2_), L["rmsq"]], writes=[L["rrstd"]])
            sc.op("act", lambda e: e.activation(out=L["rstd"](n), in_=L["rstd"](n), func=AF.Ln, scale=1.0, bias=EPS),
                  reads=[L["rrstd"]], writes=[L["rrstd"]])
            sc.op("act", lambda e: e.activation(out=L["rstd"](n), in_=L["rstd"](n), func=AF.Exp, scale=-0.5),
                  reads=[L["rrstd"]], writes=[L["rrstd"]])
            for m in range(8):
                sc.op("dve", lambda e, m=m: e.tensor_tensor(out=L["C"](m, n), in0=L["C"](m, n), in1=L["mean"](n),
                                                            op=ALU.subtract),
                      reads=[L["rC"](m, n), L["rmean"]], writes=[L["rC"](m, n)])
                sc.op("dve", lambda e, m=m: e.tensor_tensor(out=L["C"](m, n), in0=L["C"](m, n), in1=L["rstd"](n),
                                                            op=ALU.mult),
                      reads=[L["rC"](m, n), L["rrstd"]], writes=[L["rC"](m, n)])
                sc.op("act", lambda e, m=m: e.activation(out=L["Z"](m, n), in_=L["C"](m, n), func=AF.Silu,
                                                         bias=vec(V_LNB, m), scale=vec(V_LNG, m)),
                      reads=[L["rC"](m, n), "vec"], writes=[L["rZ"](m, n)])

        def pw2_tile(ti, lane=0):
            c0, n = TILES[ti]
            L = LANE[lane]
            for m in range(8):
                s = s_pw2[m // 4]
                b = nb()

                def mm2(e, m=m, s=s, b=b, n=n):
                    ins = None
                    for k in range(8):
                        ins = e.matmul(ps[:, b, 0:n], lhsT=WA[s][:, k, (m % 4) * 128:(m % 4 + 1) * 128], rhs=L["Z"](k, n),
                                       start=(k == 0), stop=(k == 7))
                    return ins
                sc.op("pe", mm2, reads=[rW(s)] + [L["rZ"](k, n) for k in range(8)], writes=[rPS(b)])
                sc.op("dve", lambda e, m=m, b=b, n=n, c0=c0: e.scalar_tensor_tensor(
                    out=X[:, m, c0:c0 + n], in0=ps[:, b, 0:n], scalar=vec(V_BPW2, m), in1=X[:, m, c0:c0 + n],
                    op0=ALU.add, op1=ALU.add), reads=[rPS(b), rX(m, c0, n), "vec"], writes=[rX(m, c0, n)])

        def next_chunk(ti, m, la=3):
            m2 = m + la
            if m2 < 8:
                return (ti, m2)
            if ti + 1 < 4:
                return (ti + 1, m2 - 8)
            return None

        merged = [False]
        dg_build(0, 2)
        dg_build(0, 3)
        dg_build(0, 4)
        for ti in range(4):
            if ti == 3:
                merged[0] = True
                nb_mod[0] = 4
            for m in range(8):
                conv_chunk(ti, m)
                if ti == 3:
                    conv_chunk(4, m, lane=1, db=dg_slot[(3, m)])
                if ti == 0 and m == 1:
                    nc_out()
                nx = next_chunk(ti, m)
                if nx is not None and m < 7 and nx not in dg_slot:
                    dg_build(*nx)
                if m == 2 and ti > 0:
                    pw2_tile(ti - 1)
            nx = next_chunk(ti, 7)
            if nx is not None and nx not in dg_slot:
                dg_build(*nx)
            if ti < 3:
                ln_tile(ti)
        while pend:
            pend.pop(0)()
        ln_tile(3)
        ln_tile(4, lane=1)
        pw2_tile(3)
        pw2_tile(4, lane=1)
        nb_mod[0] = 8
        issue_wload()
        issue_wload()
        issue_wload()

        chk('conv')
        def mlp(gi, after_tile=None, skip_rms=False, last_tiles=None):
            if not skip_rms:
                rmsnorm_to_H(gi, *TILES[0])
                rmsnorm_to_H(gi, *TILES[1])
            for fb in range(4):
                for hb in range(2):
                    s = next_w()
                    for j in range(4):
                        hc = hb * 4 + j
                        for ti in range(5):
                            c0, n = TILES[ti]
                            b = nb()

                            def mm1(e, s=s, j=j, c0=c0, n=n, b=b):
                                ins = None
                                for k in range(8):
                                    ins = e.matmul(ps[:, b, 0:n], lhsT=WA[s][:, k, j * 128:(j + 1) * 128],
                                                   rhs=H[:, k, c0:c0 + n], start=(k == 0), stop=(k == 7))
                                return ins
                            sc.op("pe", mm1, reads=[rW(s)] + [rH(k, c0, n) for k in range(8)], writes=[rPS(b)])
                            i = nsq()
                            sc.op("act", lambda e, b=b, n=n, i=i: e.activation(out=SQR[:, i, 0:n], in_=ps[:, b, 0:n], func=AF.Relu),
                                  reads=[rPS(b)], writes=[rSQ(i)])
                            sc.op(alt_eng("dve", "pool"), lambda e, hc=hc, c0=c0, n=n, i=i: e.tensor_tensor(
                                out=HID[:, hc, c0:c0 + n], in0=SQR[:, i, 0:n], in1=SQR[:, i, 0:n], op=ALU.mult),
                                reads=[rSQ(i)], writes=[rHID(hc, c0, n)])
                            if (not skip_rms) and fb == 0 and hb == 0 and j == 0 and ti + 2 < 5:
                                rmsnorm_to_H(gi, *TILES[ti + 2])
                    issue_wload()
                sA = next_w()
                sB = next_w()
                last = (fb == 3)
                tl = TILES
                if last and last_tiles is not None:
                    tl = last_tiles
                order = [(ti, m) for ti in range(len(tl)) for m in range(8)] if last else \
                    [(ti, m) for m in range(8) for ti in range(5)]
                for (ti, m) in order:
                    c0, n = tl[ti]
                    b = nb()

                    def mm2(e, sA=sA, sB=sB, m=m, c0=c0, n=n, b=b):
                        ins = None
                        for k in range(8):
                            sl = sA if k < 4 else sB
                            ins = e.matmul(ps[:, b, 0:n], lhsT=WB[sl][:, k % 4, m * 128:(m + 1) * 128],
                                           rhs=HID[:, k, c0:c0 + n], start=(k == 0), stop=(k == 7))
                        return ins
                    sc.op("pe", mm2, reads=[rW(sA), rW(sB)] + [rHID(k, c0, n) for k in range(8)], writes=[rPS(b)])
                    sc.op("dve", lambda e, m=m, b=b, c0=c0, n=n: e.tensor_tensor(out=X[:, m, c0:c0 + n], in0=ps[:, b, 0:n],
                                                                                 in1=X[:, m, c0:c0 + n], op=ALU.add),
                          reads=[rPS(b), rX(m, c0, n)], writes=[rX(m, c0, n)])
                    if last and after_tile is not None and m == 7:
                        after_tile(ti)
                issue_wload()
                issue_wload()

        mlp(1)
        chk('mlp0')

        s_pool = next_w()
        PBR = Wreg[0:1, s_pool * 4096 + 2048:s_pool * 4096 + 3072]
        onesrow = Wreg[0:1, s_pool * 4096 + 3072:s_pool * 4096 + 3584]
        sc.op("pool", lambda e: e.memset(onesrow, 1.0), reads=[rW(s_pool)], writes=[("W", s_pool * 8192 + 6144, s_pool * 8192 + 7168)])
        HB = Sreg[:, 0:2 * 8 * 527].rearrange("p (b c n) -> p b c n", b=2, c=8)
        AB = Sreg[:, 8432:8432 + 2 * 6 * 527].rearrange("p (b c n) -> p b c n", b=2, c=6)
        TT1 = Sreg[:, 14756:14756 + 2 * 527].rearrange("p (c n) -> p c n", c=2)
        TT2 = Sreg[:, 15810:15810 + 2 * 527].rearrange("p (c n) -> p c n", c=2)
        WPS = Sreg[:, 16864:16864 + 2048].rearrange("p (g k n) -> p g k n", g=4, k=2)
        WPN = Sreg[:, 18912:18912 + 2048].rearrange("p (g k n) -> p g k n", g=4, k=2)
        P15 = Sreg[:, 20960:20960 + 8 * 16].rearrange("p (c n) -> p c n", c=8)
        rHB = lambda b, c, a=0, n=527: ("S", ((b * 8 + c) * 527 + a) * 2, ((b * 8 + c) * 527 + a + n) * 2)
        rAB = lambda b, c: ("S", (8432 + (b * 6 + c) * 527) * 2, (8432 + (b * 6 + c + 1) * 527) * 2)
        rTT1 = lambda c: ("S", (14756 + c * 527) * 2, (14756 + (c + 1) * 527) * 2)
        rTT2 = lambda c: ("S", (15810 + c * 527) * 2, (15810 + (c + 1) * 527) * 2)
        rWPS = ("S", 16864 * 2, (16864 + 2048) * 2)
        rWPN = ("S", 18912 * 2, (18912 + 2048) * 2)
        rP15 = ("S", 20960 * 2, (20960 + 128) * 2)
        for g, w in enumerate((2, 4, 8, 16)):
            sc.op("dve", lambda e, g=g, w=w: e.tensor_scalar(out=WPS[:, g, :, :], in0=WP[s_pool][:, g, :, :], scalar1=1.0 / w,
                                                             scalar2=None, op0=ALU.mult),
                  reads=[rW(s_pool)], writes=[rWPS])
        sc.op("dve", lambda e: e.tensor_scalar(out=WPN[:, :, :, :], in0=WP[s_pool][:, :, :, :], scalar1=-1.0, scalar2=None,
                                                op0=ALU.mult), reads=[rW(s_pool)], writes=[rWPN])
        RI = 3
        def stageA(ti):
            c0, n = TILES[ti]
            hb_ = ti % 2
            rms_stats(c0, n, ri=RI)
            if ti < 4:
                next_ = 15 + n
                if ti == 0:
                    sc.op("pool", lambda e, hb_=hb_: e.memset(HB[:, hb_, :, 0:15], 0.0), writes=[rHB(hb_, c, 0, 15) for c in range(8)])
                else:
                    sc.op("pool", lambda e, hb_=hb_: e.tensor_copy(out=HB[:, hb_, :, 0:15], in_=HB[:, 1 - hb_, :, 512:527]),
                          reads=[rHB(1 - hb_, c, 512, 15) for c in range(8)], writes=[rHB(hb_, c, 0, 15) for c in range(8)])
                for c in range(8):
                    rms_apply(2, c, c0, n, HB[:, hb_, c, 15:15 + n], [rHB(hb_, c, 15, n)], ri=RI)
            else:
                next_ = 31 + 368
                sc.op("pool", lambda e, hb_=hb_: e.tensor_copy(out=HB[:, hb_, :, 0:15], in_=HB[:, 1 - hb_, :, 512:527]),
                      reads=[rHB(1 - hb_, c, 512, 15) for c in range(8)], writes=[rHB(hb_, c, 0, 15) for c in range(8)])
                for q in range(2):
                    s_ = in_ctr[0] % 2
                    in_ctr[0] += 1
                    sc.dma("sp", [(IN[0:120, s_, :], spool[q * 8:(q + 1) * 8].rearrange("s j d -> (s j) d"))], d_in[s_],
                           writes=[rIN(s_)])
                    for half in range(2):
                        b = nb()

                        def tr(e, half=half, b=b, s_=s_):
                            ins = None
                            for cc in range(4):
                                c = half * 4 + cc
                                ins = e.transpose(out=ps[:, b, cc * 120:(cc + 1) * 120],
                                                  in_=IN[0:120, s_, c * 128:(c + 1) * 128], identity=ident[0:120, 0:120])
                            return ins
                        sc.op("pe", tr, reads=[rIN(s_), "ident"], writes=[rPS(b)])
                        src = ps[:, b, 0:480].rearrange("p (c s j) -> p c s j", c=4, s=8)
                        dst = HB[:, hb_, half * 4:(half + 1) * 4, 31 + q * 184:31 + (q + 1) * 184] \
                            .rearrange("p c (s j) -> p c s j", j=23)[:, :, :, 0:15]
                        copy_op(alt_eng(), dst, src, [rPS(b)], [rHB(hb_, half * 4 + cc, 31 + q * 184, 184) for cc in range(4)])
                for c in range(8):
                    rms_apply(2, c, 2048, 16, HB[:, hb_, c, 15:31], [rHB(hb_, c, 15, 16)], ri=RI)
                    sc.op("dve", lambda e, c=c, hb_=hb_: e.scalar_tensor_tensor(
                        out=HB[:, hb_, c, 31:399].rearrange("p (s j) -> p s j", j=23)[:, :, 15:23],
                        in0=X[:, c, 2064:2192].rearrange("p (s t) -> p s t", t=8), scalar=g32(2, c),
                        in1=ST[:, RI, 16:144].rearrange("p (s t) -> p s t", t=8), op0=ALU.mult, op1=ALU.mult),
                        reads=[rX(c, 2064, 128), rST(RI), ("g32", 2, 3)], writes=[rHB(hb_, c, 31, 368)])
                    sc.op("dve", lambda e, c=c: e.scalar_tensor_tensor(
                        out=VO[:, c, 15:30], in0=X[:, c, 2049:2064], scalar=g32(2, c), in1=ST[:, RI, 1:16],
                        op0=ALU.mult, op1=ALU.mult), reads=[rX(c, 2049, 15), rST(RI), ("g32", 2, 3)], writes=[("ST", 0, 6144)])
                    sc.op("dve", lambda e, c=c: e.scalar_tensor_tensor(
                        out=VO[:, c, 30:158], in0=X[:, c, 2064:2192], scalar=g32(2, c), in1=ST[:, RI, 16:144],
                        op0=ALU.mult, op1=ALU.mult), reads=[rX(c, 2064, 128), rST(RI), ("g32", 2, 3)], writes=[("ST", 0, 6144)])
            for c in range(2, 8):
                eng = "pool" if c % 2 else "dve"
                pe_ = c % 2
                steps = c // 2
                dst1 = AB[:, hb_, c - 2, :] if steps == 1 else TT1[:, pe_, :]
                r1 = rAB(hb_, c - 2) if steps == 1 else rTT1(pe_)
                sc.op(eng, lambda e, c=c, dst1=dst1: e.tensor_tensor(
                    out=dst1[:, 1:next_], in0=HB[:, hb_, c, 1:next_], in1=HB[:, hb_, c, 0:next_ - 1], op=ALU.add),
                    reads=[rHB(hb_, c, 0, next_)], writes=[r1])
                if steps >= 2:
                    dst2 = AB[:, hb_, c - 2, :] if steps == 2 else TT2[:, pe_, :]
                    r2 = rAB(hb_, c - 2) if steps == 2 else rTT2(pe_)
                    sc.op(eng, lambda e, dst2=dst2, pe_=pe_: e.tensor_tensor(
                        out=dst2[:, 3:next_], in0=TT1[:, pe_, 3:next_], in1=TT1[:, pe_, 1:next_ - 2], op=ALU.add),
                        reads=[rTT1(pe_)], writes=[r2])
                if steps == 3:
                    sc.op(eng, lambda e, c=c, pe_=pe_: e.tensor_tensor(
                        out=AB[:, hb_, c - 2, 7:next_], in0=TT2[:, pe_, 7:next_], in1=TT2[:, pe_, 3:next_ - 4], op=ALU.add),
                        reads=[rTT2(pe_)], writes=[rAB(hb_, c - 2)])
            if ti == 0:
                PA = PT[:, 0, 0:240].rearrange("p (c n) -> p c n", c=8)
                PB = PT[:, 1, 0:240].rearrange("p (c n) -> p c n", c=8)
                IV8 = PT[:, 1, 256:384].rearrange("p (c n) -> p c n", c=8)
                rPA = ("PT", 0, 960)
                rPB = ("PT", 2048, 2048 + 960)
                rIV8 = ("PT", 2048 + 1024, 2048 + 1536)
                for c in range(8):
                    sc.op("pool", lambda e, c=c: e.tensor_copy(out=IV8[:, c, :], in_=INVC[:, (c // 2) * 16:(c // 2 + 1) * 16]),
                          reads=[("invc", 0, 64)], writes=[rIV8])
                hb_all = [rHB(hb_, c, 0, 30) for c in range(8)]
                sc.op("dve", lambda e, hb_=hb_: e.tensor_tensor(out=PA[:, :, 1:30], in0=HB[:, hb_, :, 1:30],
                                                               in1=HB[:, hb_, :, 0:29], op=ALU.add),
                      reads=hb_all, writes=[rPA])
                sc.op("dve", lambda e: e.tensor_tensor(out=PB[:, 2:8, 3:30], in0=PA[:, 2:8, 3:30], in1=PA[:, 2:8, 1:28], op=ALU.add),
                      reads=[rPA], writes=[rPB])
                sc.op("dve", lambda e: e.tensor_tensor(out=PA[:, 4:8, 7:30], in0=PB[:, 4:8, 7:30], in1=PB[:, 4:8, 3:26], op=ALU.add),
                      reads=[rPB], writes=[rPA])
                sc.op("dve", lambda e: e.tensor_tensor(out=PB[:, 6:8, 15:30], in0=PA[:, 6:8, 15:30], in1=PA[:, 6:8, 7:22], op=ALU.add),
                      reads=[rPA], writes=[rPB])
                for (buf, rbuf, cA) in ((PA, rPA, 0), (PB, rPB, 2), (PA, rPA, 4), (PB, rPB, 6)):
                    sc.op("dve", lambda e, buf=buf, cA=cA: e.tensor_tensor(out=buf[:, cA:cA + 2, 15:30], in0=buf[:, cA:cA + 2, 15:30],
                                                                           in1=IV8[:, cA:cA + 2, 0:15], op=ALU.mult),
                          reads=[rbuf, rIV8], writes=[rbuf])
                    sc.op("dve", lambda e, buf=buf, cA=cA, hb_=hb_: e.tensor_tensor(out=P15[:, cA:cA + 2, 0:15], in0=buf[:, cA:cA + 2, 15:30],
                                                                                    in1=HB[:, hb_, cA:cA + 2, 15:30], op=ALU.subtract),
                          reads=[rbuf, rHB(hb_, cA, 15, 15), rHB(hb_, cA + 1, 15, 15)], writes=[rP15])

        def stageB(ti):
            c0, n = TILES[ti]
            hb_ = ti % 2
            for m in range(8):
                g = m // 2
                mi = m % 2
                b = nb()
                stride = (1, 2, 4, 8)[g]

                def mmp(e, g=g, mi=mi, b=b, n=n, ti=ti, hb_=hb_, stride=stride, m=m):
                    ins = None
                    if ti < 4:
                        segs = [("flat", 15 if ti == 0 else 0, n, 15 + (15 if ti == 0 else 0))]
                    else:
                        segs = [("flat", 0, 16, 15), ("samp", 16, 128, 0)]
                    for (kind, pc0, cnt_, e0) in segs:
                        pc1 = n if kind == "flat" and ti < 4 else pc0 + cnt_
                        terms = []
                        for ki in range(2):
                            cs = 2 * g + ki
                            for j in range(2):
                                sh = j * stride
                                srcbuf = (HB[:, hb_, cs, :] if g < 1 else AB[:, hb_, cs - 2, :])
                                terms.append((WPS[:, g, ki, mi * 128:(mi + 1) * 128], srcbuf, sh))
                            terms.append((WPN[:, g, ki, mi * 128:(mi + 1) * 128], HB[:, hb_, cs, :], 0))
                        for idx, (lw, srcbuf, sh) in enumerate(terms):
                            if kind == "flat":
                                ncols = pc1 - pc0
                                rhs = srcbuf[:, e0 - sh:e0 - sh + ncols]
                                outp = ps[:, b, pc0:pc1]
                            else:
                                rhs = srcbuf[:, 31:399].rearrange("p (s j) -> p s j", j=23)[:, :, 15 - sh:23 - sh]
                                outp = ps[:, b, 16:144].rearrange("p (s t) -> p s t", t=8)
                            ins = e.matmul(outp, lhsT=lw, rhs=rhs, start=(idx == 0), stop=False)
                        if kind == "flat":
                            ins = e.matmul(ps[:, b, pc0:pc1], lhsT=PBR[0:1, m * 128:(m + 1) * 128], rhs=onesrow[0:1, 0:pc1 - pc0],
                                           start=False, stop=True)
                        else:
                            ins = e.matmul(ps[:, b, 16:144], lhsT=PBR[0:1, m * 128:(m + 1) * 128], rhs=onesrow[0:1, 0:128],
                                           start=False, stop=True)
                    if ti == 0:
                        for ki in range(2):
                            ins = e.matmul(ps[:, b, 0:15], lhsT=WP[s_pool][:, g, ki, mi * 128:(mi + 1) * 128],
                                           rhs=P15[:, 2 * g + ki, 0:15], start=(ki == 0), stop=False)
                        ins = e.matmul(ps[:, b, 0:15], lhsT=PBR[0:1, m * 128:(m + 1) * 128], rhs=onesrow[0:1, 0:15],
                                       start=False, stop=True)
                    return ins
                rd = [rW(s_pool), rWPS, rWPN, rP15] + [rHB(hb_, 2 * g + ki, 0, 527) for ki in range(2)]
                if g >= 1:
                    rd += [rAB(hb_, 2 * g + ki - 2) for ki in range(2)]
                sc.op("pe", mmp, reads=rd, writes=[rPS(b)])
                sc.op("dve", lambda e, m=m, b=b, n=n, c0=c0: e.scalar_tensor_tensor(
                    out=X[:, m, c0:c0 + n], in0=ps[:, b, 0:n], scalar=vec(V_PS, m), in1=X[:, m, c0:c0 + n],
                    op0=ALU.mult, op1=ALU.add), reads=[rPS(b), rX(m, c0, n), "vec"], writes=[rX(m, c0, n)])

        def np_out():
            out_rows(lambda c: VO[:, c, 15:30], 15, IN[:, 0, :], rIN(0), lambda: [(npp[:, :], IN[0:15, 0, :])],
                     d_in[0], "npp", [("ST", 0, 6144)])
            out_rows(lambda c: VO[:, c, 30:158], 128, IN[:, 1, :], rIN(1),
                     lambda: [(nps[s_, 7:15, :], IN[s_ * 8:(s_ + 1) * 8, 1, :]) for s_ in range(16)], d_in[1], "nps_new",
                     [("ST", 0, 6144)])

        stageA(0)
        for ti in range(5):
            if ti + 1 < 5:
                stageA(ti + 1)
            stageB(ti)
            if ti == 3:
                np_out()
            if ti >= 1:
                rmsnorm_to_H(3, *TILES[ti - 1])
        rmsnorm_to_H(3, *TILES[4])
        issue_wload()

        ost_ctr = [0]

        FTL = [(0, 512), (512, 512), (1024, 512), (1536, 256), (1792, 256), (2048, 144)]
        NF = len(FTL)

        def final_S(fi):
            c0, n = FTL[fi]
            yb = fi % 2
            rms_stats(c0, n, ri=1)
            for c in range(8):
                rms_apply(4, c, c0, n, YT[:, yb, c, 0:n], [rYT(yb, c, 0, n)], ri=1)

        def final_Tr(fi):
            c0, n = FTL[fi]
            yb = fi % 2
            blocks = []
            if fi < NF - 1:
                for blk in range(n // 128):
                    cc = c0 + blk * 128
                    if cc == 0:
                        blocks.append((0, 128, 16, yp[0:112, :]))
                    else:
                        blocks.append((blk * 128, 128, 0, yp[cc - 16:cc + 112, :]))
            else:
                blocks.append((0, 16, 0, yp[2032:2048, :]))
                blocks.append((16, 128, 0, ys[:, :]))
            for (o, nc_, skip, dst) in blocks:
                sap, srng, sd = STG[ost_ctr[0] % 4]
                ost_ctr[0] += 1
                out_rows(lambda c, yb=yb, o=o, nc_=nc_: YT[:, yb, c, o:o + nc_], nc_, sap, srng,
                         lambda dst=dst, sap=sap, skip=skip, nc_=nc_: [(dst, sap[skip:nc_, :])], sd, ("y", fi, o),
                         [rYT(yb, c, 0, 512) for c in range(8)])

        def after_last(fi):
            if fi >= 2:
                final_Tr(fi - 2)
            if fi >= 1:
                final_S(fi - 1)
            if fi == NF - 1:
                final_Tr(NF - 2)
                final_S(NF - 1)
                final_Tr(NF - 1)

        chk('pool')
        mlp(3, skip_rms=True, after_tile=after_last, last_tiles=FTL)
        chk('mlp1')

        sc.finish()
        sc.emit()
    return nc


_NC_CACHE = {}
_STOP = None
_NOW = False


def _to_pc(v):
    return np.ascontiguousarray(np.asarray(v, np.float32).reshape(8, 128).T)


def kernel(x_prompt, x_sample, state_conv, state_pool, meta_tokens, norm_mix_g, norm_mlp_g,
           conv_w_pw1, conv_b_pw1, conv_w_dw, conv_b_dw, conv_ln_g, conv_ln_b,
           conv_w_pw2, conv_b_pw2, pool_w, pool_b, pool_scale, mlp_w1, mlp_w2, final_g):
    f = lambda a: np.ascontiguousarray(np.asarray(a, dtype=np.float32))
    x_prompt, x_sample, state_conv, state_pool = f(x_prompt), f(x_sample), f(state_conv), f(state_pool)
    vec_list = [norm_mix_g[0], norm_mix_g[1], norm_mlp_g[0], norm_mlp_g[1], final_g,
                np.asarray(conv_b_pw1)[0, :D], np.asarray(conv_b_pw1)[0, D:], conv_b_dw[0], conv_ln_g[0], conv_ln_b[0],
                conv_b_pw2[0], np.asarray(pool_b)[0].reshape(-1), pool_scale[0]]
    vec_list += [np.asarray(conv_w_dw)[0, k] for k in range(31)]
    vecs = np.ascontiguousarray(np.concatenate([_to_pc(v) for v in vec_list], axis=1))
    shared = {
        "meta": f(meta_tokens), "vecs": vecs, "pbrow": f(np.asarray(pool_b)[0].reshape(1, D)), "w_pw1": f(np.asarray(conv_w_pw1)[0]), "w_pw2": f(np.asarray(conv_w_pw2)[0]),
        "w_pool": f(np.asarray(pool_w)[0]), "w_m1": f(mlp_w1), "w_m2": f(mlp_w2),
    }
    in_maps = []
    for c in range(NCORES):
        m = dict(shared)
        m["xp"] = x_prompt[c]
        m["xs"] = x_sample[16 * c:16 * (c + 1)].reshape(128, D)
        m["sconv"] = state_conv[0, 16 * c:16 * (c + 1)]
        m["spool"] = state_pool[0, 16 * c:16 * (c + 1)]
        in_maps.append(m)
    if "nc" not in _NC_CACHE:
        _NC_CACHE["nc"] = build_program(_STOP)
    nc = _NC_CACHE["nc"]
    res = run_bass_kernel_spmd(nc, in_maps, core_ids=list(range(NCORES)))
    R = res.results
    if _STOP is not None:
        _NC_CACHE["dbg"] = R
    y_prompt = np.stack([R[c]["yp"] for c in range(NCORES)], axis=0)
    y_sample = np.concatenate([R[c]["ys"].reshape(16, 8, D) for c in range(NCORES)], axis=0)
    ncp = np.stack([R[c]["ncp"] for c in range(NCORES)], axis=0)[None]
    ncs = np.concatenate([R[c]["ncs"] for c in range(NCORES)], axis=0)[None]
    npp = np.stack([R[c]["npp"] for c in range(NCORES)], axis=0)[None]
    nps = np.concatenate([R[c]["nps"] for c in range(NCORES)], axis=0)[None]
    return (y_prompt.astype(np.float32), y_sample.astype(np.float32), ncp.astype(np.float32),
            ncs.astype(np.float32), npp.astype(np.float32), nps.astype(np.float32))
```

```python
import numpy as np
from contextlib import ExitStack
import concourse.bass as bass
import concourse.mybir as mybir
from concourse.bass_utils import run_bass_kernel_spmd

F32 = mybir.dt.float32
BF16 = mybir.dt.bfloat16
AF = mybir.ActivationFunctionType
ALU = mybir.AluOpType

ENGS = ("pe", "act", "dve", "pool", "sp")
NCORES = 8
D = 1024
NT = 2192
NP = 2064
EPS = 1e-6
TILES = [(0, 512), (512, 512), (1024, 512), (1536, 512), (2048, 144)]
VW = 2702
VS0 = 2094
NSLOT = 3
DEBUG_DUMP = False
SELF_DIST = 2

V_GMIX0, V_GMIX1, V_GMLP0, V_GMLP1, V_GFIN, V_BA, V_BG, V_BDW, V_LNG, V_LNB, V_BPW2, V_PB, V_PS = range(13)
V_WDW = 13
NV = 13 + 31


class DSem:
    def __init__(self, handle, key):
        self.h = handle
        self.key = key
        self.count = 0


class Sched:
    def __init__(self, nc, stack):
        self.nc = nc
        self.stack = stack
        self.lists = {e: [] for e in ENGS}
        self.sem = {e: stack.enter_context(nc.semaphore("s_" + e)) for e in ENGS if e != "sp"}
        self.count = {e: 0 for e in ENGS}
        self.opidx = {e: 0 for e in ENGS}
        self.waited = {e: {} for e in ENGS}
        self.reg = {}
        self.semh = {e: self.sem[e] for e in self.sem}
        self.ndsem = 0
        self.out_events = []
        self.nwaits = 0
        self.dead = False

    def dsem(self, name=None):
        self.ndsem += 1
        key = "d%d" % self.ndsem
        h = self.stack.enter_context(self.nc.semaphore(name or key))
        self.semh[key] = h
        return DSem(h, key)

    @staticmethod
    def _norm(k):
        if isinstance(k, tuple) and len(k) == 3 and isinstance(k[1], int) and isinstance(k[2], int) \
                and isinstance(k[0], str):
            return k
        return (k, 0, 1)

    def _deps_and_register(self, reads, writes, ev):
        deps = []
        for is_w, lst in ((False, reads), (True, writes)):
            for k in lst:
                rg, lo, hi = self._norm(k)
                ents = self.reg.setdefault(rg, [])
                keep = []
                merged = False
                for e in ents:
                    elo, ehi, ew, eev = e
                    if ehi <= lo or elo >= hi:
                        keep.append(e)
                        continue
                    if eev is ev:
                        if is_w and lo <= elo and ehi <= hi:
                            continue
                        keep.append(e)
                        continue
                    if is_w or ew:
                        deps.append(eev)
                    if is_w and lo <= elo and ehi <= hi:
                        continue
                    if (not is_w) and (not ew) and elo == lo and ehi == hi and eev[1] == ev[1]:
                        e[3] = ev
                        merged = True
                    keep.append(e)
                if not merged:
                    keep.append([lo, hi, is_w, ev])
                self.reg[rg] = keep
        return deps

    def _emit_waits(self, eng, deps, is_dma):
        waits = {}
        for (deng, skey, val, didx, d_is_dma) in deps:
            if deng == eng and not d_is_dma and not is_dma:
                if eng == "pe":
                    continue
                if eng == "pool" and didx < self.opidx[eng] - SELF_DIST:
                    continue
            if self.waited[eng].get(skey, 0) >= val:
                continue
            if waits.get(skey, 0) < val:
                waits[skey] = val
        out = []
        for skey, val in waits.items():
            self.waited[eng][skey] = val
            out.append((self.semh[skey], val))
            self.nwaits += 1
        return out

    def op(self, eng, fn, reads=(), writes=()):
        if self.dead:
            return None
        self.count[eng] += 1
        ev = (eng, eng, self.count[eng], self.opidx[eng], False)
        deps = self._deps_and_register(reads, writes, ev)
        waits = self._emit_waits(eng, deps, False)
        self.opidx[eng] += 1
        self.lists[eng].append((waits, fn, [(self.sem[eng], 1)]))
        return ev

    def dma(self, eng, pairs, dsem, reads=(), writes=(), is_output=False, slow=False):
        if self.dead:
            return None
        dsem.count += 16 * len(pairs)
        ev = ("dma_" + eng, dsem.key, dsem.count, -1, True)
        deps = self._deps_and_register(reads, writes, ev)
        waits = self._emit_waits(eng, deps, True)
        self.opidx[eng] += 1

        def fn(e, pairs=pairs, h=dsem.h, slow=slow):
            for (o, i) in pairs:
                if slow:
                    e.dma_start(out=o, in_=i, allow_slow_non_contiguous=True).then_inc(h, 16)
                else:
                    e.dma_start(out=o, in_=i).then_inc(h, 16)
            return None

        self.lists[eng].append((waits, fn, None))
        if is_output:
            self.out_events.append(ev)
        return ev

    def finish(self):
        waits = {}
        for ev in self.out_events:
            if waits.get(ev[1], 0) < ev[2]:
                waits[ev[1]] = ev[2]
        for e in self.sem:
            if self.count[e] > 0:
                waits[e] = self.count[e]
        for rg, ents in self.reg.items():
            for e in ents:
                ev = e[3]
                if ev[4] and waits.get(ev[1], 0) < ev[2]:
                    waits[ev[1]] = ev[2]
        self.lists["sp"].append(([(self.semh[k], v) for k, v in waits.items()], None, None))

    def emit(self):
        nc = self.nc
        lists = self.lists

        def replay(name, eng):
            for (waits, fn, incs) in lists[name]:
                for (h, v) in waits:
                    eng.wait_ge(h, v)
                if fn is None:
                    continue
                ins = fn(eng)
                if incs:
                    for (h, a) in incs:
                        ins.then_inc(h, a)

        with nc.Block() as block:
            @block.tensor
            def _(e):
                replay("pe", e)

            @block.scalar
            def _(e):
                replay("act", e)

            @block.vector
            def _(e):
                replay("dve", e)

            @block.gpsimd
            def _(e):
                replay("pool", e)

            @block.sync
            def _(e):
                replay("sp", e)


class _Stop(Exception):
    pass


def build_program(stop=None):
    nc = bass.Bass("TRN2", target_bir_lowering=False)
    dt_in = lambda name, shape: nc.dram_tensor(name, shape, F32, kind="ExternalInput").ap()
    dt_out = lambda name, shape: nc.dram_tensor(name, shape, F32, kind="ExternalOutput").ap()
    xp = dt_in("xp", [2048, D])
    xs = dt_in("xs", [128, D])
    sconv = dt_in("sconv", [16, 30, D])
    spool = dt_in("spool", [16, 15, D])
    meta = dt_in("meta", [16, D])
    vecs = dt_in("vecs", [128, NV * 8])
    pbrow = dt_in("pbrow", [1, D])
    w_pw1 = dt_in("w_pw1", [D, 2 * D])
    w_pw2 = dt_in("w_pw2", [D, D])
    w_pool = dt_in("w_pool", [4, 256, 256])
    w_m1 = dt_in("w_m1", [2, D, 4 * D])
    w_m2 = dt_in("w_m2", [2, 4 * D, D])
    yp = dt_out("yp", [2048, D])
    ys = dt_out("ys", [128, D])
    ncp = dt_out("ncp", [30, D])
    ncs = dt_out("ncs", [16, 30, D])
    npp = dt_out("npp", [15, D])
    nps = dt_out("nps", [16, 15, D])
    if stop is not None:
        dbg_x = dt_out("dbg_x", [128, 8 * NT])
        dbg_h = nc.dram_tensor("dbg_h", [128, 8 * NT], BF16, kind="ExternalOutput").ap()
        dbg_s = nc.dram_tensor("dbg_s", [128, 8 * VW], BF16, kind="ExternalOutput").ap()

    with ExitStack() as st:
        sc = Sched(nc, st)
        SB = lambda name, shape, dt: st.enter_context(nc.sbuf_tensor(name, shape, dt))
        Xt = SB("X", [128, 8 * NT], F32)
        Hreg = SB("Hreg", [128, 8 * NT], BF16)
        Sreg = SB("Sreg", [128, 8 * VW], BF16)
        Wreg = SB("Wreg", [128, NSLOT * 4096], BF16)
        ZBt = SB("ZB", [128, 8 * 512], BF16)
        SQRt = SB("SQR", [128, 8 * 512], BF16)
        STt = SB("ST", [128, 4 * 512], F32)
        INt = SB("IN", [128, 2 * 1024], F32)
        PTt = SB("PT", [128, 2 * 512], F32)
        VEC = SB("VEC", [128, NV * 8], F32)
        G32 = SB("G32", [128, 5 * 8], F32)
        PBS = SB("PBS", [128, 8], F32)
        INVC = SB("INVC", [128, 4 * 16], F32)
        ident = SB("ident", [128, 128], F32)
        identb = SB("identb", [128, 128], BF16)
        ones = SB("ones", [128, 128], BF16)
        ps = st.enter_context(nc.psum_tensor("ps", [128, 8, 512], F32))

        X = Xt[:, :].rearrange("p (c n) -> p c n", c=8)
        H = Hreg[:, :].rearrange("p (c n) -> p c n", c=8)
        Cv = Hreg[:, 0:8192].bitcast(F32).rearrange("p (c n) -> p c n", c=8)
        DG = Hreg[:, 8192:8192 + 2 * 31 * 128].rearrange("p (b k n) -> p b k n", b=2, k=31)
        HF = Hreg[:, 0:2 * 8 * 527 * 2].bitcast(F32).rearrange("p (b c n) -> p b c n", b=2, c=8)
        YT = Hreg[:, 0:2 * 8 * 512 * 2].bitcast(F32).rearrange("p (b c n) -> p b c n", b=2, c=8)
        VE = Sreg[:, :].rearrange("p (c n) -> p c n", c=8)
        HID = Sreg[:, 0:8 * NT].rearrange("p (c n) -> p c n", c=8)
        OST = Sreg[:, 0:10 * 2048].bitcast(F32).rearrange("p (s n) -> p s n", s=10)
        WA = [Wreg[:, s * 4096:(s + 1) * 4096].rearrange("p (k m) -> p k m", k=8) for s in range(NSLOT)]
        WB = [Wreg[:, s * 4096:(s + 1) * 4096].rearrange("p (k m) -> p k m", k=4) for s in range(NSLOT)]
        WP = [Wreg[:, s * 4096:s * 4096 + 2048].rearrange("p (g k n) -> p g k n", g=4, k=2) for s in range(NSLOT)]
        ZB = ZBt[:, :].rearrange("p (c n) -> p c n", c=8)
        SQR = SQRt[:, :].rearrange("p (c n) -> p c n", c=8)
        ST = STt[:, :].rearrange("p (c n) -> p c n", c=4)
        VO = STt[:, 0:8 * 158].rearrange("p (c n) -> p c n", c=8)
        IN = INt[:, :].rearrange("p (s n) -> p s n", s=2)
        ZBF = ZBt[:, :].bitcast(F32).rearrange("p (s n) -> p s n", s=2)
        PT = PTt[:, :].rearrange("p (s n) -> p s n", s=2)
        vec = lambda v, c: VEC[:, v * 8 + c:v * 8 + c + 1]
        g32 = lambda i, c: G32[:, i * 8 + c:i * 8 + c + 1]

        rX = lambda c, a, n: ("X", (c * NT + a) * 4, (c * NT + a + n) * 4)
        rH = lambda c, a, n: ("H", (c * NT + a) * 2, (c * NT + a + n) * 2)
        rC = lambda c, a=0, n=512: ("H", (c * 512 + a) * 4, (c * 512 + a + n) * 4)
        rDG = lambda b, k0=0, k1=31: ("H", 16384 + (b * 31 + k0) * 256, 16384 + (b * 31 + k1) * 256)
        rHF = lambda b, c, a=0, n=527: ("H", ((b * 8 + c) * 527 + a) * 4, ((b * 8 + c) * 527 + a + n) * 4)
        rYT = lambda b, c, a=0, n=512: ("H", ((b * 8 + c) * 512 + a) * 4, ((b * 8 + c) * 512 + a + n) * 4)
        rVE = lambda c, a, n: ("S", (c * VW + a) * 2, (c * VW + a + n) * 2)
        rHID = lambda c, a, n: ("S", (c * NT + a) * 2, (c * NT + a + n) * 2)
        rOST = lambda s: ("S", s * 4096, (s + 1) * 4096)
        rW = lambda s: ("W", s * 8192, (s + 1) * 8192)
        rZ = lambda c, a=0, n=512: ("ZB", (c * 512 + a) * 2, (c * 512 + a + n) * 2)
        rSQ = lambda i: ("SQR", i * 1024, (i + 1) * 1024)
        rST = lambda i: ("ST", i * 2048, (i + 1) * 2048)
        rSTall = ("ST", 0, 8192)
        rIN = lambda s: ("IN", s * 4096, (s + 1) * 4096)
        rPT = lambda s: ("PT", s * 2048, (s + 1) * 2048)
        rPS = lambda b: ("PS", b, b + 1)

        def chk(stage):
            if stop == stage and not sc.dead:
                dd = sc.dsem("d_dbg")
                sc.dma("sp", [(dbg_x, Xt[:, :])], dd, reads=[("X", 0, 8 * NT * 4)], writes=["dbg_x"], is_output=True)
                sc.dma("sp", [(dbg_h, Hreg[:, :])], dd, reads=[("H", 0, 8 * NT * 2)], writes=["dbg_h"], is_output=True)
                sc.dma("sp", [(dbg_s, Sreg[:, :])], dd, reads=[("S", 0, 8 * VW * 2)], writes=["dbg_s"], is_output=True)
                sc.dead = True

        bank_ctr = [0]

        nb_mod = [8]

        def nb():
            b = bank_ctr[0] % nb_mod[0]
            bank_ctr[0] += 1
            return b

        sq_ctr = [0]

        def nsq():
            i = sq_ctr[0] % 8
            sq_ctr[0] += 1
            return i

        alt = [0]

        def alt_eng(a="act", b="dve"):
            alt[0] += 1
            return a if alt[0] % 2 else b

        def copy_op(eng, out, in_, reads, writes):
            if eng == "act":
                sc.op("act", lambda e: e.activation(out=out, in_=in_, func=AF.Copy), reads=reads, writes=writes)
            else:
                sc.op(eng, lambda e: e.tensor_copy(out=out, in_=in_), reads=reads, writes=writes)

        d_const = sc.dsem("d_const")
        d_in = [sc.dsem("d_in0"), sc.dsem("d_in1"), sc.dsem("d_in2"), sc.dsem("d_in3")]
        STG = [(IN[:, 0, :], ("IN", 0, 4096), d_in[0]), (IN[:, 1, :], ("IN", 4096, 8192), d_in[1]),
               (ZBF[:, 0, :], ("ZB", 0, 4096), d_in[2]), (ZBF[:, 1, :], ("ZB", 4096, 8192), d_in[3])]
        d_w = [sc.dsem("d_w%d" % s) for s in range(NSLOT)]
        d_ost = [sc.dsem("d_ost%d" % s) for s in range(10)]
        d_misc = sc.dsem("d_misc")
        d_pt = sc.dsem("d_pt")

        sc.op("pool", lambda e: e.memset(ident[:], 0.0), writes=["ident"])
        sc.op("pool", lambda e: e.affine_select(out=ident[:], in_=ident[:], compare_op=ALU.not_equal, fill=1.0,
                                                base=0, pattern=[[-1, 128]], channel_multiplier=1),
              reads=["ident"], writes=["ident"])
        sc.op("pool", lambda e: e.memset(ones[:], 1.0), writes=["ones"])
        sc.op("pool", lambda e: e.tensor_copy(out=identb[:], in_=ident[:]), reads=["ident"], writes=["identb"])

        for g, w in enumerate((2, 4, 8, 16)):
            sc.op("pool", lambda e, g=g, w=w: e.memset(INVC[:, g * 16:(g + 1) * 16], 1.0 / w),
                  writes=[("invc", g * 16, (g + 1) * 16)])
            for t in range(w - 1):
                sc.op("pool", lambda e, g=g, t=t: e.memset(INVC[:, g * 16 + t:g * 16 + t + 1], 1.0 / (t + 1)),
                      writes=[("invc", g * 16 + t, g * 16 + t + 1)])
        sc.dma("sp", [(VEC[:], vecs)], d_const, writes=["vec"])
        sc.op("pool", lambda e: e.memset(VE[:, :, 0:30], 0.0), writes=[rVE(c, 0, 30) for c in range(8)])
        for i, v in enumerate((V_GMIX0, V_GMLP0, V_GMIX1, V_GMLP1, V_GFIN)):
            sc.op("dve", lambda e, i=i, v=v: e.tensor_scalar(out=G32[:, i * 8:(i + 1) * 8], in0=VEC[:, v * 8:(v + 1) * 8],
                                                             scalar1=32.0, scalar2=None, op0=ALU.mult),
                  reads=["vec"], writes=[("g32", i, i + 1)])
        sc.op("dve", lambda e: e.tensor_tensor(out=PBS[:], in0=VEC[:, V_PB * 8:(V_PB + 1) * 8],
                                               in1=VEC[:, V_PS * 8:(V_PS + 1) * 8], op=ALU.mult),
              reads=["vec"], writes=["pbs"])

        chk('c0')
        wloads = []
        for hb in range(4):
            wloads.append(("A", [(lambda s, hb=hb: WA[s][:, :, 0:256],
                                  w_pw1[:, 256 * hb:256 * hb + 256].rearrange("(k p) m -> p k m", p=128)),
                                 (lambda s, hb=hb: WA[s][:, :, 256:512],
                                  w_pw1[:, D + 256 * hb:D + 256 * hb + 256].rearrange("(k p) m -> p k m", p=128))]))
        for hb in range(2):
            wloads.append(("A", [(lambda s: WA[s][:, :, :],
                                  w_pw2[:, 512 * hb:512 * hb + 512].rearrange("(k p) m -> p k m", p=128))]))

        def mlp_loads(layer):
            for fb in range(4):
                for hb in range(2):
                    c0 = fb * 1024 + hb * 512
                    wloads.append(("A", [(lambda s: WA[s][:, :, :],
                                          w_m1[layer, :, c0:c0 + 512].rearrange("(k p) m -> p k m", p=128))]))
                for hb in range(2):
                    r0 = fb * 1024 + hb * 512
                    wloads.append(("B", [(lambda s: WB[s][:, :, :],
                                          w_m2[layer, r0:r0 + 512, :].rearrange("(k p) m -> p k m", p=128))]))
        mlp_loads(0)
        wloads.append(("P", [(lambda s: WP[s][:, :, :, :], w_pool.rearrange("g (k p) n -> p g k n", p=128)),
                             (lambda s: Wreg[0:1, s * 4096 + 2048:s * 4096 + 3072], pbrow)]))
        mlp_loads(1)
        wnext = [0]

        def issue_wload():
            i = wnext[0]
            if i >= len(wloads):
                return
            wnext[0] += 1
            s = i % NSLOT
            pairs = [(vf(s), ap) for (vf, ap) in wloads[i][1]]
            sc.dma("pool", pairs, d_w[s], writes=[rW(s)])

        wuse = [0]

        def next_w():
            i = wuse[0]
            wuse[0] += 1
            assert sc.dead or i < wnext[0], "weight load not issued"
            return i % NSLOT

        for _ in range(NSLOT):
            if not _NOW:
                issue_wload()

        chk('cw')
        in_ctr = [0]

        def load_rows(src_ap, nrows, dst_fn, dst_ranges_fn):
            sap, srng, sd = STG[in_ctr[0] % 4]
            in_ctr[0] += 1
            sc.dma("sp", [(sap[0:nrows, :], src_ap)], sd, writes=[srng])
            for half in range(2):
                b = nb()

                def tr(e, half=half, b=b, sap=sap):
                    ins = None
                    for cc in range(4):
                        c = half * 4 + cc
                        ins = e.transpose(out=ps[:, b, cc * nrows:(cc + 1) * nrows],
                                          in_=sap[0:nrows, c * 128:(c + 1) * 128],
                                          identity=ident[0:nrows, 0:nrows])
                    return ins
                sc.op("pe", tr, reads=[srng, "ident"], writes=[rPS(b)])
                src = ps[:, b, 0:4 * nrows].rearrange("p (c n) -> p c n", c=4)
                copy_op(alt_eng(), dst_fn(half), src, [rPS(b)], dst_ranges_fn(half))

        def load_x_cols(col0, ncols, src_ap):
            load_rows(src_ap, ncols,
                      lambda half: X[:, half * 4:(half + 1) * 4, col0:col0 + ncols],
                      lambda half: [rX(half * 4 + cc, col0, ncols) for cc in range(4)])

        def rms_stats(c0, n, ri=0):
            b = nb()
            for c in range(8):
                i = nsq()
                sc.op("act", lambda e, c=c, i=i: e.activation(out=SQR[:, i, 0:n], in_=X[:, c, c0:c0 + n], func=AF.Square),
                      reads=[rX(c, c0, n)], writes=[rSQ(i)])
                sc.op("pe", lambda e, c=c, i=i, b=b: e.matmul(ps[:, b, 0:n], lhsT=ones[:], rhs=SQR[:, i, 0:n],
                                                             start=(c == 0), stop=(c == 7)),
                      reads=[rSQ(i), "ones"], writes=[rPS(b)])
            sc.op("act", lambda e, b=b: e.activation(out=ST[:, ri, 0:n], in_=ps[:, b, 0:n], func=AF.Ln,
                                                     scale=1.0, bias=D * EPS),
                  reads=[rPS(b)], writes=[rST(ri)])
            sc.op("act", lambda e: e.activation(out=ST[:, ri, 0:n], in_=ST[:, ri, 0:n], func=AF.Exp, scale=-0.5),
                  reads=[rST(ri)], writes=[rST(ri)])

        def rms_apply(gi, c, c0, n, out_ap, out_rng, eng="dve", ri=0):
            sc.op(eng, lambda e: e.scalar_tensor_tensor(out=out_ap, in0=X[:, c, c0:c0 + n], scalar=g32(gi, c),
                                                        in1=ST[:, ri, 0:n], op0=ALU.mult, op1=ALU.mult),
                  reads=[rX(c, c0, n), rST(ri), ("g32", gi, gi + 1)], writes=out_rng)

        def rmsnorm_to_H(gi, c0, n):
            rms_stats(c0, n)
            for c in range(8):
                rms_apply(gi, c, c0, n, H[:, c, c0:c0 + n], [rH(c, c0, n)])

        chk('c2')

        def load_tile(ti):
            c0, n = TILES[ti]
            if ti == 0:
                load_x_cols(0, 16, meta)
                load_x_cols(16, 112, xp[0:112, :])
                for j in range(3):
                    load_x_cols(128 + 128 * j, 128, xp[112 + 128 * j:240 + 128 * j, :])
            elif ti < 4:
                for j in range(4):
                    cc0 = c0 + 128 * j
                    load_x_cols(cc0, 128, xp[cc0 - 16:cc0 + 112, :])
            else:
                load_x_cols(2048, 16, xp[2032:2048, :])
                load_x_cols(2064, 128, xs)

        def pw1_step(hb, s, ti):
            c0, n = TILES[ti]
            for mi in range(2):
                m = hb * 2 + mi
                bg = nb()
                ba = nb()

                def mm_g(e, s=s, mi=mi, c0=c0, n=n, bg=bg):
                    ins = None
                    for k in range(8):
                        ins = e.matmul(ps[:, bg, 0:n], lhsT=WA[s][:, k, 256 + mi * 128:256 + (mi + 1) * 128],
                                       rhs=H[:, k, c0:c0 + n], start=(k == 0), stop=(k == 7))
                    return ins
                sc.op("pe", mm_g, reads=[rW(s)] + [rH(k, c0, n) for k in range(8)], writes=[rPS(bg)])

                def mm_a(e, s=s, mi=mi, c0=c0, n=n, ba=ba):
                    ins = None
                    for k in range(8):
                        ins = e.matmul(ps[:, ba, 0:n], lhsT=WA[s][:, k, mi * 128:(mi + 1) * 128],
                                       rhs=H[:, k, c0:c0 + n], start=(k == 0), stop=(k == 7))
                    return ins
                sc.op("pe", mm_a, reads=[rW(s)] + [rH(k, c0, n) for k in range(8)], writes=[rPS(ba)])
                sg = 2 + (m + ti) % 2
                sc.op("act", lambda e, bg=bg, n=n, m=m, sg=sg: e.activation(out=ST[:, sg, 0:n], in_=ps[:, bg, 0:n],
                                                                           func=AF.Sigmoid, bias=vec(V_BG, m), scale=1.0),
                      reads=[rPS(bg), "vec"], writes=[rST(sg)])
                if ti < 4:
                    sc.op("dve", lambda e, ba=ba, n=n, m=m, sg=sg, c0=c0: e.scalar_tensor_tensor(
                        out=VE[:, m, 30 + c0:30 + c0 + n], in0=ps[:, ba, 0:n], scalar=vec(V_BA, m), in1=ST[:, sg, 0:n],
                        op0=ALU.add, op1=ALU.mult), reads=[rPS(ba), rST(sg), "vec"], writes=[rVE(m, 30 + c0, n)])
                else:
                    sc.op("dve", lambda e, ba=ba, m=m, sg=sg: e.scalar_tensor_tensor(
                        out=VE[:, m, 30 + 2048:30 + 2064], in0=ps[:, ba, 0:16], scalar=vec(V_BA, m), in1=ST[:, sg, 0:16],
                        op0=ALU.add, op1=ALU.mult), reads=[rPS(ba), rST(sg), "vec"], writes=[rVE(m, 30 + 2048, 16)])
                    sc.op("dve", lambda e, ba=ba, m=m, sg=sg: e.scalar_tensor_tensor(
                        out=VE[:, m, VS0:VS0 + 608].rearrange("p (s j) -> p s j", j=38)[:, :, 30:38],
                        in0=ps[:, ba, 16:144].rearrange("p (s t) -> p s t", t=8), scalar=vec(V_BA, m),
                        in1=ST[:, sg, 16:144].rearrange("p (s t) -> p s t", t=8),
                        op0=ALU.add, op1=ALU.mult), reads=[rPS(ba), rST(sg), "vec"], writes=[rVE(m, VS0, 608)])

        def load_states():
            for q in range(4):
                def dstf(half, q=q):
                    return VE[:, half * 4:(half + 1) * 4, VS0 + q * 4 * 38:VS0 + (q + 1) * 4 * 38] \
                        .rearrange("p c (s j) -> p c s j", j=38)[:, :, :, 0:30]
                s_ = in_ctr[0] % 2
                in_ctr[0] += 1
                sc.dma("sp", [(IN[0:120, s_, :], sconv[q * 4:(q + 1) * 4].rearrange("s j d -> (s j) d"))], d_in[s_],
                       writes=[rIN(s_)])
                for half in range(2):
                    b = nb()

                    def tr(e, half=half, b=b, s_=s_):
                        ins = None
                        for cc in range(4):
                            c = half * 4 + cc
                            ins = e.transpose(out=ps[:, b, cc * 120:(cc + 1) * 120], in_=IN[0:120, s_, c * 128:(c + 1) * 128],
                                              identity=ident[0:120, 0:120])
                        return ins
                    sc.op("pe", tr, reads=[rIN(s_), "ident"], writes=[rPS(b)])
                    src = ps[:, b, 0:480].rearrange("p (c s j) -> p c s j", c=4, s=4)
                    copy_op(alt_eng(), dstf(half), src, [rPS(b)],
                            [rVE(half * 4 + cc, VS0 + q * 152, 152) for cc in range(4)])

            chk('c1')
            sc.dma("sp", [(ncs[:, 0:22, :], sconv[:, 8:30, :])], d_pt, writes=["ncs_old"], is_output=True)
            sc.dma("sp", [(nps[:, 0:7, :], spool[:, 8:15, :])], d_pt, writes=["nps_old"], is_output=True)


        dg_slot = {}
        DG2 = Wreg[:, 0:31 * 128].rearrange("p (k n) -> p k n", k=31)

        DG3 = INt[:, :].bitcast(BF16)[:, 0:31 * 128].rearrange("p (k n) -> p k n", k=31)
        DG4 = ZBt[:, 0:31 * 128].rearrange("p (k n) -> p k n", k=31)

        def DGv(db, k):
            if db < 2:
                return DG[:, db, k, :]
            return (DG2, DG3, DG4)[db - 2][:, k, :]

        def rDGx(db, k0=0, k1=31):
            if db < 2:
                return rDG(db, k0, k1)
            return (("W", "IN", "ZB")[db - 2], k0 * 256, k1 * 256)

        dg_ctr = [0]

        def dg_build(ti, m, db=None):
            if db is None:
                db = dg_ctr[0] % 3
                dg_ctr[0] += 1
            dg_slot[(ti, m)] = db
            for k in range(31):
                if k % 3 == 0:
                    sc.op("dve", lambda e, k=k, m=m, db=db: e.tensor_scalar(out=DGv(db, k), in0=identb[:],
                                                                          scalar1=vec(V_WDW + k, m), scalar2=None, op0=ALU.mult),
                          reads=["identb", "vec"], writes=[rDGx(db, k, k + 1)])
                elif k % 3 == 1:
                    sc.op("pool", lambda e, k=k, m=m, db=db: e.tensor_tensor(
                        out=DGv(db, k), in0=identb[:], in1=vec(V_WDW + k, m).to_broadcast([128, 128]), op=ALU.mult),
                        reads=["identb", "vec"], writes=[rDGx(db, k, k + 1)])
                else:
                    sc.op("act", lambda e, k=k, m=m, db=db: e.activation(out=DGv(db, k), in_=identb[:], func=AF.Identity,
                                                                         scale=vec(V_WDW + k, m)),
                          reads=["identb", "vec"], writes=[rDGx(db, k, k + 1)])

        pw1_slots = [next_w(), next_w(), next_w()]
        load_tile(0)
        for step in range(8):
            if step + 1 < 5:
                load_tile(step + 1)
            if step < 5:
                rmsnorm_to_H(0, *TILES[step])
            if step == 2:
                load_states()
            for hb in range(3):
                ti = step - 1 - hb
                if 0 <= ti < 5:
                    pw1_step(hb, pw1_slots[hb], ti)
                    if ti == 4:
                        issue_wload()
        chk('load')
        dg_build(0, 0, db=3)
        dg_build(0, 1, db=4)
        s3 = next_w()
        for ti in range(5):
            pw1_step(3, s3, ti)

        chk('pw1')
        for c in range(8):
            sc.op("dve", lambda e, c=c: e.tensor_copy(out=VO[:, c, 0:30], in_=VE[:, c, 30 + 2034:30 + 2064]),
                  reads=[rVE(c, 30 + 2034, 30)], writes=[rSTall])
            sc.op("dve", lambda e, c=c: e.tensor_copy(
                out=VO[:, c, 30:158].rearrange("p (s t) -> p s t", t=8),
                in_=VE[:, c, VS0:VS0 + 608].rearrange("p (s j) -> p s j", j=38)[:, :, 30:38]),
                reads=[rVE(c, VS0, 608)], writes=[rSTall])

        def out_rows(src_fn, nrows, stg_slot_ap, stg_rng, dma_pairs_fn, dsem, name, src_reads):
            for half in range(2):
                b = nb()

                def tr(e, half=half, b=b):
                    ins = None
                    for cc in range(4):
                        c = half * 4 + cc
                        ins = e.transpose(out=ps[0:nrows, b, cc * 128:(cc + 1) * 128], in_=src_fn(c), identity=ident[:])
                    return ins
                sc.op("pe", tr, reads=list(src_reads) + ["ident"], writes=[rPS(b)])
                copy_op(alt_eng(), stg_slot_ap[0:nrows, half * 512:(half + 1) * 512], ps[0:nrows, b, :], [rPS(b)], [stg_rng])
            sc.dma("sp", dma_pairs_fn(), dsem, reads=[stg_rng], writes=[name], is_output=True)

        def nc_out():
            out_rows(lambda c: VO[:, c, 0:30], 30, IN[:, 0, :], rIN(0), lambda: [(ncp[:, :], IN[0:30, 0, :])], d_in[0], "ncp",
                     [rSTall])
            out_rows(lambda c: VO[:, c, 30:158], 128, IN[:, 1, :], rIN(1),
                     lambda: [(ncs[s_, 22:30, :], IN[s_ * 8:(s_ + 1) * 8, 1, :]) for s_ in range(16)], d_in[1], "ncs_new",
                     [rSTall])

        chk('ncout')
        s_pw2 = [next_w(), next_w()]
        nb_mod[0] = 6
        bs1 = 6
        bs2 = 7
        pend = []
        STATS_LAG = 3

        C4 = INt[:, 0:8 * 144].rearrange("p (c n) -> p c n", c=8)
        Z4 = INt[:, 1152:1152 + 576].bitcast(BF16).rearrange("p (c n) -> p c n", c=8)
        SQ4 = PTt[:, :].bitcast(BF16).rearrange("p (c n) -> p c n", c=8)
        rC4 = lambda m: ("IN", m * 576, (m + 1) * 576)
        rZ4 = lambda m: ("IN", 4608 + m * 288, 4608 + (m + 1) * 288)
        rSQ4 = lambda i: ("PT", i * 512, (i + 1) * 512)
        sq4_ctr = [0]

        def nsq4():
            i = sq4_ctr[0] % 8
            sq4_ctr[0] += 1
            return i

        LANE = {
            0: dict(C=lambda m, n: Cv[:, m, 0:n], rC=lambda m, n: rC(m, 0, n), Z=lambda m, n: ZB[:, m, 0:n],
                    rZ=lambda m, n: rZ(m, 0, n), sq=lambda i, n: SQR[:, i, 0:n], rsq=rSQ, nsq=nsq, bs=(6, 7),
                    mean=lambda n: ST[:, 0, 0:n], rstd=lambda n: ST[:, 1, 0:n], msq=lambda n: ST[:, 2, 0:n],
                    rmean=rST(0), rrstd=rST(1), rmsq=rST(2)),
            1: dict(C=lambda m, n: C4[:, m, 0:n], rC=lambda m, n: rC4(m), Z=lambda m, n: Z4[:, m, 0:n],
                    rZ=lambda m, n: rZ4(m), sq=lambda i, n: SQ4[:, i, 0:n], rsq=rSQ4, nsq=nsq4, bs=(4, 5),
                    mean=lambda n: ST[:, 3, 0:n], rstd=lambda n: ST[:, 3, 160:160 + n], msq=lambda n: ST[:, 3, 320:320 + n],
                    rmean=("ST", 6144, 6144 + 640), rrstd=("ST", 6144 + 640, 6144 + 1280), rmsq=("ST", 6144 + 1280, 6144 + 1920)),
        }

        def conv_chunk(ti, m, lane=0, db=None):
            c0, n = TILES[ti]
            L = LANE[lane]
            if db is None:
                db = dg_slot[(ti, m)]
            b = nb()
            if ti < 4:
                def conv(e, m=m, db=db, b=b, c0=c0, n=n):
                    ins = None
                    for k in range(31):
                        ins = e.matmul(ps[:, b, 0:n], lhsT=DGv(db, k), rhs=VE[:, m, c0 + k:c0 + k + n],
                                       start=(k == 0), stop=(k == 30))
                    return ins
                sc.op("pe", conv, reads=[rDGx(db), rVE(m, c0, n + 30)], writes=[rPS(b)])
            else:
                def conv(e, m=m, db=db, b=b):
                    ins = None
                    for k in range(31):
                        ins = e.matmul(ps[:, b, 0:16], lhsT=DGv(db, k), rhs=VE[:, m, 2048 + k:2048 + k + 16],
                                       start=(k == 0), stop=(k == 30))
                    for k in range(31):
                        ins = e.matmul(ps[:, b, 16:144].rearrange("p (s t) -> p s t", t=8), lhsT=DGv(db, k),
                                       rhs=VE[:, m, VS0:VS0 + 608].rearrange("p (s j) -> p s j", j=38)[:, :, k:k + 8],
                                       start=(k == 0), stop=(k == 30))
                    return ins
                sc.op("pe", conv, reads=[rDGx(db), rVE(m, 2048, 46), rVE(m, VS0, 608)], writes=[rPS(b)])
            while len(pend) >= STATS_LAG * (2 if merged[0] else 1):
                pend.pop(0)()
            i1 = L["nsq"]()
            i2 = L["nsq"]()
            bs1_, bs2_ = L["bs"]
            sc.op("act", lambda e: e.activation(out=L["C"](m, n), in_=ps[:, b, 0:n], func=AF.Identity,
                                                bias=vec(V_BDW, m), scale=1.0),
                  reads=[rPS(b), "vec"], writes=[L["rC"](m, n)])
            sc.op("act", lambda e: e.activation(out=L["sq"](i1, n), in_=ps[:, b, 0:n],
                                                func=AF.Identity, bias=vec(V_BDW, m), scale=1.0),
                  reads=[rPS(b), "vec"], writes=[L["rsq"](i1)])
            sc.op("act", lambda e: e.activation(out=L["sq"](i2, n), in_=ps[:, b, 0:n],
                                                func=AF.Square, bias=vec(V_BDW, m), scale=1.0),
                  reads=[rPS(b), "vec"], writes=[L["rsq"](i2)])

            def stats_mm():
                sc.op("pe", lambda e: e.matmul(ps[:, bs1_, 0:n], lhsT=ones[:], rhs=L["sq"](i1, n),
                                               start=(m == 0), stop=(m == 7)),
                      reads=[L["rsq"](i1), "ones"], writes=[rPS(bs1_)])
                sc.op("pe", lambda e: e.matmul(ps[:, bs2_, 0:n], lhsT=ones[:], rhs=L["sq"](i2, n),
                                               start=(m == 0), stop=(m == 7)),
                      reads=[L["rsq"](i2), "ones"], writes=[rPS(bs2_)])
            pend.append(stats_mm)

        def ln_tile(ti, lane=0):
            c0, n = TILES[ti]
            L = LANE[lane]
            bs1_, bs2_ = L["bs"]
            while pend:
                pend.pop(0)()
            sc.op("dve", lambda e: e.tensor_scalar(out=L["mean"](n), in0=ps[:, bs1_, 0:n], scalar1=1.0 / D,
                                                   scalar2=None, op0=ALU.mult),
                  reads=[rPS(bs1_)], writes=[L["rmean"]])
            sc.op("dve", lambda e: e.tensor_tensor(out=L["msq"](n), in0=L["mean"](n), in1=L["mean"](n), op=ALU.mult),
                  reads=[L["rmean"]], writes=[L["rmsq"]])
            sc.op("dve", lambda e: e.scalar_tensor_tensor(out=L["rstd"](n), in0=ps[:, bs2_, 0:n], scalar=1.0 / D,
                                                          in1=L["msq"](n), op0=ALU.mult, op1=ALU.subtract),
                  reads=[rPS(bs2_), L["rmsq"]], writes=[L["rrstd"]])
            sc.op("act", lambda e: e.activation(out=L["rstd"](n), in_=L["rstd"](n), func=AF.Ln, scale=1.0, bias=EPS),
                  reads=[L["rrstd"]], writes=[L["rrstd"]])
            sc.op("act", lambda e: e.activation(out=L["rstd"](n), in_=L["rstd"](n), func=AF.Exp, scale=-0.5),
                  reads=[L["rrstd"]], writes=[L["rrstd"]])
            for m in range(8):
                sc.op("dve", lambda e, m=m: e.tensor_tensor(out=L["C"](m, n), in0=L["C"](m, n), in1=L["mean"](n),
                                                            op=ALU.subtract),
                      reads=[L["rC"](m, n), L["rmean"]], writes=[L["rC"](m, n)])
                sc.op("dve", lambda e, m=m: e.tensor_tensor(out=L["C"](m, n), in0=L["C"](m, n), in1=L["rstd"](n),
                                                            op=ALU.mult),
                      reads=[L["rC"](m, n), L["rrstd"]], writes=[L["rC"](m, n)])
                sc.op("act", lambda e, m=m: e.activation(out=L["Z"](m, n), in_=L["C"](m, n), func=AF.Silu,
                                                         bias=vec(V_LNB, m), scale=vec(V_LNG, m)),
                      reads=[L["rC"](m, n), "vec"], writes=[L["rZ"](m, n)])

        def pw2_tile(ti, lane=0):
            c0, n = TILES[ti]
            L = LANE[lane]
            for m in range(8):
                s = s_pw2[m // 4]
                b = nb()

                def mm2(e, m=m, s=s, b=b, n=n):
                    ins = None
                    for k in range(8):
                        ins = e.matmul(ps[:, b, 0:n], lhsT=WA[s][:, k, (m % 4) * 128:(m % 4 + 1) * 128], rhs=L["Z"](k, n),
                                       start=(k == 0), stop=(k == 7))
                    return ins
                sc.op("pe", mm2, reads=[rW(s)] + [L["rZ"](k, n) for k in range(8)], writes=[rPS(b)])
                sc.op("dve", lambda e, m=m, b=b, n=n, c0=c0: e.scalar_tensor_tensor(
                    out=X[:, m, c0:c0 + n], in0=ps[:, b, 0:n], scalar=vec(V_BPW2, m), in1=X[:, m, c0:c0 + n],
                    op0=ALU.add, op1=ALU.add), reads=[rPS(b), rX(m, c0, n), "vec"], writes=[rX(m, c0, n)])

        def next_chunk(ti, m, la=3):
            m2 = m + la
            if m2 < 8:
                return (ti, m2)
            if ti + 1 < 4:
                return (ti + 1, m2 - 8)
            return None

        merged = [False]
        dg_build(0, 2)
        dg_build(0, 3)
        dg_build(0, 4)
        for ti in range(4):
            if ti == 3:
                merged[0] = True
                nb_mod[0] = 4
            for m in range(8):
                conv_chunk(ti, m)
                if ti == 3:
                    conv_chunk(4, m, lane=1, db=dg_slot[(3, m)])
                if ti == 0 and m == 1:
                    nc_out()
                nx = next_chunk(ti, m)
                if nx is not None and m < 7 and nx not in dg_slot:
                    dg_build(*nx)
                if m == 2 and ti > 0:
                    pw2_tile(ti - 1)
            nx = next_chunk(ti, 7)
            if nx is not None and nx not in dg_slot:
                dg_build(*nx)
            if ti < 3:
                ln_tile(ti)
        while pend:
            pend.pop(0)()
        ln_tile(3)
        ln_tile(4, lane=1)
        pw2_tile(3)
        pw2_tile(4, lane=1)
        nb_mod[0] = 8
        issue_wload()
        issue_wload()
        issue_wload()

        chk('conv')
        def mlp(gi, after_tile=None, skip_rms=False, last_tiles=None):
            if not skip_rms:
                rmsnorm_to_H(gi, *TILES[0])
                rmsnorm_to_H(gi, *TILES[1])
            for fb in range(4):
                for hb in range(2):
                    s = next_w()
                    for j in range(4):
                        hc = hb * 4 + j
                        for ti in range(5):
                            c0, n = TILES[ti]
                            b = nb()

                            def mm1(e, s=s, j=j, c0=c0, n=n, b=b):
                                ins = None
                                for k in range(8):
                                    ins = e.matmul(ps[:, b, 0:n], lhsT=WA[s][:, k, j * 128:(j + 1) * 128],
                                                   rhs=H[:, k, c0:c0 + n], start=(k == 0), stop=(k == 7))
                                return ins
                            sc.op("pe", mm1, reads=[rW(s)] + [rH(k, c0, n) for k in range(8)], writes=[rPS(b)])
                            i = nsq()
                            sc.op("act", lambda e, b=b, n=n, i=i: e.activation(out=SQR[:, i, 0:n], in_=ps[:, b, 0:n], func=AF.Relu),
                                  reads=[rPS(b)], writes=[rSQ(i)])
                            sc.op(alt_eng("dve", "pool"), lambda e, hc=hc, c0=c0, n=n, i=i: e.tensor_tensor(
                                out=HID[:, hc, c0:c0 + n], in0=SQR[:, i, 0:n], in1=SQR[:, i, 0:n], op=ALU.mult),
                                reads=[rSQ(i)], writes=[rHID(hc, c0, n)])
                            if (not skip_rms) and fb == 0 and hb == 0 and j == 0 and ti + 2 < 5:
                                rmsnorm_to_H(gi, *TILES[ti + 2])
                    issue_wload()
                sA = next_w()
                sB = next_w()
                last = (fb == 3)
                tl = TILES
                if last and last_tiles is not None:
                    tl = last_tiles
                order = [(ti, m) for ti in range(len(tl)) for m in range(8)] if last else \
                    [(ti, m) for m in range(8) for ti in range(5)]
                for (ti, m) in order:
                    c0, n = tl[ti]
                    b = nb()

                    def mm2(e, sA=sA, sB=sB, m=m, c0=c0, n=n, b=b):
                        ins = None
                        for k in range(8):
                            sl = sA if k < 4 else sB
                            ins = e.matmul(ps[:, b, 0:n], lhsT=WB[sl][:, k % 4, m * 128:(m + 1) * 128],
                                           rhs=HID[:, k, c0:c0 + n], start=(k == 0), stop=(k == 7))
                        return ins
                    sc.op("pe", mm2, reads=[rW(sA), rW(sB)] + [rHID(k, c0, n) for k in range(8)], writes=[rPS(b)])
                    sc.op("dve", lambda e, m=m, b=b, c0=c0, n=n: e.tensor_tensor(out=X[:, m, c0:c0 + n], in0=ps[:, b, 0:n],
                                                                                 in1=X[:, m, c0:c0 + n], op=ALU.add),
                          reads=[rPS(b), rX(m, c0, n)], writes=[rX(m, c0, n)])
                    if last and after_tile is not None and m == 7:
                        after_tile(ti)
                issue_wload()
                issue_wload()

        mlp(1)
        chk('mlp0')

        s_pool = next_w()
        PBR = Wreg[0:1, s_pool * 4096 + 2048:s_pool * 4096 + 3072]
        onesrow = Wreg[0:1, s_pool * 4096 + 3072:s_pool * 4096 + 3584]
        sc.op("pool", lambda e: e.memset(onesrow, 1.0), reads=[rW(s_pool)], writes=[("W", s_pool * 8192 + 6144, s_pool * 8192 + 7168)])
        HB = Sreg[:, 0:2 * 8 * 527].rearrange("p (b c n) -> p b c n", b=2, c=8)
        AB = Sreg[:, 8432:8432 + 2 * 6 * 527].rearrange("p (b c n) -> p b c n", b=2, c=6)
        TT1 = Sreg[:, 14756:14756 + 2 * 527].rearrange("p (c n) -> p c n", c=2)
        TT2 = Sreg[:, 15810:15810 + 2 * 527].rearrange("p (c n) -> p c n", c=2)
        WPS = Sreg[:, 16864:16864 + 2048].rearrange("p (g k n) -> p g k n", g=4, k=2)
        WPN = Sreg[:, 18912:18912 + 2048].rearrange("p (g k n) -> p g k n", g=4, k=2)
        P15 = Sreg[:, 20960:20960 + 8 * 16].rearrange("p (c n) -> p c n", c=8)
        rHB = lambda b, c, a=0, n=527: ("S", ((b * 8 + c) * 527 + a) * 2, ((b * 8 + c) * 527 + a + n) * 2)
        rAB = lambda b, c: ("S", (8432 + (b * 6 + c) * 527) * 2, (8432 + (b * 6 + c + 1) * 527) * 2)
        rTT1 = lambda c: ("S", (14756 + c * 527) * 2, (14756 + (c + 1) * 527) * 2)
        rTT2 = lambda c: ("S", (15810 + c * 527) * 2, (15810 + (c + 1) * 527) * 2)
        rWPS = ("S", 16864 * 2, (16864 + 2048) * 2)
        rWPN = ("S", 18912 * 2, (18912 + 2048) * 2)
        rP15 = ("S", 20960 * 2, (20960 + 128) * 2)
        for g, w in enumerate((2, 4, 8, 16)):
            sc.op("dve", lambda e, g=g, w=w: e.tensor_scalar(out=WPS[:, g, :, :], in0=WP[s_pool][:, g, :, :], scalar1=1.0 / w,
                                                             scalar2=None, op0=ALU.mult),
                  reads=[rW(s_pool)], writes=[rWPS])
        sc.op("dve", lambda e: e.tensor_scalar(out=WPN[:, :, :, :], in0=WP[s_pool][:, :, :, :], scalar1=-1.0, scalar2=None,
                                                op0=ALU.mult), reads=[rW(s_pool)], writes=[rWPN])
        RI = 3
        def stageA(ti):
            c0, n = TILES[ti]
            hb_ = ti % 2
            rms_stats(c0, n, ri=RI)
            if ti < 4:
                next_ = 15 + n
                if ti == 0:
                    sc.op("pool", lambda e, hb_=hb_: e.memset(HB[:, hb_, :, 0:15], 0.0), writes=[rHB(hb_, c, 0, 15) for c in range(8)])
                else:
                    sc.op("pool", lambda e, hb_=hb_: e.tensor_copy(out=HB[:, hb_, :, 0:15], in_=HB[:, 1 - hb_, :, 512:527]),
                          reads=[rHB(1 - hb_, c, 512, 15) for c in range(8)], writes=[rHB(hb_, c, 0, 15) for c in range(8)])
                for c in range(8):
                    rms_apply(2, c, c0, n, HB[:, hb_, c, 15:15 + n], [rHB(hb_, c, 15, n)], ri=RI)
            else:
                next_ = 31 + 368
                sc.op("pool", lambda e, hb_=hb_: e.tensor_copy(out=HB[:, hb_, :, 0:15], in_=HB[:, 1 - hb_, :, 512:527]),
                      reads=[rHB(1 - hb_, c, 512, 15) for c in range(8)], writes=[rHB(hb_, c, 0, 15) for c in range(8)])
                for q in range(2):
                    s_ = in_ctr[0] % 2
                    in_ctr[0] += 1
                    sc.dma("sp", [(IN[0:120, s_, :], spool[q * 8:(q + 1) * 8].rearrange("s j d -> (s j) d"))], d_in[s_],
                           writes=[rIN(s_)])
                    for half in range(2):
                        b = nb()

                        def tr(e, half=half, b=b, s_=s_):
                            ins = None
                            for cc in range(4):
                                c = half * 4 + cc
                                ins = e.transpose(out=ps[:, b, cc * 120:(cc + 1) * 120],
                                                  in_=IN[0:120, s_, c * 128:(c + 1) * 128], identity=ident[0:120, 0:120])
                            return ins
                        sc.op("pe", tr, reads=[rIN(s_), "ident"], writes=[rPS(b)])
                        src = ps[:, b, 0:480].rearrange("p (c s j) -> p c s j", c=4, s=8)
                        dst = HB[:, hb_, half * 4:(half + 1) * 4, 31 + q * 184:31 + (q + 1) * 184] \
                            .rearrange("p c (s j) -> p c s j", j=23)[:, :, :, 0:15]
                        copy_op(alt_eng(), dst, src, [rPS(b)], [rHB(hb_, half * 4 + cc, 31 + q * 184, 184) for cc in range(4)])
                for c in range(8):
                    rms_apply(2, c, 2048, 16, HB[:, hb_, c, 15:31], [rHB(hb_, c, 15, 16)], ri=RI)
                    sc.op("dve", lambda e, c=c, hb_=hb_: e.scalar_tensor_tensor(
                        out=HB[:, hb_, c, 31:399].rearrange("p (s j) -> p s j", j=23)[:, :, 15:23],
                        in0=X[:, c, 2064:2192].rearrange("p (s t) -> p s t", t=8), scalar=g32(2, c),
                        in1=ST[:, RI, 16:144].rearrange("p (s t) -> p s t", t=8), op0=ALU.mult, op1=ALU.mult),
                        reads=[rX(c, 2064, 128), rST(RI), ("g32", 2, 3)], writes=[rHB(hb_, c, 31, 368)])
                    sc.op("dve", lambda e, c=c: e.scalar_tensor_tensor(
                        out=VO[:, c, 15:30], in0=X[:, c, 2049:2064], scalar=g32(2, c), in1=ST[:, RI, 1:16],
                        op0=ALU.mult, op1=ALU.mult), reads=[rX(c, 2049, 15), rST(RI), ("g32", 2, 3)], writes=[("ST", 0, 6144)])
                    sc.op("dve", lambda e, c=c: e.scalar_tensor_tensor(
                        out=VO[:, c, 30:158], in0=X[:, c, 2064:2192], scalar=g32(2, c), in1=ST[:, RI, 16:144],
                        op0=ALU.mult, op1=ALU.mult), reads=[rX(c, 2064, 128), rST(RI), ("g32", 2, 3)], writes=[("ST", 0, 6144)])
            for c in range(2, 8):
                eng = "dve" if c < 4 else "pool"
                pe_ = c % 2
                steps = c // 2
                dst1 = AB[:, hb_, c - 2, :] if steps == 1 else TT1[:, pe_, :]
                r1 = rAB(hb_, c - 2) if steps == 1 else rTT1(pe_)
                sc.op(eng, lambda e, c=c, dst1=dst1: e.tensor_tensor(
                    out=dst1[:, 1:next_], in0=HB[:, hb_, c, 1:next_], in1=HB[:, hb_, c, 0:next_ - 1], op=ALU.add),
                    reads=[rHB(hb_, c, 0, next_)], writes=[r1])
                if steps >= 2:
                    dst2 = AB[:, hb_, c - 2, :] if steps == 2 else TT2[:, pe_, :]
                    r2 = rAB(hb_, c - 2) if steps == 2 else rTT2(pe_)
                    sc.op(eng, lambda e, dst2=dst2, pe_=pe_: e.tensor_tensor(
                        out=dst2[:, 3:next_], in0=TT1[:, pe_, 3:next_], in1=TT1[:, pe_, 1:next_ - 2], op=ALU.add),
                        reads=[rTT1(pe_)], writes=[r2])
                if steps == 3:
                    sc.op(eng, lambda e, c=c, pe_=pe_: e.tensor_tensor(
                        out=AB[:, hb_, c - 2, 7:next_], in0=TT2[:, pe_, 7:next_], in1=TT2[:, pe_, 3:next_ - 4], op=ALU.add),
                        reads=[rTT2(pe_)], writes=[rAB(hb_, c - 2)])
            if ti == 0:
                PA = PT[:, 0, 0:240].rearrange("p (c n) -> p c n", c=8)
                PB = PT[:, 1, 0:240].rearrange("p (c n) -> p c n", c=8)
                IV8 = PT[:, 1, 256:384].rearrange("p (c n) -> p c n", c=8)
                rPA = ("PT", 0, 960)
                rPB = ("PT", 2048, 2048 + 960)
                rIV8 = ("PT", 2048 + 1024, 2048 + 1536)
                for c in range(8):
                    sc.op("pool", lambda e, c=c: e.tensor_copy(out=IV8[:, c, :], in_=INVC[:, (c // 2) * 16:(c // 2 + 1) * 16]),
                          reads=[("invc", 0, 64)], writes=[rIV8])
                hb_all = [rHB(hb_, c, 0, 30) for c in range(8)]
                sc.op("dve", lambda e, hb_=hb_: e.tensor_tensor(out=PA[:, :, 1:30], in0=HB[:, hb_, :, 1:30],
                                                               in1=HB[:, hb_, :, 0:29], op=ALU.add),
                      reads=hb_all, writes=[rPA])
                sc.op("dve", lambda e: e.tensor_tensor(out=PB[:, 2:8, 3:30], in0=PA[:, 2:8, 3:30], in1=PA[:, 2:8, 1:28], op=ALU.add),
                      reads=[rPA], writes=[rPB])
                sc.op("dve", lambda e: e.tensor_tensor(out=PA[:, 4:8, 7:30], in0=PB[:, 4:8, 7:30], in1=PB[:, 4:8, 3:26], op=ALU.add),
                      reads=[rPB], writes=[rPA])
                sc.op("dve", lambda e: e.tensor_tensor(out=PB[:, 6:8, 15:30], in0=PA[:, 6:8, 15:30], in1=PA[:, 6:8, 7:22], op=ALU.add),
                      reads=[rPA], writes=[rPB])
                for (buf, rbuf, cA) in ((PA, rPA, 0), (PB, rPB, 2), (PA, rPA, 4), (PB, rPB, 6)):
                    sc.op("dve", lambda e, buf=buf, cA=cA: e.tensor_tensor(out=buf[:, cA:cA + 2, 15:30], in0=buf[:, cA:cA + 2, 15:30],
                                                                           in1=IV8[:, cA:cA + 2, 0:15], op=ALU.mult),
                          reads=[rbuf, rIV8], writes=[rbuf])
                    sc.op("dve", lambda e, buf=buf, cA=cA, hb_=hb_: e.tensor_tensor(out=P15[:, cA:cA + 2, 0:15], in0=buf[:, cA:cA + 2, 15:30],
                                                                                    in1=HB[:, hb_, cA:cA + 2, 15:30], op=ALU.subtract),
                          reads=[rbuf, rHB(hb_, cA, 15, 15), rHB(hb_, cA + 1, 15, 15)], writes=[rP15])

        def stageB(ti):
            c0, n = TILES[ti]
            hb_ = ti % 2
            for m in range(8):
                g = m // 2
                mi = m % 2
                b = nb()
                stride = (1, 2, 4, 8)[g]

                def mmp(e, g=g, mi=mi, b=b, n=n, ti=ti, hb_=hb_, stride=stride, m=m):
                    ins = None
                    if ti < 4:
                        segs = [("flat", 15 if ti == 0 else 0, n, 15 + (15 if ti == 0 else 0))]
                    else:
                        segs = [("flat", 0, 16, 15), ("samp", 16, 128, 0)]
                    for (kind, pc0, cnt_, e0) in segs:
                        pc1 = n if kind == "flat" and ti < 4 else pc0 + cnt_
                        terms = []
                        for ki in range(2):
                            cs = 2 * g + ki
                            for j in range(2):
                                sh = j * stride
                                srcbuf = (HB[:, hb_, cs, :] if g < 1 else AB[:, hb_, cs - 2, :])
                                terms.append((WPS[:, g, ki, mi * 128:(mi + 1) * 128], srcbuf, sh))
                            terms.append((WPN[:, g, ki, mi * 128:(mi + 1) * 128], HB[:, hb_, cs, :], 0))
                        for idx, (lw, srcbuf, sh) in enumerate(terms):
                            if kind == "flat":
                                ncols = pc1 - pc0
                                rhs = srcbuf[:, e0 - sh:e0 - sh + ncols]
                                outp = ps[:, b, pc0:pc1]
                            else:
                                rhs = srcbuf[:, 31:399].rearrange("p (s j) -> p s j", j=23)[:, :, 15 - sh:23 - sh]
                                outp = ps[:, b, 16:144].rearrange("p (s t) -> p s t", t=8)
                            ins = e.matmul(outp, lhsT=lw, rhs=rhs, start=(idx == 0), stop=False)
                        if kind == "flat":
                            ins = e.matmul(ps[:, b, pc0:pc1], lhsT=PBR[0:1, m * 128:(m + 1) * 128], rhs=onesrow[0:1, 0:pc1 - pc0],
                                           start=False, stop=True)
                        else:
                            ins = e.matmul(ps[:, b, 16:144], lhsT=PBR[0:1, m * 128:(m + 1) * 128], rhs=onesrow[0:1, 0:128],
                                           start=False, stop=True)
                    if ti == 0:
                        for ki in range(2):
                            ins = e.matmul(ps[:, b, 0:15], lhsT=WP[s_pool][:, g, ki, mi * 128:(mi + 1) * 128],
                                           rhs=P15[:, 2 * g + ki, 0:15], start=(ki == 0), stop=False)
                        ins = e.matmul(ps[:, b, 0:15], lhsT=PBR[0:1, m * 128:(m + 1) * 128], rhs=onesrow[0:1, 0:15],
                                       start=False, stop=True)
                    return ins
                rd = [rW(s_pool), rWPS, rWPN, rP15] + [rHB(hb_, 2 * g + ki, 0, 527) for ki in range(2)]
                if g >= 1:
                    rd += [rAB(hb_, 2 * g + ki - 2) for ki in range(2)]
                sc.op("pe", mmp, reads=rd, writes=[rPS(b)])
                sc.op("dve", lambda e, m=m, b=b, n=n, c0=c0: e.scalar_tensor_tensor(
                    out=X[:, m, c0:c0 + n], in0=ps[:, b, 0:n], scalar=vec(V_PS, m), in1=X[:, m, c0:c0 + n],
                    op0=ALU.mult, op1=ALU.add), reads=[rPS(b), rX(m, c0, n), "vec"], writes=[rX(m, c0, n)])

        def np_out():
            out_rows(lambda c: VO[:, c, 15:30], 15, IN[:, 0, :], rIN(0), lambda: [(npp[:, :], IN[0:15, 0, :])],
                     d_in[0], "npp", [("ST", 0, 6144)])
            out_rows(lambda c: VO[:, c, 30:158], 128, IN[:, 1, :], rIN(1),
                     lambda: [(nps[s_, 7:15, :], IN[s_ * 8:(s_ + 1) * 8, 1, :]) for s_ in range(16)], d_in[1], "nps_new",
                     [("ST", 0, 6144)])

        stageA(0)
        for ti in range(5):
            if ti + 1 < 5:
                stageA(ti + 1)
            stageB(ti)
            if ti == 3:
                np_out()
            if ti >= 1:
                rmsnorm_to_H(3, *TILES[ti - 1])
        rmsnorm_to_H(3, *TILES[4])
        issue_wload()

        ost_ctr = [0]

        FTL = [(0, 512), (512, 512), (1024, 512), (1536, 256), (1792, 256), (2048, 144)]
        NF = len(FTL)

        def final_S(fi):
            c0, n = FTL[fi]
            yb = fi % 2
            rms_stats(c0, n, ri=1)
            for c in range(8):
                rms_apply(4, c, c0, n, YT[:, yb, c, 0:n], [rYT(yb, c, 0, n)], ri=1)

        def final_Tr(fi):
            c0, n = FTL[fi]
            yb = fi % 2
            blocks = []
            if fi < NF - 1:
                for blk in range(n // 128):
                    cc = c0 + blk * 128
                    if cc == 0:
                        blocks.append((0, 128, 16, yp[0:112, :]))
                    else:
                        blocks.append((blk * 128, 128, 0, yp[cc - 16:cc + 112, :]))
            else:
                blocks.append((0, 16, 0, yp[2032:2048, :]))
                blocks.append((16, 128, 0, ys[:, :]))
            for (o, nc_, skip, dst) in blocks:
                sap, srng, sd = STG[ost_ctr[0] % 4]
                ost_ctr[0] += 1
                out_rows(lambda c, yb=yb, o=o, nc_=nc_: YT[:, yb, c, o:o + nc_], nc_, sap, srng,
                         lambda dst=dst, sap=sap, skip=skip, nc_=nc_: [(dst, sap[skip:nc_, :])], sd, ("y", fi, o),
                         [rYT(yb, c, 0, 512) for c in range(8)])

        def after_last(fi):
            if fi >= 2:
                final_Tr(fi - 2)
            if fi >= 1:
                final_S(fi - 1)
            if fi == NF - 1:
                final_Tr(NF - 2)
                final_S(NF - 1)
                final_Tr(NF - 1)

        chk('pool')
        mlp(3, skip_rms=True, after_tile=after_last, last_tiles=FTL)
        chk('mlp1')

        sc.finish()
        sc.emit()
    return nc


_NC_CACHE = {}
_STOP = None
_NOW = False


def _to_pc(v):
    return np.ascontiguousarray(np.asarray(v, np.float32).reshape(8, 128).T)


def kernel(x_prompt, x_sample, state_conv, state_pool, meta_tokens, norm_mix_g, norm_mlp_g,
           conv_w_pw1, conv_b_pw1, conv_w_dw, conv_b_dw, conv_ln_g, conv_ln_b,
           conv_w_pw2, conv_b_pw2, pool_w, pool_b, pool_scale, mlp_w1, mlp_w2, final_g):
    f = lambda a: np.ascontiguousarray(np.asarray(a, dtype=np.float32))
    x_prompt, x_sample, state_conv, state_pool = f(x_prompt), f(x_sample), f(state_conv), f(state_pool)
    vec_list = [norm_mix_g[0], norm_mix_g[1], norm_mlp_g[0], norm_mlp_g[1], final_g,
                np.asarray(conv_b_pw1)[0, :D], np.asarray(conv_b_pw1)[0, D:], conv_b_dw[0], conv_ln_g[0], conv_ln_b[0],
                conv_b_pw2[0], np.asarray(pool_b)[0].reshape(-1), pool_scale[0]]
    vec_list += [np.asarray(conv_w_dw)[0, k] for k in range(31)]
    vecs = np.ascontiguousarray(np.concatenate([_to_pc(v) for v in vec_list], axis=1))
    shared = {
        "meta": f(meta_tokens), "vecs": vecs, "pbrow": f(np.asarray(pool_b)[0].reshape(1, D)), "w_pw1": f(np.asarray(conv_w_pw1)[0]), "w_pw2": f(np.asarray(conv_w_pw2)[0]),
        "w_pool": f(np.asarray(pool_w)[0]), "w_m1": f(mlp_w1), "w_m2": f(mlp_w2),
    }
    in_maps = []
    for c in range(NCORES):
        m = dict(shared)
        m["xp"] = x_prompt[c]
        m["xs"] = x_sample[16 * c:16 * (c + 1)].reshape(128, D)
        m["sconv"] = state_conv[0, 16 * c:16 * (c + 1)]
        m["spool"] = state_pool[0, 16 * c:16 * (c + 1)]
        in_maps.append(m)
    if "nc" not in _NC_CACHE:
        _NC_CACHE["nc"] = build_program(_STOP)
    nc = _NC_CACHE["nc"]
    res = run_bass_kernel_spmd(nc, in_maps, core_ids=list(range(NCORES)))
    R = res.results
    if _STOP is not None:
        _NC_CACHE["dbg"] = R
    y_prompt = np.stack([R[c]["yp"] for c in range(NCORES)], axis=0)
    y_sample = np.concatenate([R[c]["ys"].reshape(16, 8, D) for c in range(NCORES)], axis=0)
    ncp = np.stack([R[c]["ncp"] for c in range(NCORES)], axis=0)[None]
    ncs = np.concatenate([R[c]["ncs"] for c in range(NCORES)], axis=0)[None]
    npp = np.stack([R[c]["npp"] for c in range(NCORES)], axis=0)[None]
    nps = np.concatenate([R[c]["nps"] for c in range(NCORES)], axis=0)[None]
    return (y_prompt.astype(np.float32), y_sample.astype(np.float32), ncp.astype(np.float32),
            ncs.astype(np.float32), npp.astype(np.float32), nps.astype(np.float32))
```
